# Optimizing a Trainium2 kernel written in Bass

```python
import math
import jax
import jax.numpy as jnp
from jax import lax
import numpy as np

D_MODEL = 2048
BATCH = 4
SEQ = 2048
DEPTH = 4

GRID_W = 64
CTX_LEN = 256
N_MIXERS = 4
REPEATS = DEPTH // N_MIXERS
DN_ALPHA = (2 * DEPTH) ** 0.25
DN_BETA = (8 * DEPTH) ** -0.25
LN_EPS = 1e-5
N_MOD = 6

D_FF = 5632
FFN_CONV = 3

RW_HEAD = 64
RW_HEADS = D_MODEL // RW_HEAD
RW_DECAY_LORA = 96
RW_ICLR_LORA = 64
RW_GATE_LORA = 256
RW_DECAY_SCALE = 0.606531
RW_GN_EPS = 64e-5

DA_HEAD = 128
DA_HEADS = D_MODEL // (2 * DA_HEAD)
DA_QBLOCK = 128
ROPE_BASE = 10000.0
ROPE_FREQS = DA_HEAD // 4

HG_EXPAND = 128
HG_HEADS = D_MODEL // HG_EXPAND
HG_HEAD_V = D_MODEL // HG_HEADS
HG_CHUNK = 64

LR_WIDTH = D_MODEL
LR_BLOCKS = 8
LR_BS = LR_WIDTH // LR_BLOCKS
LR_CONV = 4
LR_C = 8.0

kernel_name = 'hybrid_interleaved_diffusion_backbone'


def layer_norm(x, g, b):
    xf = x.astype(jnp.float32)
    mu = jnp.mean(xf, -1, keepdims=True)
    var = jnp.mean(jnp.square(xf - mu), -1, keepdims=True)
    return ((xf - mu) * lax.rsqrt(var + LN_EPS)).astype(x.dtype) * g + b


def rms_norm(x, g, eps):
    xf = x.astype(jnp.float32)
    return (xf * lax.rsqrt(jnp.mean(jnp.square(xf), -1, keepdims=True) + eps)).astype(x.dtype) * g


def split_apply(fn_ctx, fn_lat, z, n_ctx):
    if n_ctx == 0:
        return fn_lat(z)
    return jnp.concatenate([fn_ctx(z[:, :n_ctx]), fn_lat(z[:, n_ctx:])], axis=1)


def seg_flip(t, n_ctx):
    return jnp.concatenate([jnp.flip(t[:, :n_ctx], 1), jnp.flip(t[:, n_ctx:], 1)], axis=1)


def dwconv_centred(x, w, b):
    k_w, n = w.shape[0], x.shape[1]
    xp = jnp.pad(x, ((0, 0), ((k_w - 1) // 2, k_w // 2), (0, 0)))
    return b + sum(xp[:, j:j + n] * w[j] for j in range(k_w))


def centred_shift_delta(x):
    xp = jnp.pad(x, ((0, 0), (1, 1), (0, 0)))
    return 0.5 * (xp[:, :-2] + xp[:, 2:]) - x


def ada_modulate(z, n_ctx, m_ctx, m_lat, j):
    mod = lambda m: (lambda s: s * (1 + m[:, None, j + 1]) + m[:, None, j])
    return split_apply(mod(m_ctx), mod(m_lat), z, n_ctx)


def ada_gate(y, n_ctx, m_ctx, m_lat, j):
    gate = lambda m: (lambda s: s * m[:, None, j])
    return split_apply(gate(m_ctx), gate(m_lat), y, n_ctx)


def prefix_axial_rope(n_ctx, n_lat):
    n_rows = n_lat // GRID_W
    row = jnp.repeat(jnp.arange(n_rows, dtype=jnp.float32), GRID_W)
    col = jnp.tile(jnp.arange(GRID_W, dtype=jnp.float32), n_rows)
    inv_freq = ROPE_BASE ** (-jnp.arange(ROPE_FREQS, dtype=jnp.float32) / ROPE_FREQS)
    ang_r, ang_c = row[:, None] * inv_freq, col[:, None] * inv_freq
    ang = jnp.concatenate([ang_r, ang_r, ang_c, ang_c], axis=-1)
    ang = jnp.concatenate([jnp.zeros((n_ctx, DA_HEAD), jnp.float32), ang], axis=0)
    return jnp.cos(ang), jnp.sin(ang)


def rotate_quarters(t):
    qa, qb, qc, qd = jnp.split(t, 4, axis=-1)
    return jnp.concatenate([-qb, qa, -qd, qc], axis=-1)


def rwkv7_scan(r, w, k, v, kk, a):
    def step(S, inp):
        r_t, w_t, k_t, v_t, kk_t, a_t = inp
        sa = jnp.einsum('ghij,ghj->ghi', S, kk_t)
        S = (S * w_t[:, :, None, :] - sa[..., None] * (kk_t * a_t)[:, :, None, :]
             + v_t[..., None] * k_t[:, :, None, :])
        return S, jnp.einsum('ghij,ghj->ghi', S, r_t)
    S0 = jnp.zeros(r.shape[1:] + (r.shape[-1],), r.dtype)
    return lax.scan(step, S0, (r, w, k, v, kk, a))[1]


def gla_chunked(q, k, v, logf):
    g_, n, h_, dk = q.shape
    dv = v.shape[-1]
    nc = n // HG_CHUNK
    blk = lambda t: jnp.moveaxis(t.reshape(g_, nc, HG_CHUNK, h_, t.shape[-1]), 1, 0)
    q, k, v = blk(q), blk(k), blk(v)
    b = jnp.cumsum(blk(logf).astype(jnp.float32), axis=2)
    b_end = b[:, :, -1:]
    qd = q * jnp.exp(b)
    kd = k * jnp.exp(-b)
    ke = k * jnp.exp(b_end - b)
    causal = jnp.tril(jnp.ones((HG_CHUNK, HG_CHUNK), bool))
    att = jnp.where(causal, jnp.einsum('ngthk,ngshk->nghts', qd, kd), 0.0)
    o_intra = jnp.einsum('nghts,ngshv->ngthv', att, v.astype(att.dtype))

    def step(S, inp):
        qd_n, ke_n, v_n, dec_n = inp
        o_n = jnp.einsum('gthk,ghkv->gthv', qd_n, S)
        S = dec_n[..., None] * S + jnp.einsum('gshk,gshv->ghkv', ke_n, v_n)
        return S, o_n
    S0 = jnp.zeros((g_, h_, dk, dv), qd.dtype)
    _, o_inter = lax.scan(step, S0, (qd, ke, v.astype(qd.dtype), jnp.exp(b_end[:, :, 0])))
    o = o_intra + o_inter
    return jnp.moveaxis(o, 0, 1).reshape(g_, n, h_, dv)


def rwkv7_mixer(h, n_ctx, drop_ctx, mu, w_rkv, w0, w1, w2, a0, a1, a2, g1, g2, k_k, k_a, r_k,
                gn_g, gn_b, w_o):
    bsz, n, _ = h.shape
    dt = h.dtype
    dx = split_apply(centred_shift_delta, centred_shift_delta, h, n_ctx)
    xs = h[None] + dx[None] * mu[:, None, None, :]
    r, k, v = jnp.einsum('nbld,nde->nble', xs[:3], w_rkv)
    w_log = w0[:, None, None] + jnp.einsum('nblr,nrd->nbld', jnp.tanh(jnp.einsum('bld,ndr->nblr', xs[3], w1)), w2)
    decay = jnp.exp(-RW_DECAY_SCALE * jax.nn.sigmoid(w_log))
    iclr = jax.nn.sigmoid(a0[:, None, None] + jnp.einsum('nblr,nrd->nbld', jnp.einsum('bld,ndr->nblr', xs[4], a1), a2))
    g = jax.nn.sigmoid(xs[5] @ g1) @ g2
    kk = (k * k_k).reshape(bsz, n, RW_HEADS, RW_HEAD)
    kk = kk * lax.rsqrt(jnp.sum(jnp.square(kk.astype(jnp.float32)), -1, keepdims=True) + 1e-12).astype(kk.dtype)
    kk = kk.reshape(bsz, n, D_MODEL)
    k_dir = k[None] * (1 + (iclr - 1) * k_a)

    def dirs(t_f, t_b):
        t = jnp.concatenate([t_f, seg_flip(t_b, n_ctx)], axis=0).astype(dt)
        return jnp.moveaxis(t.reshape(t.shape[:2] + (RW_HEADS, RW_HEAD)), 1, 0)
    y = rwkv7_scan(dirs(r, r), dirs(decay[0], decay[1]), dirs(k_dir[0], k_dir[1]),
                   dirs(v, v), dirs(kk, kk), dirs(iclr[0], iclr[1]))
    y = jnp.moveaxis(y, 0, 1)
    y = y[:bsz] + seg_flip(y[bsz:], n_ctx)
    heads = lambda t: t.reshape(t.shape[:2] + (RW_HEADS, RW_HEAD))
    rh, vh = heads(r), heads(v)
    bonus = jnp.sum(rh * heads(k_dir[0] + k_dir[1]) * r_k, -1, keepdims=True) * vh
    if drop_ctx:
        y, bonus, g = y[:, n_ctx:], bonus[:, n_ctx:], g[:, n_ctx:]
    yf = y.astype(jnp.float32)
    mean = jnp.mean(yf, -1, keepdims=True)
    var = jnp.mean(jnp.square(yf - mean), -1, keepdims=True)
    yn = ((yf - mean) * lax.rsqrt(var + RW_GN_EPS)).astype(dt)
    yn = yn.reshape(yn.shape[:2] + (D_MODEL,)) * gn_g + gn_b
    out = (yn + bonus.reshape(yn.shape)) * g
    return out @ w_o


def diff_attention_mixer(h, n_ctx, drop_ctx, layer_idx, rope_cos, rope_sin, w_qkv, lam_vec, sub_g, w_o):
    bsz, n, _ = h.shape
    q, k, v = jnp.split(h @ w_qkv, 3, axis=-1)
    cos, sin = rope_cos[:, None].astype(h.dtype), rope_sin[:, None].astype(h.dtype)
    rope = lambda t: t * cos + rotate_quarters(t) * sin
    q = rope(q.reshape(bsz, n, 2 * DA_HEADS, DA_HEAD)).reshape(bsz, n, DA_HEADS, 2, DA_HEAD)
    k = rope(k.reshape(bsz, n, 2 * DA_HEADS, DA_HEAD)).reshape(bsz, n, DA_HEADS, 2, DA_HEAD)
    v = v.reshape(bsz, n, DA_HEADS, 2 * DA_HEAD)
    lam_init = 0.8 - 0.6 * math.exp(-0.3 * layer_idx)
    lv = lam_vec.astype(jnp.float32)
    lam = jnp.exp(jnp.sum(lv[0] * lv[1])) - jnp.exp(jnp.sum(lv[2] * lv[3])) + lam_init

    def attend(qb, kb, vb):
        s = jnp.einsum('bqhmd,bkhmd->bhmqk', qb, kb).astype(jnp.float32) * (DA_HEAD ** -0.5)
        p = jax.nn.softmax(s, axis=-1)
        p = p[:, :, 0] - lam * p[:, :, 1]
        return jnp.einsum('bhqk,bkhe->bqhe', p.astype(vb.dtype), vb)

    q_lat = q[:, n_ctx:]
    n_blk = q_lat.shape[1] // DA_QBLOCK
    qb = jnp.moveaxis(q_lat.reshape(bsz, n_blk, DA_QBLOCK, DA_HEADS, 2, DA_HEAD), 1, 0)
    o = lax.map(lambda blk: attend(blk, k, v), qb)
    o = jnp.moveaxis(o, 0, 1).reshape(bsz, n - n_ctx, DA_HEADS, 2 * DA_HEAD)
    if not drop_ctx:
        o = jnp.concatenate([attend(q[:, :n_ctx], k[:, :n_ctx], v[:, :n_ctx]), o], axis=1)
    o = rms_norm(o, sub_g, 1e-5) * (1 - lam_init)
    return o.reshape(o.shape[:2] + (D_MODEL,)) @ w_o


def hgrn2_mixer(h, n_ctx, drop_ctx, layer_idx, w_in, lower, norm_g, w_o):
    bsz, n, _ = h.shape
    q, i_in, g_out, f_fwd, f_bwd = jnp.split(h @ w_in, 5, axis=-1)
    lb = jnp.cumsum(jax.nn.softmax(lower.astype(jnp.float32), axis=1), axis=1)
    lb = (lb - lb[:, :1])[:, layer_idx]
    f = lb[:, None, None] + (1.0 - lb[:, None, None]) * jax.nn.sigmoid(jnp.stack([f_fwd, f_bwd]).astype(jnp.float32))
    heads_k = lambda t: t.reshape(t.shape[:-1] + (HG_HEADS, HG_EXPAND))
    heads_v = lambda t: t.reshape(t.shape[:-1] + (HG_HEADS, HG_HEAD_V))
    both = lambda t_f, t_b: jnp.concatenate([t_f, seg_flip(t_b, n_ctx)], axis=0)
    qh, vh, fh = heads_k(jax.nn.silu(q)), heads_v(i_in), heads_k(f)
    o = gla_chunked(both(qh, qh), both(1.0 - fh[0], 1.0 - fh[1]), both(vh, vh),
                    both(jnp.log(fh[0]), jnp.log(fh[1])))
    o = o[:bsz] + seg_flip(o[bsz:], n_ctx)
    gh = heads_v(g_out)
    if drop_ctx:
        o, gh = o[:, n_ctx:], gh[:, n_ctx:]
    o = rms_norm(o, norm_g, 1e-5).astype(h.dtype) * jax.nn.silu(gh)
    return o.reshape(o.shape[:2] + (D_MODEL,)) @ w_o


def rglru_mixer(h, n_ctx, drop_ctx, w_in, conv_w, conv_b, w_gate, b_gate, lam, w_o):
    bsz, n, _ = h.shape
    gate_branch, xb = jnp.split(h @ w_in, 2, axis=-1)
    conv = lambda s: dwconv_centred(s, conv_w, conv_b)
    xb = split_apply(conv, conv, xb, n_ctx)
    gates = jnp.einsum('blni,dgnij->dgblnj', xb.reshape(bsz, n, LR_BLOCKS, LR_BS), w_gate)
    gates = jax.nn.sigmoid(gates.reshape(2, 2, bsz, n, LR_WIDTH) + b_gate[:, :, None, None])
    rec_gate, in_gate = gates[:, 0], gates[:, 1]
    log_a = -LR_C * rec_gate * jax.nn.softplus(-lam)[:, None, None]
    a = jnp.exp(log_a)
    u = jnp.sqrt(-jnp.expm1(2.0 * log_a)) * in_gate * xb[None]
    both = lambda t: jnp.concatenate([t[0], seg_flip(t[1], n_ctx)], axis=0)
    combine = lambda p, q: (p[0] * q[0], q[0] * p[1] + q[1])
    _, hs = lax.associative_scan(combine, (both(a), both(u)), axis=1)
    y = hs[:bsz] + seg_flip(hs[bsz:], n_ctx)
    if drop_ctx:
        y, gate_branch = y[:, n_ctx:], gate_branch[:, n_ctx:]
    return (y * jax.nn.gelu(gate_branch)) @ w_o


def conv_ffn(h, n_ctx, w_up, conv_w, conv_b, w_down):
    conv = lambda s: dwconv_centred(s, conv_w, conv_b)
    u = split_apply(conv, conv, h @ w_up, n_ctx)
    gate, val = jnp.split(u, 2, axis=-1)
    return (jax.nn.silu(gate) * val) @ w_down


def setup_inputs(seed: int = 0) -> dict:
    key = jax.random.key(seed)
    ks = iter(jax.random.split(key, 48))
    f32 = jnp.float32
    D, R, W = D_MODEL, REPEATS, LR_WIDTH

    def nrm(shape, scale):
        return scale * jax.random.normal(next(ks), shape, f32)

    inp = {}
    inp['x'] = nrm((BATCH, SEQ, D), 1.0)
    inp['c'] = nrm((BATCH, D), 1.0)
    inp['ctx'] = nrm((BATCH, CTX_LEN, D), 1.0)
    inp['c_ctx'] = nrm((D,), 1.0)
    inp['ada_w'] = nrm((DEPTH, D, N_MOD * D), 0.5 * D ** -0.5)
    inp['ada_b'] = nrm((DEPTH, N_MOD * D), 0.02)
    inp['ln_g'] = 1.0 + nrm((DEPTH, 2, D), 0.02)
    inp['ln_b'] = nrm((DEPTH, 2, D), 0.02)
    inp['ffn_w_up'] = nrm((DEPTH, D, 2 * D_FF), D ** -0.5)
    inp['ffn_conv_w'] = nrm((DEPTH, FFN_CONV, 2 * D_FF), FFN_CONV ** -0.5)
    inp['ffn_conv_b'] = nrm((DEPTH, 2 * D_FF), 0.02)
    inp['ffn_w_down'] = nrm((DEPTH, D_FF, D), DN_BETA * D_FF ** -0.5)
    inp['rw_mu'] = jax.random.uniform(next(ks), (R, 6, D), f32)
    inp['rw_w_rkv'] = nrm((R, 3, D, D), D ** -0.5)
    inp['rw_w0'] = nrm((R, 2, D), 0.5)
    inp['rw_w1'] = nrm((R, 2, D, RW_DECAY_LORA), D ** -0.5)
    inp['rw_w2'] = nrm((R, 2, RW_DECAY_LORA, D), RW_DECAY_LORA ** -0.5)
    inp['rw_a0'] = nrm((R, 2, D), 0.5)
    inp['rw_a1'] = nrm((R, 2, D, RW_ICLR_LORA), D ** -0.5)
    inp['rw_a2'] = nrm((R, 2, RW_ICLR_LORA, D), RW_ICLR_LORA ** -0.5)
    inp['rw_g1'] = nrm((R, D, RW_GATE_LORA), D ** -0.5)
    inp['rw_g2'] = nrm((R, RW_GATE_LORA, D), RW_GATE_LORA ** -0.5)
    inp['rw_k_k'] = 1.0 + nrm((R, D), 0.1)
    inp['rw_k_a'] = 1.0 + nrm((R, D), 0.1)
    inp['rw_r_k'] = nrm((R, RW_HEADS, RW_HEAD), 0.1)
    inp['rw_gn_g'] = 1.0 + nrm((R, D), 0.02)
    inp['rw_gn_b'] = nrm((R, D), 0.02)
    inp['rw_w_o'] = nrm((R, D, D), DN_BETA * D ** -0.5)
    inp['da_w_qkv'] = nrm((R, D, 3 * D), D ** -0.5)
    inp['da_lambda'] = nrm((R, 4, DA_HEAD), 0.1)
    inp['da_sub_g'] = 1.0 + nrm((R, 2 * DA_HEAD), 0.02)
    inp['da_w_o'] = nrm((R, D, D), DN_BETA * D ** -0.5)
    inp['hg_w_in'] = nrm((R, D, 5 * D), D ** -0.5)
    inp['hg_lower'] = nrm((2, DEPTH, D), 0.1)
    inp['hg_norm_g'] = 1.0 + nrm((R, HG_HEAD_V), 0.02)
    inp['hg_w_o'] = nrm((R, D, D), DN_BETA * D ** -0.5)
    inp['lr_w_in'] = nrm((R, D, 2 * W), D ** -0.5)
    inp['lr_conv_w'] = nrm((R, LR_CONV, W), LR_CONV ** -0.5)
    inp['lr_conv_b'] = nrm((R, W), 0.02)
    inp['lr_w_gate'] = nrm((R, 2, 2, LR_BLOCKS, LR_BS, LR_BS), LR_BS ** -0.5)
    inp['lr_b_gate'] = nrm((R, 2, 2, W), 0.02)
    a_init = jax.random.uniform(next(ks), (R, 2, W), f32, 0.9, 0.999)
    inp['lr_lambda'] = jnp.log(a_init) - jnp.log1p(-a_init)
    inp['lr_w_o'] = nrm((R, W, D), DN_BETA * W ** -0.5)
    return inp


def reference(x, c, ctx, c_ctx, ada_w, ada_b, ln_g, ln_b, ffn_w_up, ffn_conv_w, ffn_conv_b, ffn_w_down,
              rw_mu, rw_w_rkv, rw_w0, rw_w1, rw_w2, rw_a0, rw_a1, rw_a2, rw_g1, rw_g2, rw_k_k, rw_k_a,
              rw_r_k, rw_gn_g, rw_gn_b, rw_w_o, da_w_qkv, da_lambda, da_sub_g, da_w_o,
              hg_w_in, hg_lower, hg_norm_g, hg_w_o, lr_w_in, lr_conv_w, lr_conv_b, lr_w_gate, lr_b_gate,
              lr_lambda, lr_w_o):
    n_ctx = ctx.shape[1]
    rope_cos, rope_sin = prefix_axial_rope(n_ctx, x.shape[1])
    z = jnp.concatenate([ctx, x], axis=1)
    silu_c = jax.nn.silu(c)
    silu_cc = jax.nn.silu(c_ctx)[None]
    for i in range(DEPTH):
        rep, kind, last = i // N_MIXERS, i % N_MIXERS, i == DEPTH - 1
        m_lat = (silu_c @ ada_w[i] + ada_b[i]).reshape(-1, N_MOD, D_MODEL)
        m_ctx = (silu_cc @ ada_w[i] + ada_b[i]).reshape(1, N_MOD, D_MODEL)
        h = ada_modulate(z, n_ctx, m_ctx, m_lat, 0)
        if kind == 0:
            y = rwkv7_mixer(h, n_ctx, last, rw_mu[rep], rw_w_rkv[rep], rw_w0[rep], rw_w1[rep], rw_w2[rep],
                            rw_a0[rep], rw_a1[rep], rw_a2[rep], rw_g1[rep], rw_g2[rep], rw_k_k[rep],
                            rw_k_a[rep], rw_r_k[rep], rw_gn_g[rep], rw_gn_b[rep], rw_w_o[rep])
        elif kind == 1:
            y = diff_attention_mixer(h, n_ctx, last, i, rope_cos, rope_sin, da_w_qkv[rep], da_lambda[rep],
                                     da_sub_g[rep], da_w_o[rep])
        elif kind == 2:
            y = hgrn2_mixer(h, n_ctx, last, i, hg_w_in[rep], hg_lower, hg_norm_g[rep], hg_w_o[rep])
        else:
            y = rglru_mixer(h, n_ctx, last, lr_w_in[rep], lr_conv_w[rep], lr_conv_b[rep], lr_w_gate[rep],
                            lr_b_gate[rep], lr_lambda[rep], lr_w_o[rep])
        if last:
            z, n_ctx = z[:, n_ctx:], 0
        z = layer_norm(DN_ALPHA * z + ada_gate(y, n_ctx, m_ctx, m_lat, 2), ln_g[i, 0], ln_b[i, 0])
        h = ada_modulate(z, n_ctx, m_ctx, m_lat, 3)
        y = conv_ffn(h, n_ctx, ffn_w_up[i], ffn_conv_w[i], ffn_conv_b[i], ffn_w_down[i])
        z = layer_norm(DN_ALPHA * z + ada_gate(y, n_ctx, m_ctx, m_lat, 5), ln_g[i, 1], ln_b[i, 1])
    return z[:, n_ctx:]
```

```python
import contextlib
import math
import numpy as np
import concourse.bass as bass
import concourse.mybir as mybir
from concourse.bass_utils import run_bass_kernel_spmd

F32 = mybir.dt.float32
BF16 = mybir.dt.bfloat16
AF = mybir.ActivationFunctionType
ALU = mybir.AluOpType
AX = mybir.AxisListType

D = 2048
KC = 16
NCTX = 256
SEQ = 2048
T = NCTX + SEQ
FF = 5632
FB = FF // 128
DEPTH = 4
DN_ALPHA = (2 * DEPTH) ** 0.25
LN_EPS = 1e-5
NQ = 20
NCORES = 4


class Reg:
    __slots__ = ("w", "r")

    def __init__(self):
        self.w = None
        self.r = {}


class KB:
    ENGS = ["pe", "act", "dve", "pool", "sp"]

    def __init__(self, nc, es):
        self.nc = nc
        self.sem = {}
        for e in self.ENGS:
            self.sem[e] = es.enter_context(nc.semaphore("s_" + e))
        self.sem["bar"] = es.enter_context(nc.semaphore("s_bar"))
        self.dq = ["sp", "pool", "act"]
        for q in self.dq:
            for j in range(NQ):
                self.sem[(q, j)] = es.enter_context(nc.semaphore("d_%s_%d" % (q, j)))
        self.tick = {e: 0 for e in self.ENGS}
        self.pending = {e: False for e in self.ENGS}
        self.dcount = {q: 0 for q in self.dq}
        self.dlast = {}
        self.seen = {e: {} for e in self.ENGS}
        self.prog = {e: [] for e in self.ENGS}
        self.regs = []
        self.nbar = 0
        self.ninstr = 0

    def reg(self):
        r = Reg()
        self.regs.append(r)
        return r

    def regs_n(self, n):
        return [self.reg() for _ in range(n)]

    def _need(self, eng, waits, ev):
        if ev is None:
            return
        sid, val = ev
        if eng == "pe" and sid == "pe":
            return
        if self.seen[eng].get(sid, 0) >= val:
            return
        if waits.get(sid, 0) < val:
            waits[sid] = val

    def emit(self, eng, fn, reads=(), writes=(), inc=True, dma=False):
        waits = {}
        for r in reads:
            self._need(eng, waits, r.w)
        for w in writes:
            self._need(eng, waits, w.w)
            for sid, val in w.r.items():
                self._need(eng, waits, (sid, val))
        if dma:
            j = self.dcount[eng]
            self.dcount[eng] = j + 1
            slot = (eng, j % NQ)
            val = 16 * (j // NQ + 1)
            if val > 16:
                self._need(eng, waits, (slot, val - 16))
            ev = (slot, val)
            self.dlast[slot] = val
            kind = 2
        else:
            if inc:
                self.tick[eng] += 1
                ev = (eng, self.tick[eng])
                self.pending[eng] = False
                kind = 1
            else:
                ev = (eng, self.tick[eng] + 1)
                self.pending[eng] = True
                kind = 0
        for sid, val in waits.items():
            self.seen[eng][sid] = val
        for r in reads:
            if r.r.get(ev[0], 0) < ev[1]:
                r.r[ev[0]] = ev[1]
        for w in writes:
            w.w = ev
            w.r = {}
        self.prog[eng].append((list(waits.items()), fn, kind, ev))
        self.ninstr += 1
        return ev

    def barrier(self):
        for e in self.ENGS:
            assert not self.pending[e], e
        self.nbar += 1
        waits = {}
        for e in self.ENGS:
            if e != "sp" and self.tick[e] > 0:
                self._need("sp", waits, (e, self.tick[e]))
        for slot, val in self.dlast.items():
            self._need("sp", waits, (slot, val))
        for sid, val in waits.items():
            self.seen["sp"][sid] = val
        nb = self.nbar
        self.prog["sp"].append((list(waits.items()), ("bar", nb), 3, None))
        for e in self.ENGS:
            if e != "sp":
                self.prog[e].append(([("bar", nb)], None, 4, None))
                for e2 in self.ENGS:
                    self.seen[e][e2] = self.tick[e2]
                for slot, val in self.dlast.items():
                    self.seen[e][slot] = val
        for r in self.regs:
            r.w = None
            r.r = {}

    def replay(self, block):
        nc = self.nc
        sem = self.sem

        def run(engname, eh):
            for waits, fn, kind, ev in self.prog[engname]:
                for sid, val in waits:
                    eh.wait_ge(sem[sid], val)
                if kind == 3:
                    eh.sem_inc(sem["bar"], 1)
                    continue
                if kind == 4:
                    continue
                ins = fn(eh)
                if kind == 1:
                    ins.then_inc(sem[ev[0]], 1)
                elif kind == 2:
                    ins.then_inc(sem[ev[0]], 16)

        @block.tensor
        def _(e):
            run("pe", e)

        @block.scalar
        def _(e):
            run("act", e)

        @block.vector
        def _(e):
            run("dve", e)

        @block.gpsimd
        def _(e):
            run("pool", e)

        @block.sync
        def _(e):
            run("sp", e)


def _split_cols(c0, c1, w=512):
    out = []
    c = c0
    while c < c1:
        e = min(c + w, c1)
        if c < NCTX < e:
            e = NCTX
        out.append((c, e))
        c = e
    return out


class Prog:
    def __init__(self, cfg):
        self.cfg = cfg
        self.nc = bass.Bass("TRN2", target_bir_lowering=False)
        self.es = contextlib.ExitStack()
        self.inputs = {}
        self.kb = None

    def din(self, name, shape, dt=F32):
        t = self.nc.dram_tensor(name, list(shape), dt, kind="ExternalInput")
        self.inputs[name] = (tuple(shape), dt)
        return t.ap()

    def dscratch(self, name, shape, dt):
        return self.nc.dram_tensor(name, list(shape), dt, kind="Internal").ap()

    def dout(self, name, shape, dt=F32):
        return self.nc.dram_tensor(name, list(shape), dt, kind="ExternalOutput").ap()

    def carve_reset(self, mark=None):
        self.apos = self.abase if mark is None else mark

    def carve(self, shape, dt):
        n = int(np.prod(shape))
        words = n if dt == F32 else (n + 1) // 2
        words = (words + 7) // 8 * 8
        assert self.apos + words <= self.asize, ("arena overflow", self.apos, words, self.asize)
        ap = self.arena[:, self.apos:self.apos + words]
        self.apos += words
        if dt != F32:
            ap = ap.bitcast(dt)
        ap = ap[:, 0:n]
        if len(shape) == 2:
            ap = ap.rearrange("p (a b) -> p a b", b=shape[1])
        elif len(shape) == 3:
            ap = ap.rearrange("p (a b c) -> p a b c", b=shape[1], c=shape[2])
        return ap

    def carve_lo(self, shape, dt):
        save = (self.apos, self.asize)
        self.apos, self.asize = self.lopos, self.lo_end
        ap = self.carve(shape, dt)
        self.lopos = self.apos
        self.apos, self.asize = save
        return ap

    def dma(self, q, out, in_, reads, writes):
        fn = lambda e: e.dma_start(out=out, in_=in_)
        return self.kb.emit(q, fn, reads, writes, dma=True)

    def act(self, out, in_, func, reads, writes, bias=None, scale=None, eng="act"):
        kw = {}
        if bias is not None:
            kw["bias"] = bias
        if scale is not None:
            kw["scale"] = scale
        fn = lambda e: e.activation(out=out, in_=in_, func=func, **kw)
        return self.kb.emit("act", fn, reads, writes)

    def tt(self, eng, out, in0, in1, op, reads, writes):
        fn = lambda e: e.tensor_tensor(out=out, in0=in0, in1=in1, op=op)
        return self.kb.emit(eng, fn, reads, writes)

    def ts(self, eng, out, in0, s1, s2, op0, op1, reads, writes):
        if op1 is None:
            fn = lambda e: e.tensor_scalar(out=out, in0=in0, scalar1=s1, scalar2=None, op0=op0)
        else:
            fn = lambda e: e.tensor_scalar(out=out, in0=in0, scalar1=s1, scalar2=s2, op0=op0, op1=op1)
        return self.kb.emit(eng, fn, reads, writes)

    def stt(self, out, in0, scalar, in1, op0, op1, reads, writes):
        fn = lambda e: e.scalar_tensor_tensor(out=out, in0=in0, scalar=scalar, in1=in1, op0=op0, op1=op1)
        return self.kb.emit("dve", fn, reads, writes)

    def copy(self, eng, out, in_, reads, writes):
        if eng == "act":
            fn = lambda e: e.activation(out=out, in_=in_, func=AF.Copy)
        else:
            fn = lambda e: e.tensor_copy(out=out, in_=in_)
        return self.kb.emit(eng, fn, reads, writes)

    def mm(self, out, lhsT, rhs, start, stop, reads, writes, inc=None):
        fn = lambda e: e.matmul(out, lhsT=lhsT, rhs=rhs, start=start, stop=stop)
        return self.kb.emit("pe", fn, reads, writes, inc=(stop if inc is None else inc))

    def bank(self):
        b = self.pbank % 8
        self.pbank += 1
        return self.ps[:, b, :], self.psr[b]

    def build(self):
        nc, es, cfg = self.nc, self.es, self.cfg
        layers = cfg["layers"]
        P = self
        zin = P.din("zin", [D, T])
        P.Z = P.dscratch("Z", [D, T], F32)
        P.ACTT = P.dscratch("ACTT", [FF, T], BF16)
        P.OT = P.dscratch("OT", [D, T], BF16)
        outT = P.dout("outT", [D, SEQ])
        scT = P.din("scT", [128, KC * 2])
        adaw = P.din("adaw", [DEPTH, 96, 128, KC * 128])
        adab = P.din("adab", [128, DEPTH * 96])
        lng = P.din("lng", [128, DEPTH * 2 * KC])
        lnb = P.din("lnb", [128, DEPTH * 2 * KC])
        wup = P.din("wup", [DEPTH, FB, 128, KC * 256])
        wdn = P.din("wdn", [DEPTH, KC, 128, FB * 128])
        fcw = P.din("fcw", [128, DEPTH * 88 * 3])
        fcb = P.din("fcb", [128, DEPTH * 88])
        P.mix_inputs()
        dbg = {}
        for name, shape in cfg.get("debug", {}).items():
            dbg[name] = P.dout(name, shape)
        P.dbg = dbg

        P.asize = 51200
        P.arena = es.enter_context(nc.sbuf_tensor("arena", [128, P.asize], F32))[:]
        P.ps = es.enter_context(nc.psum_tensor("ps", [128, 8, 512], F32))[:]
        kb = P.kb = KB(nc, es)
        P.psr = kb.regs_n(8)
        P.pbank = 0
        P.apos = 0
        P.XT = P.carve([KC * T], BF16)
        P.XTr = kb.reg()
        P.WS = [P.carve([44 * 128], BF16) for _ in range(3)]
        P.WSr = kb.regs_n(3)
        P.wsi = 0
        P.lo_end = P.apos
        P.mod = P.carve([DEPTH * 96 * 2], F32)
        P.mod1 = P.carve([DEPTH * 96 * 2], F32)
        P.modr = kb.reg()
        P.ones = P.carve([128], BF16)
        P.onesr = kb.reg()
        P.lng_t = P.carve([DEPTH * 2 * KC], F32)
        P.lnb_t = P.carve([DEPTH * 2 * KC], F32)
        P.adab_t = P.carve([DEPTH * 96], F32)
        P.fcw_t = P.carve([88 * 3], F32)
        P.fcb_t = P.carve([88], F32)
        P.parr = kb.reg()
        P.Gp = P.carve([2 * KC], F32)
        P.Bp = P.carve([2 * KC], F32)
        P.gpr = P.carve([2 * KC], F32)
        P.gpreg = kb.reg()
        P.ident = P.carve([128], BF16)
        P.identr = kb.reg()
        P.QKr = kb.reg()
        P.bctr = {}
        P.abase = P.apos
        P.Zr = kb.reg()
        P.ACTTr = kb.reg()
        P.OTr = kb.reg()
        P.outr = kb.reg()

        with nc.Block() as block:
            kb.emit("pool", lambda e: e.memset(P.ones, 1.0), [], [P.onesr])
            if hasattr(P, "ident_in"):
                P.dma("pool", P.ident, P.ident_in, [], [P.identr])
            P.dma("sp", P.lng_t, lng, [], [P.parr])
            P.dma("sp", P.lnb_t, lnb, [], [P.parr])
            P.dma("sp", P.adab_t, adab, [], [P.parr])
            for i in range(8):
                P.dma("sp", P.Z[i * 256:(i + 1) * 256, :], zin[i * 256:(i + 1) * 256, :], [], [P.Zr])
            P.phase_ada(scT, adaw)
            kb.barrier()
            first = True
            for li in layers:
                last = (li == DEPTH - 1)
                if cfg.get("mixer", True):
                    if first:
                        P.ln_phase(li, None, 0, P.Z, 0, T)
                        kb.barrier()
                    P.mixer(li)
                    kb.barrier()
                    c0 = NCTX if last else 0
                    P.gemm_resid(li, "wo", c0)
                    kb.barrier()
                    P.ln_phase(li, 0, 3, P.Z, c0, T)
                    kb.barrier()
                else:
                    c0 = NCTX if last else 0
                    P.ln_phase(li, None, 3, P.Z, c0, T)
                    kb.barrier()
                P.dma("sp", P.fcw_t, fcw[:, li * 264:(li + 1) * 264], [], [P.parr])
                P.dma("sp", P.fcb_t, fcb[:, li * 88:(li + 1) * 88], [], [P.parr])
                P.ffn_up(li, wup, c0)
                kb.barrier()
                P.gemm_resid(li, "down", c0, wdn=wdn)
                kb.barrier()
                if last:
                    P.ln_phase(li, 1, None, outT, c0, T, dst_off=NCTX, dstr=P.outr)
                else:
                    nxt = li + 1 if cfg.get("mixer", True) else None
                    P.ln_phase(li, 1, (0 if nxt is not None else None), P.Z, 0, T, mod_layer=nxt)
                kb.barrier()
                first = False
            for name, ap in dbg.items():
                if name == "dbg_mod":
                    P.dma("sp", ap, P.mod, [P.modr], [P.outr])
                if name == "dbg_OT":
                    for i in range(8):
                        P.dma("pool", ap[i * 256:(i + 1) * 256, :], P.OT[i * 256:(i + 1) * 256, :], [P.OTr], [P.outr])
                if name == "dbg_Z":
                    for i in range(8):
                        P.dma("sp", ap[i * 256:(i + 1) * 256, :], P.Z[i * 256:(i + 1) * 256, :], [P.Zr], [P.outr])
            kb.barrier()
            kb.replay(block)
        return nc

    def wslot(self):
        i = self.wsi % 3
        self.wsi += 1
        return self.WS[i], self.WSr[i]

    def phase_ada(self, scT, adaw):
        P, kb = self, self.kb
        P.carve_reset()
        raw = P.carve([KC * 2], F32)
        sc = P.carve([KC * 2], BF16)
        rr = kb.reg()
        P.dma("sp", raw, scT, [], [rr])
        P.act(sc, raw, AF.Silu, [rr], [rr])
        sc3 = sc.rearrange("p (k s) -> p k s", s=2)
        nblk = DEPTH * 96
        loads = {}

        def load(b):
            ws, wr = P.wslot()
            l, j = divmod(b, 96)
            P.dma("pool", ws[:, 0:KC * 128], adaw[l, j], [], [wr])
            loads[b] = (ws, wr)

        for b in range(min(2, nblk)):
            load(b)
        for b in range(nblk):
            if b + 2 < nblk:
                load(b + 2)
            ws, wr = loads.pop(b)
            pb, pr = P.bank()
            for kc in range(KC):
                P.mm(pb[:, 0:2], ws[:, kc * 128:(kc + 1) * 128], sc3[:, kc, :], kc == 0, kc == KC - 1,
                     [wr, rr], [pr])
            P.act(P.mod[:, b * 2:b * 2 + 2], pb[:, 0:2], AF.Identity, [pr, P.parr], [P.modr],
                  bias=P.adab_t[:, b:b + 1])
        P.ts("dve", P.mod1, P.mod, 1.0, None, ALU.add, None, [P.modr], [P.modr])

    def modv(self, table, l, j, seg):
        v = table.rearrange("p (l j k s) -> p l j k s", l=DEPTH, j=6, k=KC)
        return v[:, l, j, :, seg]

    def ln_phase(self, li, which, modj, dst, c0, c1, dst_off=0, dstr=None, mod_layer=None):
        P, kb = self, self.kb
        ml = li if mod_layer is None else mod_layer
        dstr = P.Zr if dstr is None else dstr
        P.carve_reset()
        XT = P.XT.rearrange("p (k t) -> p k t", t=T)
        if modj is not None:
            for seg in range(2):
                Gs = P.Gp[:, seg * KC:(seg + 1) * KC]
                Bs = P.Bp[:, seg * KC:(seg + 1) * KC]
                s1 = P.modv(P.mod1, ml, modj + 1, seg)
                sh = P.modv(P.mod, ml, modj, seg)
                if which is None:
                    P.copy("dve", Gs, s1, [P.modr], [P.gpreg])
                    P.copy("dve", Bs, sh, [P.modr], [P.gpreg])
                else:
                    g = P.lng_t[:, (li * 2 + which) * KC:(li * 2 + which + 1) * KC]
                    b = P.lnb_t[:, (li * 2 + which) * KC:(li * 2 + which + 1) * KC]
                    P.tt("dve", Gs, g, s1, ALU.mult, [P.modr, P.parr], [P.gpreg])
                    P.tt("dve", Bs, b, s1, ALU.mult, [P.modr, P.parr], [P.gpreg])
                    P.tt("dve", Bs, Bs, sh, ALU.add, [P.modr, P.gpreg], [P.gpreg])
        NB = 3
        GW = 256
        rts = [P.carve([KC, GW], F32) for _ in range(NB)]
        rtr = kb.regs_n(NB)
        if which is not None:
            rsq = P.carve([KC, GW], BF16)
            rb = P.carve([KC, GW], BF16)
            sqr, rbr = kb.reg(), kb.reg()
            st = [P.carve([GW], F32) for _ in range(4)]
            stt_r = kb.regs_n(4)
            eps = LN_EPS / (DN_ALPHA * DN_ALPHA)
            epst = P.carve([1], F32)
            epsr = kb.reg()
            kb.emit("pool", lambda e: e.memset(epst, eps), [], [epsr])
        Zv = P.Z.rearrange("(k p) t -> p k t", p=128)
        dv = dst.rearrange("(k p) t -> p k t", p=128)
        groups = _split_cols(c0, c1, GW)
        for gi, (a, b_) in enumerate(groups):
            n = b_ - a
            seg = 1 if b_ <= NCTX else 0
            rt, rr = rts[gi % NB], rtr[gi % NB]
            P.dma("sp", rt[:, :, 0:n], Zv[:, :, a:b_], [P.Zr], [rr])
            if which is not None:
                P.act(rsq[:, :, 0:n], rt[:, :, 0:n], AF.Square, [rr], [sqr])
                P.copy("pool", rb[:, :, 0:n], rt[:, :, 0:n], [rr], [rbr])
                p1, p1r = P.bank()
                p2, p2r = P.bank()
                for kc in range(KC):
                    P.mm(p1[:, 0:n], P.ones, rb[:, kc, 0:n], kc == 0, kc == KC - 1, [P.onesr, rbr], [p1r])
                for kc in range(KC):
                    P.mm(p2[:, 0:n], P.ones, rsq[:, kc, 0:n], kc == 0, kc == KC - 1, [P.onesr, sqr], [p2r])
                mean, var, rstd, nmr = [s[:, 0:n] for s in st]
                P.ts("dve", mean, p1[:, 0:n], 1.0 / D, None, ALU.mult, None, [p1r], [stt_r[0]])
                P.tt("dve", var, mean, mean, ALU.mult, [stt_r[0]], [stt_r[1]])
                P.stt(var, p2[:, 0:n], 1.0 / D, var, ALU.mult, ALU.subtract, [p2r, stt_r[1]], [stt_r[1]])
                P.act(rstd, var, AF.Ln, [stt_r[1], epsr], [stt_r[2]], bias=epst)
                P.act(rstd, rstd, AF.Exp, [stt_r[2]], [stt_r[2]], scale=-0.5)
                P.stt(nmr, mean, -1.0, rstd, ALU.mult, ALU.mult, [stt_r[0], stt_r[2]], [stt_r[3]])
                rt3 = rt[:, :, 0:n]
                P.tt("dve", rt3, rt3, rstd.unsqueeze(1).to_broadcast([128, KC, n]), ALU.mult,
                     [rr, stt_r[2]], [rr])
                P.tt("dve", rt3, rt3, nmr.unsqueeze(1).to_broadcast([128, KC, n]), ALU.add,
                     [rr, stt_r[3]], [rr])
            if modj is not None:
                for kc in range(KC):
                    eng = "pool" if kc % 2 == 0 else "dve"
                    P.ts(eng, XT[:, kc, a:b_], rt[:, kc, 0:n],
                         P.Gp[:, seg * KC + kc:seg * KC + kc + 1], P.Bp[:, seg * KC + kc:seg * KC + kc + 1],
                         ALU.mult, ALU.add, [rr, P.gpreg], [P.XTr])
            if which is not None:
                g = P.lng_t[:, (li * 2 + which) * KC:(li * 2 + which + 1) * KC]
                b = P.lnb_t[:, (li * 2 + which) * KC:(li * 2 + which + 1) * KC]
                for kc in range(KC):
                    P.act(rt[:, kc, 0:n], rt[:, kc, 0:n], AF.Identity, [rr, P.parr], [rr],
                          bias=b[:, kc:kc + 1], scale=g[:, kc:kc + 1])
                P.dma("sp", dv[:, :, a - dst_off:b_ - dst_off], rt[:, :, 0:n], [rr], [dstr])

    def ffn_up(self, li, wup, c0):
        P, kb = self, self.kb
        P.carve_reset()
        XT = P.XT.rearrange("p (k t) -> p k t", t=T)
        groups = _split_cols(c0, T)
        segs = ([(0, NCTX)] if c0 == 0 else []) + [(NCTX, T)]
        ug = [P.carve([T], F32) for _ in range(2)]
        uv = [P.carve([T], F32) for _ in range(2)]
        ugr, uvr = kb.regs_n(2), kb.regs_n(2)
        cg, cv = P.carve([T], F32), P.carve([T], F32)
        cgr, cvr = kb.reg(), kb.reg()
        ao = [P.carve([T], BF16) for _ in range(2)]
        aor = kb.regs_n(2)
        loads = {}

        def load(j):
            ws, wr = P.wslot()
            P.dma("pool", ws[:, 0:KC * 256], wup[li, j], [], [wr])
            loads[j] = (ws, wr)

        for j in range(2):
            load(j)
        for j in range(FB):
            if j + 2 < FB:
                load(j + 2)
            ws, wr = loads.pop(j)
            w3 = ws[:, 0:KC * 256].rearrange("p (k c) -> p k c", c=256)
            u_g, u_v, u_gr, u_vr = ug[j % 2], uv[j % 2], ugr[j % 2], uvr[j % 2]
            for (a, b_) in groups:
                n = b_ - a
                for half, (ut, utr) in enumerate(((u_g, u_gr), (u_v, u_vr))):
                    pb, pr = P.bank()
                    for kc in range(KC):
                        P.mm(pb[:, 0:n], w3[:, kc, half * 128:(half + 1) * 128], XT[:, kc, a:b_],
                             kc == 0, kc == KC - 1, [wr, P.XTr], [pr])
                    P.copy("act", ut[:, a:b_], pb[:, 0:n], [pr], [utr])
            for half, (ut, utr, ct, ctr) in enumerate(((u_g, u_gr, cg, cgr), (u_v, u_vr, cv, cvr))):
                blk = half * FB + j
                w0 = P.fcw_t[:, blk * 3 + 0:blk * 3 + 1]
                w1 = P.fcw_t[:, blk * 3 + 1:blk * 3 + 2]
                w2 = P.fcw_t[:, blk * 3 + 2:blk * 3 + 3]
                bb = P.fcb_t[:, blk:blk + 1]
                for (s0, s1) in segs:
                    P.ts("dve", ct[:, s0:s1], ut[:, s0:s1], w1, bb, ALU.mult, ALU.add, [utr, P.parr], [ctr])
                    P.stt(ct[:, s0 + 1:s1], ut[:, s0:s1 - 1], w0, ct[:, s0 + 1:s1], ALU.mult, ALU.add,
                          [utr, ctr, P.parr], [ctr])
                    P.stt(ct[:, s0:s1 - 1], ut[:, s0 + 1:s1], w2, ct[:, s0:s1 - 1], ALU.mult, ALU.add,
                          [utr, ctr, P.parr], [ctr])
            a_o, a_or = ao[j % 2], aor[j % 2]
            P.act(cg[:, c0:T], cg[:, c0:T], AF.Silu, [cgr], [cgr])
            P.tt("dve", a_o[:, c0:T], cg[:, c0:T], cv[:, c0:T], ALU.mult, [cgr, cvr], [a_or])
            P.dma("sp", P.ACTT[j * 128:(j + 1) * 128, c0:T], a_o[:, c0:T], [a_or], [P.ACTTr])

    def gemm_resid(self, li, kind, c0, wdn=None):
        P, kb = self, self.kb
        P.carve_reset()
        if kind == "down":
            kc_n, src, wsrc, gj = FB, P.ACTT, (lambda nb: wdn[li, nb]), 5
            srcr = P.ACTTr
        else:
            kc_n, src, wsrc, gj = KC, P.OT, P.wo_src(li), 2
            srcr = P.OTr
        for seg in range(2):
            P.ts("dve", P.gpr[:, seg * KC:(seg + 1) * KC], P.modv(P.mod, li, gj, seg), 1.0 / DN_ALPHA, None,
                 ALU.mult, None, [P.modr], [P.gpreg])
        tw = min((KC * T) // kc_n, T) // 128 * 128
        tiles = []
        c = c0
        while c < T:
            e = min(c + tw, T)
            tiles.append((c, e))
            c = e
        srcv = src.rearrange("(k p) t -> p k t", p=128)
        xflat = P.XT
        zt = [P.carve([512], F32) for _ in range(3)]
        ztr = kb.regs_n(3)
        zi = 0
        for (t0, t1) in tiles:
            tn = t1 - t0
            X = xflat[:, 0:kc_n * tn].rearrange("p (k t) -> p k t", t=tn)
            P.dma("sp", X, srcv[:, :, t0:t1], [srcr], [P.XTr])
            subs = _split_cols(t0, t1)
            loads = {}

            def load(nb):
                ws, wr = P.wslot()
                P.dma("pool", ws[:, 0:kc_n * 128], wsrc(nb), [], [wr])
                loads[nb] = (ws, wr)

            for nb in range(2):
                load(nb)
            for nb in range(KC):
                if nb + 2 < KC:
                    load(nb + 2)
                ws, wr = loads.pop(nb)
                for (a, b_) in subs:
                    n = b_ - a
                    seg = 1 if b_ <= NCTX else 0
                    z, zr = zt[zi % 3], ztr[zi % 3]
                    zi += 1
                    P.dma("sp", z[:, 0:n], P.Z[nb * 128:(nb + 1) * 128, a:b_], [P.Zr], [zr])
                    pb, pr = P.bank()
                    for kc in range(kc_n):
                        P.mm(pb[:, 0:n], ws[:, kc * 128:(kc + 1) * 128], X[:, kc, a - t0:b_ - t0],
                             kc == 0, kc == kc_n - 1, [wr, P.XTr], [pr])
                    P.stt(z[:, 0:n], pb[:, 0:n], P.gpr[:, seg * KC + nb:seg * KC + nb + 1], z[:, 0:n],
                          ALU.mult, ALU.add, [pr, zr, P.gpreg], [zr])
                    P.dma("sp", P.Z[nb * 128:(nb + 1) * 128, a:b_], z[:, 0:n], [zr], [P.Zr])

    def mix_inputs(self):
        P = self
        L = self.cfg["layers"] if self.cfg.get("mixer", True) else []
        P.wo_in = {}
        if 0 in L:
            P.rw_par = P.din("rw_par", [128, KC * 15])
            P.rw_wrkv = P.din("rw_wrkv", [48, 128, KC * 128])
            P.rw_w1 = P.din("rw_w1", [2, 128, KC * 96])
            P.rw_a1 = P.din("rw_a1", [2, 128, KC * 64])
            P.rw_g1 = P.din("rw_g1", [2, 128, KC * 128])
            P.rw_w2c = P.din("rw_w2c", [KC, 128, 768])
            P.rw_msk = P.din("rw_msk", [128, 2 * 3 * 128])
            P.rw_bd = P.din("rw_bd", [128, 128])
            P.rw_cm = P.din("rw_cm", [128, T + 64])
            if not hasattr(P, "ident_in"):
                P.ident_in = P.din("ident", [128, 128])
            P.wo_in[0] = P.din("rw_wo", [KC, 128, KC * 128])
            P.XS = P.dscratch("XS", [6, D, T], BF16)
            P.RKV = [P.dscratch("RKV%d" % i, [D, T], F32) for i in range(3)]
            P.LWD = [P.dscratch("LWD%d" % i, [D, T], F32) for i in range(2)]
            P.ICD = [P.dscratch("ICD%d" % i, [D, T], F32) for i in range(2)]
            P.G32 = P.dscratch("G32", [D, T], F32)
        if 1 in L:
            P.da_wqk = P.din("da_wqk", [32, 128, KC * 256])
            P.da_wv = P.din("da_wv", [8, 128, KC * 256])
            P.da_cs = P.din("da_cs", [128, 2 * T])
            P.da_lam = P.din("da_lam", [1, 512])
            P.da_subg = P.din("da_subg", [128, 2])
            if not hasattr(P, "ident_in"):
                P.ident_in = P.din("ident", [128, 128])
            P.wo_in[1] = P.din("da_wo", [KC, 128, KC * 128])
            P.QT = P.dscratch("QT", [D, T], BF16)
            P.KT = P.dscratch("KT", [D, T], BF16)
        if 2 in L:
            P.hg_win = P.din("hg_win", [80, 128, KC * 128])
            P.hg_low = P.din("hg_low", [128, 2 * 4 * KC])
            P.hg_ng = P.din("hg_ng", [128, 1])
            P.hg_cm = P.din("hg_cm", [128, T + 64])
            P.hg_msk = P.din("hg_msk", [128, 256])
            if not hasattr(P, "ident_in"):
                P.ident_in = P.din("ident", [128, 128])
            P.wo_in[2] = P.din("hg_wo", [KC, 128, KC * 128])
        if 3 in L:
            P.lr_win = P.din("lr_win", [32, 128, KC * 128])
            P.lr_par = P.din("lr_par", [128, KC * 11])
            P.lr_wg = P.din("lr_wg", [8, 128, 2048])
            P.wo_in[3] = P.din("lr_wo", [KC, 128, KC * 128])

    def host_mix(self, inp):
        L = self.cfg["layers"] if self.cfg.get("mixer", True) else []
        m = {}

        def blocks(w, nb):
            w = np.asarray(w, np.float32)
            k = w.shape[0] // 128
            return np.ascontiguousarray(w.reshape(k, 128, nb, 128).transpose(2, 1, 0, 3)).reshape(nb, 128, k * 128)

        if 0 in L:
            f32 = np.float32
            mu = _fm(inp["rw_mu"][0]).transpose(0, 2, 1)
            w0 = _fm(inp["rw_w0"][0]).transpose(0, 2, 1)
            a0 = _fm(inp["rw_a0"][0]).transpose(0, 2, 1)
            one = lambda k: _fm(np.asarray(inp[k][0], f32).reshape(-1))[:, :, None]
            m["rw_par"] = np.ascontiguousarray(np.concatenate(
                [mu, w0, a0, one("rw_k_k"), one("rw_k_a"), one("rw_r_k"), one("rw_gn_g"), one("rw_gn_b")],
                axis=2)).reshape(128, KC * 15)
            m["rw_wrkv"] = np.concatenate([blocks(inp["rw_w_rkv"][0][n], KC) for n in range(3)], axis=0)

            def kmaj(w):
                w = np.asarray(w, f32)
                k = w.shape[0] // 128
                return np.ascontiguousarray(w.reshape(k, 128, w.shape[1]).transpose(1, 0, 2)).reshape(128, -1)

            m["rw_w1"] = np.stack([kmaj(inp["rw_w1"][0][d]) for d in range(2)])
            m["rw_a1"] = np.stack([kmaj(inp["rw_a1"][0][d]) for d in range(2)])
            g1 = np.asarray(inp["rw_g1"][0], f32)
            m["rw_g1"] = np.stack([kmaj(g1[:, j * 128:(j + 1) * 128]) for j in range(2)])
            w2c = np.zeros((KC, 128, 768), f32)
            w2 = np.asarray(inp["rw_w2"][0], f32)
            a2 = np.asarray(inp["rw_a2"][0], f32)
            g2 = np.asarray(inp["rw_g2"][0], f32)
            for nb in range(KC):
                cs_ = slice(nb * 128, (nb + 1) * 128)
                for d in range(2):
                    w2c[nb, 0:96, d * 128:(d + 1) * 128] = w2[d][:, cs_]
                    w2c[nb, 0:64, 256 + d * 128:256 + (d + 1) * 128] = a2[d][:, cs_]
                for k2 in range(2):
                    w2c[nb, :, 512 + k2 * 128:512 + (k2 + 1) * 128] = g2[k2 * 128:(k2 + 1) * 128, cs_]
            m["rw_w2c"] = w2c
            i_ = np.arange(128)
            same = (i_[:, None] // 64) == (i_[None, :] // 64)
            row, col = i_[:, None] % 64, i_[None, :] % 64
            mk = []
            for d in range(2):
                lt = (row < col) if d == 0 else (row > col)
                le = (row <= col) if d == 0 else (row >= col)
                gt = (row > col) if d == 0 else (row < col)
                mk += [same & lt, same & le, same & gt]
            m["rw_msk"] = np.concatenate(mk, axis=1).astype(f32)
            m["rw_bd"] = same.astype(f32)
            cm = np.ones((128, T + 64), f32)
            cm[:, 0::64] = 0.0
            m["rw_cm"] = cm
            m["ident"] = np.eye(128, dtype=f32)
            m["rw_wo"] = blocks(inp["rw_w_o"][0], KC)
        if 1 in L:
            w = np.asarray(inp["da_w_qkv"][0], np.float32)
            perm = np.concatenate([np.arange(32, 64), np.arange(0, 32), np.arange(96, 128), np.arange(64, 96)])
            wqk = w[:, 0:2 * D].reshape(KC, 128, 32, 128)
            wqk2 = np.stack([wqk, wqk[:, :, :, perm]], axis=3)
            m["da_wqk"] = np.ascontiguousarray(wqk2.transpose(2, 1, 0, 3, 4)).reshape(32, 128, KC * 256)
            wv = w[:, 2 * D:3 * D].reshape(KC, 128, 8, 256)
            m["da_wv"] = np.ascontiguousarray(wv.transpose(2, 1, 0, 3)).reshape(8, 128, KC * 256)
            f32 = np.float32
            row = np.repeat(np.arange(SEQ // 64, dtype=f32), 64)
            col = np.tile(np.arange(64, dtype=f32), SEQ // 64)
            inv = (f32(10000.0) ** (-np.arange(32, dtype=f32) / f32(32))).astype(f32)
            ar, ac = (row[:, None] * inv).astype(f32), (col[:, None] * inv).astype(f32)
            ang = np.concatenate([ar, ar, ac, ac], axis=-1)
            ang = np.concatenate([np.zeros((NCTX, 128), f32), ang], axis=0)
            sign = np.concatenate([-np.ones(32, f32), np.ones(32, f32), -np.ones(32, f32), np.ones(32, f32)])
            m["da_cs"] = np.ascontiguousarray(np.concatenate([np.cos(ang).T, (np.sin(ang) * sign).T], axis=1)).astype(f32)
            m["da_lam"] = np.asarray(inp["da_lambda"][0], f32).reshape(1, 512)
            m["da_subg"] = _fm(inp["da_sub_g"][0])
            m["ident"] = np.eye(128, dtype=f32)
            m["da_wo"] = blocks(inp["da_w_o"][0], KC)
        if 2 in L:
            m["hg_win"] = blocks(inp["hg_w_in"][0], 80)
            m["hg_low"] = _fm(inp["hg_lower"]).reshape(128, 2 * 4 * KC)
            m["hg_ng"] = np.asarray(inp["hg_norm_g"][0], np.float32).reshape(128, 1)
            cm = np.ones((128, T + 64), np.float32)
            cm[:, 0::64] = 0.0
            m["hg_cm"] = cm
            s_ = np.arange(128)[:, None]
            t_ = np.arange(128)[None, :]
            same = (s_ // 64) == (t_ // 64)
            m["hg_msk"] = np.concatenate([(same & (s_ <= t_)), (same & (s_ >= t_))], axis=1).astype(np.float32)
            m["ident"] = np.eye(128, dtype=np.float32)
            m["hg_wo"] = blocks(inp["hg_w_o"][0], KC)
        if 3 in L:
            m["lr_win"] = blocks(inp["lr_w_in"][0], 32)
            cw = _fm(inp["lr_conv_w"][0]).transpose(0, 2, 1)
            cb = _fm(inp["lr_conv_b"][0])[:, :, None]
            bg = _fm(inp["lr_b_gate"][0]).reshape(128, 4, KC).transpose(0, 2, 1)
            lam = _fm(inp["lr_lambda"][0]).transpose(0, 2, 1)
            m["lr_par"] = np.ascontiguousarray(np.concatenate([cw, cb, bg, lam], axis=2)).reshape(128, KC * 11)
            wg = np.asarray(inp["lr_w_gate"][0], np.float32).reshape(2, 2, 8, 2, 128, 2, 128)
            m["lr_wg"] = np.ascontiguousarray(wg.transpose(2, 4, 0, 1, 5, 3, 6)).reshape(8, 128, 2048)
            m["lr_wo"] = blocks(inp["lr_w_o"][0], KC)
        return m

    def wo_src(self, li):
        w = self.wo_in[li]
        return lambda nb: w[nb]

    def mixer(self, li):
        return [self.mixer_rw, self.mixer_da, self.mixer_hg, self.mixer_lr][li % 4](li)

    class _Pf:
        def __init__(self, P, seq, depth=2):
            self.P, self.seq, self.depth, self.nxt, self.got = P, seq, depth, 0, {}

        def get(self, i):
            while self.nxt < len(self.seq) and self.nxt <= i + self.depth:
                src, nel = self.seq[self.nxt]
                ws, wr = self.P.wslot()
                self.P.dma("pool", ws[:, 0:nel], src, [], [wr])
                self.got[self.nxt] = (ws, wr)
                self.nxt += 1
            return self.got.pop(i)

    def proj(self, lhs_fn, kc_n, mcols, rhs_fn, rreads, groups, evac):
        for (a, b_) in groups:
            pb, pr = self.bank()
            for kc in range(kc_n):
                self.mm(pb[0:mcols, 0:b_ - a], lhs_fn(kc), rhs_fn(kc, a, b_), kc == 0, kc == kc_n - 1, rreads, [pr])
            evac(pb, pr, a, b_)

    def mixer_lr(self, li):
        P, kb = self, self.kb
        P.carve_reset()
        XT = P.XT.rearrange("p (k t) -> p k t", t=T)
        groups = _split_cols(0, T)
        segs = [(0, NCTX), (NCTX, T)]
        par = P.carve([KC * 11], F32)
        par3 = par.rearrange("p (k j) -> p k j", j=11)
        negc = P.carve([KC * 2], F32)
        nc3 = negc.rearrange("p (k d) -> p k d", d=2)
        c1 = P.carve([1], F32)
        pr_ = kb.reg()
        P.dma("sp", par, P.lr_par, [], [pr_])
        kb.emit("pool", lambda e: e.memset(c1, 1.0), [], [pr_])
        P.act(nc3, par3[:, :, 9:11], AF.Exp, [pr_], [pr_], scale=-1.0)
        P.act(negc, negc, AF.Ln, [pr_], [pr_], bias=c1)
        P.ts("dve", negc, negc, -8.0, None, ALU.mult, None, [pr_], [pr_])
        names = ["R0", "R1", "C0", "C1", "A", "U", "Y0", "Y1"]
        tl = {n: P.carve([T], F32) for n in names}
        rg = {n: kb.reg() for n in names}
        CB = P.carve([2, T], BF16)
        CBr = kb.regs_n(2)
        seq = []
        for n in range(8):
            seq += [(P.lr_win[16 + 2 * n], KC * 128), (P.lr_win[17 + 2 * n], KC * 128), (P.lr_wg[n], 2048),
                    (P.lr_win[2 * n], KC * 128), (P.lr_win[2 * n + 1], KC * 128)]
        pf = P._Pf(P, seq)
        for n in range(8):
            for jb in range(2):
                ws, wr = pf.get(5 * n + jb)
                Rt, Rr = tl["R%d" % jb], rg["R%d" % jb]
                P.proj(lambda kc: ws[:, kc * 128:(kc + 1) * 128], KC, 128, lambda kc, a, b_: XT[:, kc, a:b_],
                       [wr, P.XTr], groups,
                       lambda pb, pr, a, b_: P.copy("act", Rt[:, a:b_], pb[:, 0:b_ - a], [pr], [Rr]))
            for jb in range(2):
                kcj = 2 * n + jb
                Rt, Rr, Ct, Cr = tl["R%d" % jb], rg["R%d" % jb], tl["C%d" % jb], rg["C%d" % jb]
                w = [par3[:, kcj, j:j + 1] for j in range(5)]
                for (s0, s1) in segs:
                    P.ts("dve", Ct[:, s0:s1], Rt[:, s0:s1], w[1], w[4], ALU.mult, ALU.add, [Rr, pr_], [Cr])
                    P.stt(Ct[:, s0 + 1:s1], Rt[:, s0:s1 - 1], w[0], Ct[:, s0 + 1:s1], ALU.mult, ALU.add, [Rr, Cr, pr_], [Cr])
                    P.stt(Ct[:, s0:s1 - 1], Rt[:, s0 + 1:s1], w[2], Ct[:, s0:s1 - 1], ALU.mult, ALU.add, [Rr, Cr, pr_], [Cr])
                    P.stt(Ct[:, s0:s1 - 2], Rt[:, s0 + 2:s1], w[3], Ct[:, s0:s1 - 2], ALU.mult, ALU.add, [Rr, Cr, pr_], [Cr])
                P.copy("pool", CB[:, jb, :], Ct, [Cr], [CBr[jb]])
            wg, wgr = pf.get(5 * n + 2)
            wg6 = wg[:, 0:2048].rearrange("p (d g j i c) -> p d g j i c", d=2, g=2, j=2, i=2)
            A, U, G, H = tl["A"], tl["U"], tl["R0"], tl["R1"]
            Ar, Ur, Gr, Hr = rg["A"], rg["U"], rg["R0"], rg["R1"]
            for d in range(2):
                for jb in range(2):
                    kcj = 2 * n + jb
                    Ct, Cr, Yt, Yr = tl["C%d" % jb], rg["C%d" % jb], tl["Y%d" % jb], rg["Y%d" % jb]
                    for gt, (Xt, Xr) in enumerate(((A, Ar), (U, Ur))):
                        bia = par3[:, kcj, 5 + d * 2 + gt:6 + d * 2 + gt]
                        P.proj(lambda ic: wg6[:, d, gt, jb, ic, :], 2, 128, lambda ic, a, b_: CB[:, ic, a:b_],
                               [wgr, CBr[0], CBr[1]], groups,
                               lambda pb, pr, a, b_: P.act(Xt[:, a:b_], pb[:, 0:b_ - a], AF.Sigmoid, [pr, pr_], [Xr], bias=bia))
                    P.act(A, A, AF.Exp, [Ar, pr_], [Ar], scale=nc3[:, kcj, d:d + 1])
                    P.tt("pool", G, A, A, ALU.mult, [Ar], [Gr])
                    P.act(G, G, AF.Sqrt, [Gr, pr_], [Gr], bias=c1, scale=-1.0)
                    P.tt("dve", U, U, Ct, ALU.mult, [Ur, Cr], [Ur])
                    P.tt("dve", U, U, G, ALU.mult, [Ur, Gr], [Ur])
                    if d == 0:
                        kb.emit("dve", lambda e: e.tensor_tensor_scan(out=H, data0=A, data1=U, initial=0.0,
                                                                      op0=ALU.mult, op1=ALU.add), [Ar, Ur], [Hr])
                        P.copy("pool", Yt, H, [Hr], [Yr])
                    else:
                        rv = lambda t, s0, s1: t[:, s0:s1][:, ::-1]
                        kb.emit("dve", lambda e: e.tensor_tensor_scan(out=rv(H, 0, NCTX), data0=rv(A, 0, NCTX),
                                                                      data1=rv(U, 0, NCTX), initial=0.0,
                                                                      op0=ALU.mult, op1=ALU.add), [Ar, Ur], [Hr])
                        kb.emit("dve", lambda e: e.tensor_tensor_scan(out=rv(H, NCTX, T), data0=rv(A, NCTX, T),
                                                                      data1=rv(U, NCTX, T), initial=H[:, 0:1],
                                                                      op0=ALU.mult, op1=ALU.add), [Ar, Ur, Hr], [Hr])
                        P.tt("pool", Yt, Yt, H, ALU.add, [Yr, Hr], [Yr])
            for jb in range(2):
                kcj = 2 * n + jb
                ws, wr = pf.get(5 * n + 3 + jb)
                Yt, Yr = tl["Y%d" % jb], rg["Y%d" % jb]

                def ev(pb, pr, a, b_):
                    P.copy("act", A[:, a:b_], pb[:, 0:b_ - a], [pr], [Ar])
                    P.act(U[:, a:b_], pb[:, 0:b_ - a], AF.Square, [pr], [Ur])

                P.proj(lambda kc: ws[:, kc * 128:(kc + 1) * 128], KC, 128, lambda kc, a, b_: XT[:, kc, a:b_],
                       [wr, P.XTr], groups, ev)
                P.ts("dve", U, U, 0.044715, 1.0, ALU.mult, ALU.add, [Ur], [Ur])
                P.tt("dve", U, U, A, ALU.mult, [Ur, Ar], [Ur])
                P.act(U, U, AF.Sigmoid, [Ur], [Ur], scale=1.5957691216057308)
                P.tt("pool", A, A, U, ALU.mult, [Ar, Ur], [Ar])
                P.tt("dve", CB[:, jb, :], Yt, A, ALU.mult, [Yr, Ar], [CBr[jb]])
                P.dma("sp", P.OT[kcj * 128:(kcj + 1) * 128, :], CB[:, jb, :], [CBr[jb]], [P.OTr])

    def mixer_rw(self, li):
        P, kb = self, self.kb
        XT3 = P.XT.rearrange("p (k t) -> p k t", t=T)
        groups = _split_cols(0, T)
        segs = [(0, NCTX, 1), (NCTX, T, 0)]
        rv = lambda t: t[:, ::-1]
        P.carve_reset()
        par = P.carve([KC * 15], F32)
        par3 = par.rearrange("p (k j) -> p k j", j=15)
        parr = kb.reg()
        P.dma("sp", par, P.rw_par, [], [parr])
        Hs = [P.carve([T], F32) for _ in range(2)]
        Ts = [P.carve([T], F32) for _ in range(2)]
        Ds = [P.carve([T], F32) for _ in range(2)]
        Hr, Tr_, Dr = kb.regs_n(2), kb.regs_n(2), kb.regs_n(2)
        xo = [P.carve([T], BF16) for _ in range(3)]
        xor_ = kb.regs_n(3)
        XSr = kb.reg()
        xi = 0
        for kc in range(KC):
            H, TM, DX = Hs[kc % 2], Ts[kc % 2], Ds[kc % 2]
            hr, tr, dr = Hr[kc % 2], Tr_[kc % 2], Dr[kc % 2]
            P.dma("sp", H, P.Z[kc * 128:(kc + 1) * 128, :], [P.Zr], [hr])
            for (s0, s1, sg_) in segs:
                P.act(H[:, s0:s1], H[:, s0:s1], AF.Identity, [hr, P.modr], [hr],
                      bias=P.modv(P.mod, li, 0, sg_)[:, kc:kc + 1], scale=P.modv(P.mod1, li, 1, sg_)[:, kc:kc + 1])
            for (s0, s1, sg_) in segs:
                P.tt("pool", TM[:, s0 + 1:s1 - 1], H[:, s0:s1 - 2], H[:, s0 + 2:s1], ALU.add, [hr], [tr])
                P.copy("pool", TM[:, s0:s0 + 1], H[:, s0 + 1:s0 + 2], [hr], [tr])
                P.copy("pool", TM[:, s1 - 1:s1], H[:, s1 - 2:s1 - 1], [hr], [tr])
            P.stt(DX, TM, 0.5, H, ALU.mult, ALU.subtract, [tr, hr], [dr])
            for n in range(6):
                o, orr = xo[xi % 3], xor_[xi % 3]
                xi += 1
                P.stt(o, DX, par3[:, kc, n:n + 1], H, ALU.mult, ALU.add, [dr, hr, parr], [orr])
                P.dma("sp", P.XS[n, kc * 128:(kc + 1) * 128, :], o, [orr], [XSr])
        kb.barrier()
        P.carve_reset()
        par = P.carve([KC * 15], F32)
        par3 = par.rearrange("p (k j) -> p k j", j=15)
        P.dma("sp", par, P.rw_par, [], [parr])
        stg = [P.carve([T], F32) for _ in range(3)]
        stgr = kb.regs_n(3)
        si = [0]

        def stage():
            i = si[0] % 3
            si[0] += 1
            return stg[i], stgr[i]

        MW = P.carve([2, T], BF16)
        MA = P.carve([2, T], BF16)
        MG = P.carve([2, T], BF16)
        MWr, MAr, MGr = kb.reg(), kb.reg(), kb.reg()
        XSv = P.XS.rearrange("n (k p) t -> n p k t", p=128)
        RKVr = kb.reg()
        seq = [(P.rw_wrkv[i], KC * 128) for i in range(48)]
        seq += [(P.rw_w1[d], KC * 96) for d in range(2)] + [(P.rw_a1[d], KC * 64) for d in range(2)]
        seq += [(P.rw_g1[j], KC * 128) for j in range(2)]
        pf = P._Pf(P, seq)
        xfn = lambda kc, a, b_: XT3[:, kc, a:b_]
        for n in range(3):
            P.dma("sp", XT3, XSv[n], [XSr], [P.XTr])
            for blk in range(KC):
                ws, wr = pf.get(n * KC + blk)
                st_, sr_ = stage()
                P.proj(lambda kc: ws[:, kc * 128:(kc + 1) * 128], KC, 128, xfn, [wr, P.XTr], groups,
                       lambda pb, pr, a, b_: P.copy("act", st_[:, a:b_], pb[:, 0:b_ - a], [pr], [sr_]))
                P.dma("sp", P.RKV[n][blk * 128:(blk + 1) * 128, :], st_, [sr_], [RKVr])
        P.dma("sp", XT3, XSv[3], [XSr], [P.XTr])
        for d in range(2):
            ws, wr = pf.get(48 + d)
            w3 = ws[:, 0:KC * 96].rearrange("p (k c) -> p k c", c=96)
            P.proj(lambda kc: w3[:, kc, :], KC, 96, xfn, [wr, P.XTr], groups,
                   lambda pb, pr, a, b_: P.act(MW[0:96, d, a:b_], pb[0:96, 0:b_ - a], AF.Tanh, [pr], [MWr]))
        P.dma("sp", XT3, XSv[4], [XSr], [P.XTr])
        for d in range(2):
            ws, wr = pf.get(50 + d)
            w3 = ws[:, 0:KC * 64].rearrange("p (k c) -> p k c", c=64)
            P.proj(lambda kc: w3[:, kc, :], KC, 64, xfn, [wr, P.XTr], groups,
                   lambda pb, pr, a, b_: P.copy("act", MA[0:64, d, a:b_], pb[0:64, 0:b_ - a], [pr], [MAr]))
        P.dma("sp", XT3, XSv[5], [XSr], [P.XTr])
        for j in range(2):
            ws, wr = pf.get(52 + j)
            P.proj(lambda kc: ws[:, kc * 128:(kc + 1) * 128], KC, 128, xfn, [wr, P.XTr], groups,
                   lambda pb, pr, a, b_: P.act(MG[:, j, a:b_], pb[:, 0:b_ - a], AF.Sigmoid, [pr], [MGr]))
        pf2 = P._Pf(P, [(P.rw_w2c[nb], 768) for nb in range(KC)])
        for nb in range(KC):
            ws, wr = pf2.get(nb)
            for d in range(2):
                st_, sr_ = stage()
                P.proj(lambda kc: ws[0:96, d * 128:(d + 1) * 128], 1, 128, lambda kc, a, b_: MW[0:96, d, a:b_],
                       [wr, MWr], groups,
                       lambda pb, pr, a, b_: P.act(st_[:, a:b_], pb[:, 0:b_ - a], AF.Sigmoid, [pr, parr], [sr_],
                                                   bias=par3[:, nb, 6 + d:7 + d]))
                P.ts("dve", st_, st_, -0.606531, None, ALU.mult, None, [sr_], [sr_])
                P.dma("sp", P.LWD[d][nb * 128:(nb + 1) * 128, :], st_, [sr_], [RKVr])
                st_, sr_ = stage()
                P.proj(lambda kc: ws[0:64, 256 + d * 128:256 + (d + 1) * 128], 1, 128,
                       lambda kc, a, b_: MA[0:64, d, a:b_], [wr, MAr], groups,
                       lambda pb, pr, a, b_: P.act(st_[:, a:b_], pb[:, 0:b_ - a], AF.Sigmoid, [pr, parr], [sr_],
                                                   bias=par3[:, nb, 8 + d:9 + d]))
                P.dma("sp", P.ICD[d][nb * 128:(nb + 1) * 128, :], st_, [sr_], [RKVr])
            st_, sr_ = stage()
            P.proj(lambda kc: ws[:, 512 + kc * 128:512 + (kc + 1) * 128], 2, 128, lambda kc, a, b_: MG[:, kc, a:b_],
                   [wr, MGr], groups,
                   lambda pb, pr, a, b_: P.copy("act", st_[:, a:b_], pb[:, 0:b_ - a], [pr], [sr_]))
            P.dma("sp", P.G32[nb * 128:(nb + 1) * 128, :], st_, [sr_], [RKVr])
        kb.barrier()
        P.carve_reset()
        P.lopos = 0
        lo_names = ["R", "K", "KK", "IC", "B", "LW", "TMP", "KS", "Y"]
        tl = {n: P.carve_lo([T], F32) for n in lo_names}
        rg = {n: kb.reg() for n in lo_names}
        SQb = P.carve_lo([T], BF16)
        ZV = P.carve_lo([36, 128], BF16)
        SQr, ZVr = kb.reg(), kb.reg()
        par = P.carve([KC * 15], F32)
        par3 = par.rearrange("p (k j) -> p k j", j=15)
        ZAR = P.carve([36, 2, 128], BF16)
        ZB = P.carve([36, 128], BF16)
        ZK = P.carve([36, 128], BF16)
        ZARr, ZBr, ZKr = kb.reg(), kb.reg(), kb.reg()
        cm = P.carve([T + 64], BF16)
        BD = P.carve([128], BF16)
        msk = P.carve([2 * 3 * 128], BF16)
        I32 = P.carve([128], F32)
        cst = P.carve([4], F32)
        omka = P.carve([KC], F32)
        DEC = P.carve([36], F32)
        Hp = P.carve([128], F32)
        cr, DECr, Hpr = kb.reg(), kb.reg(), kb.reg()
        P.dma("sp", par, P.rw_par, [], [cr])
        P.dma("pool", cm, P.rw_cm, [], [cr])
        P.dma("pool", BD, P.rw_bd, [], [cr])
        P.dma("pool", msk, P.rw_msk, [], [cr])
        P.dma("sp", I32, P.ident_in, [], [cr])
        kb.emit("pool", lambda e: e.memset(cst[:, 0:1], 1e-12), [], [cr])
        kb.emit("pool", lambda e: e.memset(cst[:, 1:2], 64e-5), [], [cr])
        P.ts("dve", omka, par3[:, :, 11], -1.0, 1.0, ALU.mult, ALU.add, [cr], [cr])
        kb.emit("pool", lambda e: e.memset(ZV, 0.0), [], [ZVr])
        kb.emit("pool", lambda e: e.memset(ZAR, 0.0), [], [ZARr])
        kb.emit("pool", lambda e: e.memset(ZB, 0.0), [], [ZBr])
        kb.emit("pool", lambda e: e.memset(ZK, 0.0), [], [ZKr])
        G_ = 4
        slot = []
        for i in range(G_):
            sl = {"UL": [P.carve([2, 128], F32) for _ in range(2)], "PQ": P.carve([2, 128], F32),
                  "MK": P.carve([3, 128], BF16), "Pb": P.carve([128], BF16), "TR": P.carve([4, 128], BF16),
                  "ATX": P.carve([2, 128], BF16), "CV": P.carve([128], BF16), "MT": P.carve([128], F32),
                  "RT": P.carve([128], F32), "NY": P.carve([2, 128], F32)}
            sl["r"] = {k: kb.reg() for k in ["UL0", "UL1", "PQ", "MK", "Pb", "TR", "ATX", "CV", "MT", "RT", "NY"]}
            slot.append(sl)
        R, K_, KK, IC, Bt, LW, TMP, KS, Y = [tl[n] for n in lo_names]
        Rr, Kr, KKr, ICr, Br, LWr, TMPr, KSr, Yr = [rg[n] for n in lo_names]
        c3 = lambda t: t.rearrange("p (c i) -> p c i", i=64)

        def pad_write(eng_a, dstZ, dreg, fn):
            for hp in range(2):
                rows = slice(hp * 64, (hp + 1) * 64)
                fn(dstZ[rows, :, hp * 64:(hp + 1) * 64], rows)

        def bdsum(src_bf, sreg, dst, dreg, func=AF.Copy, **kw):
            for (a, b_) in groups:
                pb, pr = P.bank()
                P.mm(pb[:, 0:b_ - a], BD, src_bf[:, a:b_], True, True, [sreg, cr], [pr])
                P.act(dst[:, a:b_], pb[:, 0:b_ - a], func, [pr, cr], [dreg], **kw)

        for kc in range(KC):
            rows128 = slice(kc * 128, (kc + 1) * 128)
            pk = lambda j: par3[:, kc, j:j + 1]
            P.dma("sp", R, P.RKV[0][rows128, :], [RKVr], [Rr])
            P.dma("sp", K_, P.RKV[1][rows128, :], [RKVr], [Kr])
            P.dma("sp", TMP, P.RKV[2][rows128, :], [RKVr], [TMPr])
            pad_write("act", ZV, ZVr, lambda o, rows: P.copy("act", o, c3(TMP)[rows], [TMPr], [ZVr]))
            P.ts("dve", KK, K_, pk(10), None, ALU.mult, None, [Kr, cr], [KKr])
            P.act(SQb, KK, AF.Square, [KKr], [SQr])
            bdsum(SQb, SQr, TMP, TMPr, AF.Ln, bias=cst[:, 0:1])
            P.act(TMP, TMP, AF.Exp, [TMPr], [TMPr], scale=-0.5)
            P.tt("dve", KK, KK, TMP, ALU.mult, [KKr, TMPr], [KKr])
            kb.emit("pool", lambda e: e.memset(KS, 0.0), [KSr], [KSr])
            kb.emit("pool", lambda e: e.memset(Y, 0.0), [Yr], [Yr])
            for d in range(2):
                mA = msk[:, (d * 3 + 0) * 128:(d * 3 + 1) * 128]
                mR = msk[:, (d * 3 + 1) * 128:(d * 3 + 2) * 128]
                mL = msk[:, (d * 3 + 2) * 128:(d * 3 + 3) * 128]
                P.dma("sp", LW, P.LWD[d][rows128, :], [RKVr], [LWr])
                P.dma("sp", IC, P.ICD[d][rows128, :], [RKVr], [ICr])
                if d == 0:
                    kb.emit("dve", lambda e: e.tensor_tensor_scan(out=Bt, data0=cm[:, 0:T], data1=LW, initial=0.0,
                                                                  op0=ALU.mult, op1=ALU.add), [LWr, cr], [Br])
                else:
                    kb.emit("dve", lambda e: e.tensor_tensor_scan(out=rv(Bt), data0=rv(cm[:, 1:T + 1]), data1=rv(LW),
                                                                  initial=0.0, op0=ALU.mult, op1=ALU.add), [LWr, cr], [Br])
                btot = c3(Bt)[:, :, 63] if d == 0 else c3(Bt)[:, :, 0]
                P.act(DEC, btot, AF.Exp, [Br], [DECr])
                P.tt("dve", TMP, Bt, LW, ALU.subtract, [Br, LWr], [TMPr])
                P.act(TMP, TMP, AF.Exp, [TMPr], [TMPr])
                pad_write("dve", ZAR[:, :, 0, :], ZARr,
                          lambda o, rows: P.tt("dve", o, c3(KK)[rows], c3(TMP)[rows], ALU.mult, [KKr, TMPr], [ZARr]))
                P.act(TMP, Bt, AF.Exp, [Br, ZARr], [TMPr])
                pad_write("dve", ZAR[:, :, 1, :], ZARr,
                          lambda o, rows: P.tt("dve", o, c3(R)[rows], c3(TMP)[rows], ALU.mult, [Rr, TMPr], [ZARr]))
                P.act(LW, Bt, AF.Exp, [Br, LWr, TMPr], [LWr], scale=-1.0)
                P.tt("pool", TMP, KK, IC, ALU.mult, [KKr, ICr, ZARr], [TMPr])
                pad_write("dve", ZB, ZBr,
                          lambda o, rows: P.stt(o, c3(TMP)[rows], -1.0, c3(LW)[rows], ALU.mult, ALU.mult,
                                                [TMPr, LWr], [ZBr]))
                P.ts("dve", TMP, IC, pk(11), omka[:, kc:kc + 1], ALU.mult, ALU.add, [ICr, cr, ZBr], [TMPr])
                P.tt("dve", TMP, TMP, K_, ALU.mult, [TMPr, Kr], [TMPr])
                P.tt("pool", KS, KS, TMP, ALU.add, [KSr, TMPr], [KSr])
                pad_write("dve", ZK, ZKr,
                          lambda o, rows: P.tt("dve", o, c3(TMP)[rows], c3(LW)[rows], ALU.mult, [TMPr, LWr], [ZKr]))
                kb.emit("pool", lambda e: e.memset(Hp, 0.0), [Hpr], [Hpr])
                order = list(range(36)) if d == 0 else [3, 2, 1, 0] + list(range(35, 3, -1))
                for g0 in range(0, 36, G_):
                    cs_ = order[g0:g0 + G_]
                    for i, c in enumerate(cs_):
                        sl = slot[i]
                        r_ = sl["r"]
                        zar2 = ZAR[:, c, :, :].rearrange("p a b -> p (a b)")
                        n1, n1r = P.bank()
                        P.mm(n1[:, 0:256], ZB[:, c, :], zar2, True, True, [ZBr, ZARr], [n1r])
                        n2, n2r = P.bank()
                        P.mm(n2[:, 0:256], ZK[:, c, :], zar2, True, True, [ZKr, ZARr], [n2r])
                        n3, n3r = P.bank()
                        P.mm(n3[:, 0:128], ZAR[:, c, 0, :], ZB[:, c, :], True, True, [ZBr, ZARr], [n3r])
                        tb_, tbr = P.bank()
                        tpv = tb_.bitcast(BF16)
                        for j, src in enumerate((ZV[:, c, :], ZB[:, c, :], ZK[:, c, :], ZAR[:, c, 0, :])):
                            kb.emit("pe", lambda e, o=tpv[:, j * 128:(j + 1) * 128], src=src:
                                    e.transpose(out=o, in_=src, identity=P.ident),
                                    [ZVr, ZBr, ZKr, ZARr, P.identr], [tbr], inc=(j == 3))
                        UL0 = sl["UL"][0]
                        P.tt("dve", UL0[:, 0, :], n1[:, 0:128], mA, ALU.mult, [n1r, cr], [r_["UL0"]])
                        P.tt("dve", sl["MK"][:, 1, :], n1[:, 128:256], mR, ALU.mult, [n1r, cr], [r_["MK"]])
                        P.tt("dve", sl["MK"][:, 0, :], n2[:, 0:128], mA, ALU.mult, [n2r, cr], [r_["MK"]])
                        P.tt("dve", sl["MK"][:, 2, :], n2[:, 128:256], mR, ALU.mult, [n2r, cr], [r_["MK"]])
                        P.tt("dve", UL0[:, 1, :], n3[:, 0:128], mL, ALU.mult, [n3r, cr], [r_["UL0"]])
                        P.copy("act", sl["TR"].rearrange("p a b -> p (a b)"), tpv[:, 0:512], [tbr], [r_["TR"]])
                        P.copy("pool", sl["PQ"][:, 0, :], I32, [cr], [r_["PQ"]])
                        P.copy("pool", sl["PQ"][:, 1, :], I32, [cr], [r_["PQ"]])
                    for j in range(6):
                        for i, c in enumerate(cs_):
                            sl = slot[i]
                            r_ = sl["r"]
                            X, Xr = sl["UL"][j % 2], r_["UL%d" % (j % 2)]
                            Xn, Xnr = sl["UL"][(j + 1) % 2], r_["UL%d" % ((j + 1) % 2)]
                            UU, LL = X[:, 0, :], X[:, 1, :]
                            PQ = sl["PQ"]
                            if j < 5:
                                pa, par_ = P.bank()
                                P.mm(pa[:, 0:128], LL, UU, True, True, [Xr], [par_], inc=False)
                                P.mm(pa[:, 128:256], UU, LL, True, True, [Xr], [par_], inc=True)
                                P.copy("act", Xn.rearrange("p a b -> p (a b)"), pa[:, 0:256], [par_], [Xnr])
                            pb, pbr = P.bank()
                            if j < 5:
                                P.mm(pb[:, 0:128], PQ[:, 1, :], UU, True, True, [Xr, r_["PQ"]], [pbr], inc=False)
                                P.mm(pb[:, 128:256], PQ[:, 0, :], LL, True, True, [Xr, r_["PQ"]], [pbr], inc=True)
                                pq2 = PQ.rearrange("p a b -> p (a b)")
                                P.tt("dve", pq2, pb[:, 0:256], pq2, ALU.add, [pbr, r_["PQ"]], [r_["PQ"]])
                            else:
                                P.mm(pb[:, 0:128], PQ[:, 1, :], UU, True, True, [Xr, r_["PQ"]], [pbr])
                                P.tt("dve", PQ[:, 0, :], pb[:, 0:128], PQ[:, 0, :], ALU.add, [pbr, r_["PQ"]], [r_["PQ"]])
                                P.copy("act", sl["Pb"], PQ[:, 0, :], [r_["PQ"]], [r_["Pb"]])
                    for i, c in enumerate(cs_):
                        sl = slot[i]
                        r_ = sl["r"]
                        TR, MK = sl["TR"], sl["MK"]
                        m1, m1r = P.bank()
                        P.mm(m1[:, 0:128], sl["Pb"], TR[:, 3, :], True, True, [r_["Pb"], r_["TR"]], [m1r], inc=False)
                        P.mm(m1[:, 128:256], MK[:, 0, :], TR[:, 0, :], True, True, [r_["MK"], r_["TR"]], [m1r], inc=True)
                        P.copy("act", sl["ATX"].rearrange("p a b -> p (a b)"), m1[:, 0:256], [m1r], [r_["ATX"]])
                    for i, c in enumerate(cs_):
                        sl = slot[i]
                        r_ = sl["r"]
                        TR, MK, ATm, X1 = sl["TR"], sl["MK"], sl["ATX"][:, 0, :], sl["ATX"][:, 1, :]
                        m2, m2r = P.bank()
                        P.mm(m2[:, 0:128], sl["Pb"], X1, True, True, [r_["Pb"], r_["ATX"]], [m2r])
                        P.copy("act", sl["CV"], m2[:, 0:128], [m2r], [r_["CV"]])
                        m3, m3r = P.bank()
                        P.mm(m3[:, 0:128], ATm, TR[:, 1, :], True, True, [r_["ATX"], r_["TR"]], [m3r], inc=False)
                        P.mm(m3[:, 128:256], ATm, MK[:, 1, :], True, True, [r_["ATX"], r_["MK"]], [m3r], inc=True)
                        P.tt("dve", sl["MT"], m3[:, 0:128], I32, ALU.add, [m3r, cr], [r_["MT"]])
                        P.tt("dve", sl["RT"], m3[:, 128:256], ZAR[:, c, 1, :], ALU.add, [m3r, ZARr], [r_["RT"]])
                    for i, c in enumerate(cs_):
                        sl = slot[i]
                        r_ = sl["r"]
                        TR, MK, CV = sl["TR"], sl["MK"], sl["CV"]
                        m4, m4r = P.bank()
                        P.mm(m4[:, 0:128], TR[:, 1, :], CV, True, False, [r_["TR"], r_["CV"]], [m4r], inc=False)
                        P.mm(m4[:, 0:128], TR[:, 2, :], TR[:, 0, :], False, True, [r_["TR"]], [m4r], inc=False)
                        P.mm(m4[:, 128:256], CV, MK[:, 1, :], True, False, [r_["CV"], r_["MK"]], [m4r], inc=False)
                        P.mm(m4[:, 128:256], TR[:, 0, :], MK[:, 2, :], False, True, [r_["TR"], r_["MK"]], [m4r], inc=True)
                        P.copy("act", sl["NY"].rearrange("p a b -> p (a b)"), m4[:, 0:256], [m4r], [r_["NY"]])
                    for i, c in enumerate(cs_):
                        sl = slot[i]
                        r_ = sl["r"]
                        yb, ybr = P.bank()
                        P.mm(yb[:, 0:128], Hp, sl["RT"], True, True, [Hpr, r_["RT"]], [ybr])
                        for hp in range(2):
                            rows = slice(hp * 64, (hp + 1) * 64)
                            yc = Y[rows, c * 64:(c + 1) * 64]
                            P.tt("dve", yc, yb[rows, hp * 64:(hp + 1) * 64], yc, ALU.add, [ybr, Yr], [Yr])
                            P.tt("pool", yc, sl["NY"][rows, 1, hp * 64:(hp + 1) * 64], yc, ALU.add, [r_["NY"], Yr], [Yr])
                        hb, hbr = P.bank()
                        P.mm(hb[:, 0:128], sl["MT"], Hp, True, True, [Hpr, r_["MT"]], [hbr])
                        P.tt("dve", Hp, hb[:, 0:128], sl["NY"][:, 0, :], ALU.add, [hbr, r_["NY"], Hpr], [Hpr])
                        P.ts("dve", Hp, Hp, DEC[:, c:c + 1], None, ALU.mult, None, [Hpr, DECr], [Hpr])
            P.tt("dve", TMP, R, KS, ALU.mult, [Rr, KSr], [TMPr])
            P.ts("dve", SQb, TMP, pk(12), None, ALU.mult, None, [TMPr, cr], [SQr])
            bdsum(SQb, SQr, LW, LWr)
            P.dma("sp", TMP, P.RKV[2][rows128, :], [RKVr, SQr], [TMPr])
            P.tt("dve", LW, LW, TMP, ALU.mult, [LWr, TMPr], [LWr])
            P.copy("act", SQb, Y, [Yr, LWr], [SQr])
            bdsum(SQb, SQr, IC, ICr)
            P.act(SQb, Y, AF.Square, [Yr, ICr], [SQr])
            bdsum(SQb, SQr, Bt, Br)
            P.ts("dve", IC, IC, 1.0 / 64, None, ALU.mult, None, [ICr], [ICr])
            P.tt("dve", TMP, IC, IC, ALU.mult, [ICr, LWr], [TMPr])
            P.stt(Bt, Bt, 1.0 / 64, TMP, ALU.mult, ALU.subtract, [Br, TMPr], [Br])
            P.act(Bt, Bt, AF.Ln, [Br, cr], [Br], bias=cst[:, 1:2])
            P.act(Bt, Bt, AF.Exp, [Br], [Br], scale=-0.5)
            P.tt("dve", Y, Y, IC, ALU.subtract, [Yr, ICr], [Yr])
            P.tt("dve", Y, Y, Bt, ALU.mult, [Yr, Br], [Yr])
            P.ts("dve", Y, Y, pk(13), pk(14), ALU.mult, ALU.add, [Yr, cr], [Yr])
            P.tt("pool", Y, Y, LW, ALU.add, [Yr, LWr], [Yr])
            P.dma("sp", TMP, P.G32[rows128, :], [RKVr, Br], [TMPr])
            P.tt("dve", SQb, Y, TMP, ALU.mult, [Yr, TMPr], [SQr])
            P.dma("sp", P.OT[rows128, :], SQb, [SQr], [P.OTr])

    def bank_in(self, lo, hi):
        c = self.bctr.get((lo, hi), 0)
        self.bctr[(lo, hi)] = c + 1
        b = lo + c % (hi - lo)
        return self.ps[:, b, :], self.psr[b]

    def mixer_da(self, li):
        P, kb = self, self.kb
        XT = P.XT.rearrange("p (k t) -> p k t", t=T)
        groups = _split_cols(0, T)
        P.carve_reset()
        cs = P.carve([2 * T], F32)
        csr = kb.reg()
        P.dma("sp", cs, P.da_cs, [], [csr])
        t1, t2 = P.carve([512], F32), P.carve([512], F32)
        t1r, t2r = kb.reg(), kb.reg()
        qo = [P.carve([T], BF16) for _ in range(2)]
        qor = kb.regs_n(2)
        pf = P._Pf(P, [(P.da_wqk[b], KC * 256) for b in range(32)])
        for blk in range(32):
            ws, wr = pf.get(blk)
            w3 = ws[:, 0:KC * 256].rearrange("p (k c) -> p k c", c=256)
            q_o, q_or = qo[blk % 2], qor[blk % 2]
            for (a, b_) in groups:
                n = b_ - a
                pa, par_ = P.bank()
                pb, pbr = P.bank()
                for kc in range(KC):
                    P.mm(pa[:, 0:n], w3[:, kc, 0:128], XT[:, kc, a:b_], kc == 0, kc == KC - 1, [wr, P.XTr], [par_])
                for kc in range(KC):
                    P.mm(pb[:, 0:n], w3[:, kc, 128:256], XT[:, kc, a:b_], kc == 0, kc == KC - 1, [wr, P.XTr], [pbr])
                P.tt("dve", t1[:, 0:n], pa[:, 0:n], cs[:, a:b_], ALU.mult, [par_, csr], [t1r])
                P.tt("dve", t2[:, 0:n], pb[:, 0:n], cs[:, T + a:T + b_], ALU.mult, [pbr, csr], [t2r])
                P.tt("pool", t1[:, 0:n], t1[:, 0:n], t2[:, 0:n], ALU.add, [t1r, t2r], [t1r])
                P.copy("act", q_o[:, a:b_], t1[:, 0:n], [t1r], [q_or])
            dst = P.QT if blk < 16 else P.KT
            r0 = (blk % 16) * 128
            P.dma("sp", dst[r0:r0 + 128, :], q_o, [q_or], [P.QKr])
        kb.barrier()
        stop = P.cfg.get("da_stop", 9)
        if stop <= 1:
            return
        P.carve_reset()
        lam_init = 0.8 - 0.6 * math.exp(-0.3 * li)
        lt = P.carve([512], F32)
        sgt = P.carve([2], F32)
        sm = P.carve([8], F32)
        smr = kb.reg()
        P.dma("sp", lt, P.da_lam.to_broadcast([128, 512]), [], [smr])
        P.dma("sp", sgt, P.da_subg, [], [smr])
        P.ts("dve", sgt, sgt, 1.0 - lam_init, None, ALU.mult, None, [smr], [smr])
        P.tt("dve", lt[:, 0:128], lt[:, 0:128], lt[:, 128:256], ALU.mult, [smr], [smr])
        P.tt("dve", lt[:, 256:384], lt[:, 256:384], lt[:, 384:512], ALU.mult, [smr], [smr])
        kb.emit("dve", lambda e: e.tensor_reduce(out=sm[:, 0:1], in_=lt[:, 0:128], axis=AX.X, op=ALU.add), [smr], [smr])
        kb.emit("dve", lambda e: e.tensor_reduce(out=sm[:, 1:2], in_=lt[:, 256:384], axis=AX.X, op=ALU.add), [smr], [smr])
        P.act(sm[:, 0:2], sm[:, 0:2], AF.Exp, [smr], [smr])
        P.stt(sm[:, 2:3], sm[:, 1:2], -lam_init, sm[:, 0:1], ALU.add, ALU.subtract, [smr], [smr])
        neglam = sm[:, 2:3]
        kb.emit("pool", lambda e: e.memset(sm[:, 3:4], 1e-5), [smr], [smr])
        epsc = sm[:, 3:4]
        Va = P.carve([18, 256], BF16)
        Var = kb.reg()
        qk = [P.carve([T], BF16) for _ in range(4)]
        qkr = kb.regs_n(4)
        PT = [P.carve([512], BF16) for _ in range(4)]
        PTr = kb.regs_n(4)
        pti = 0
        Od = P.carve([2, 512], F32)
        Odr = kb.reg()
        tmp = P.carve([2, 512], F32)
        tmpr = kb.reg()
        rl = P.carve([512], F32)
        rlr = kb.reg()
        sq = P.carve([2, 512], BF16)
        sqr = kb.reg()
        OTh = P.carve([2, T], BF16)
        OThr = kb.reg()
        pfv = P._Pf(P, [(P.da_wv[h], KC * 256) for h in range(8)], depth=1)
        for hd in range(8):
            ws, wr = pfv.get(hd)
            w3 = ws[:, 0:KC * 256].rearrange("p (k c) -> p k c", c=256)
            for tb in range(18):
                pb, pr = P.bank_in(3, 8)
                for kc in range(KC):
                    P.mm(pb[:, 0:256], XT[:, kc, tb * 128:(tb + 1) * 128], w3[:, kc, :], kc == 0, kc == KC - 1,
                         [wr, P.XTr], [pr])
                P.copy("act", Va[:, tb, :], pb[:, 0:256], [pr], [Var])
            for m in range(2):
                r0 = (hd * 2 + m) * 128
                P.dma("sp", qk[m], P.QT[r0:r0 + 128, :], [P.QKr], [qkr[m]])
                P.dma("sp", qk[2 + m], P.KT[r0:r0 + 128, :], [P.QKr], [qkr[2 + m]])
            for (a, b_) in groups:
                n = b_ - a
                nk = 2 if b_ <= NCTX else 18
                for m in range(2):
                    for kbk in range(nk):
                        sb, sr = P.bank_in(3, 7)
                        P.mm(sb[:, 0:n], qk[2 + m][:, kbk * 128:(kbk + 1) * 128], qk[m][:, a:b_], True, True,
                             [qkr[m], qkr[2 + m]], [sr])
                        pt, ptr = PT[pti % 4], PTr[pti % 4]
                        pti += 1
                        P.act(pt[:, 0:n], sb[:, 0:n], AF.Exp, [sr], [ptr], scale=float(128 ** -0.5))
                        st_, sp_ = kbk == 0, kbk == nk - 1
                        for eh in range(2):
                            P.mm(P.ps[:, eh, 0:n], Va[:, kbk, eh * 128:(eh + 1) * 128], pt[:, 0:n], st_, sp_,
                                 [ptr, Var], [P.psr[eh]])
                        P.mm(P.ps[:, 2, 0:n], P.ones, pt[:, 0:n], st_, sp_, [ptr, P.onesr], [P.psr[2]])
                    P.act(rl[:, 0:n], P.ps[:, 2, 0:n], AF.Ln, [P.psr[2]], [rlr])
                    P.act(rl[:, 0:n], rl[:, 0:n], AF.Exp, [rlr], [rlr], scale=-1.0)
                    for eh in range(2):
                        if m == 0:
                            P.tt("dve", Od[:, eh, 0:n], P.ps[:, eh, 0:n], rl[:, 0:n], ALU.mult, [P.psr[eh], rlr], [Odr])
                        else:
                            P.tt("dve", tmp[:, eh, 0:n], P.ps[:, eh, 0:n], rl[:, 0:n], ALU.mult, [P.psr[eh], rlr], [tmpr])
                            P.stt(Od[:, eh, 0:n], tmp[:, eh, 0:n], neglam, Od[:, eh, 0:n], ALU.mult, ALU.add,
                                  [tmpr, Odr, smr], [Odr])
                if stop <= 5:
                    continue
                P.act(sq[:, :, 0:n], Od[:, :, 0:n], AF.Square, [Odr], [sqr])
                rb_, rbr = P.ps[:, 7, :], P.psr[7]
                for eh in range(2):
                    P.mm(rb_[:, 0:n], P.ones, sq[:, eh, 0:n], eh == 0, eh == 1, [sqr, P.onesr], [rbr])
                P.act(rl[:, 0:n], rb_[:, 0:n], AF.Ln, [rbr, smr], [rlr], bias=epsc, scale=1.0 / 256)
                P.act(rl[:, 0:n], rl[:, 0:n], AF.Exp, [rlr], [rlr], scale=-0.5)
                for eh in range(2):
                    P.stt(OTh[:, eh, a:b_], Od[:, eh, 0:n], sgt[:, eh:eh + 1], rl[:, 0:n], ALU.mult, ALU.mult,
                          [Odr, rlr, smr], [OThr])
            for eh in range(2):
                r0 = hd * 256 + eh * 128
                P.dma("sp", P.OT[r0:r0 + 128, :], OTh[:, eh, :], [OThr], [P.OTr])

    def mixer_hg(self, li):
        P, kb = self, self.kb
        P.carve_reset()
        XT = P.XT.rearrange("p (k t) -> p k t", t=T)
        groups = _split_cols(0, T)
        rv = lambda t: t[:, ::-1]
        low = P.carve([2 * 4 * KC], F32)
        low4 = low.rearrange("p (d l k) -> p d l k", d=2, l=4)
        lb = P.carve([2 * KC], F32)
        oml = P.carve([2 * KC], F32)
        den = P.carve([2 * KC], F32)
        sm = P.carve([4], F32)
        cr = kb.reg()
        lb3 = lb.rearrange("p (d k) -> p d k", d=2)
        den3 = den.rearrange("p (d k) -> p d k", d=2)
        P.dma("sp", low, P.hg_low, [], [cr])
        P.dma("sp", sm[:, 0:1], P.hg_ng, [], [cr])
        kb.emit("pool", lambda e: e.memset(sm[:, 1:2], 1e-5), [], [cr])
        kb.emit("pool", lambda e: e.memset(sm[:, 2:3], 1.0), [], [cr])
        P.act(low, low, AF.Exp, [cr], [cr])
        P.tt("dve", den3, low4[:, :, 0, :], low4[:, :, 1, :], ALU.add, [cr], [cr])
        P.tt("dve", den3, den3, low4[:, :, 2, :], ALU.add, [cr], [cr])
        P.tt("dve", den3, den3, low4[:, :, 3, :], ALU.add, [cr], [cr])
        kb.emit("dve", lambda e: e.reciprocal(out=den, in_=den), [cr], [cr])
        P.copy("dve", lb3, low4[:, :, 1, :], [cr], [cr])
        for l in range(2, li + 1):
            P.tt("dve", lb3, lb3, low4[:, :, l, :], ALU.add, [cr], [cr])
        P.tt("dve", lb, lb, den, ALU.mult, [cr], [cr])
        P.ts("dve", oml, lb, -1.0, 1.0, ALU.mult, ALU.add, [cr], [cr])
        cm = P.carve([T + 64], BF16)
        msk = P.carve([256], BF16)
        P.dma("pool", cm, P.hg_cm, [], [cr])
        P.dma("pool", msk, P.hg_msk, [], [cr])
        names = ["Q", "F", "K", "TMP", "O"]
        tl = {n: P.carve([T], F32) for n in names}
        rg = {n: kb.reg() for n in names}
        bn = ["QD", "KD", "KE", "SQ"]
        bt = {n: P.carve([T], BF16) for n in bn}
        br = {n: kb.reg() for n in bn}
        Vt = P.carve([18, 128], BF16)
        ATT = P.carve([18, 128], BF16)
        KET = P.carve([18, 128], BF16)
        Vr, ATr, KEr = kb.reg(), kb.reg(), kb.reg()
        S = P.carve([128], F32)
        Sb = P.carve([128], BF16)
        DEC = P.carve([36], F32)
        Sr, Sbr, DECr = kb.reg(), kb.reg(), kb.reg()
        seq = []
        for hd in range(16):
            seq += [(P.hg_win[c * 16 + hd], KC * 128) for c in (0, 1, 3, 4, 2)]
        pf = P._Pf(P, seq)
        Q, F, K_, TMP, O = [tl[n] for n in names]
        Qr, Fr, Kr, TMPr, Or = [rg[n] for n in names]

        def fm_proj(ws, wr, func, dst, dstr):
            P.proj(lambda kc: ws[:, kc * 128:(kc + 1) * 128], KC, 128, lambda kc, a, b_: XT[:, kc, a:b_],
                   [wr, P.XTr], groups,
                   lambda pb, pr, a, b_: P.act(dst[:, a:b_], pb[:, 0:b_ - a], func, [pr], [dstr]))

        for hd in range(16):
            ws, wr = pf.get(5 * hd + 0)
            fm_proj(ws, wr, AF.Silu, Q, Qr)
            ws, wr = pf.get(5 * hd + 1)
            for tb in range(18):
                pb, pr = P.bank()
                for kc in range(KC):
                    P.mm(pb[:, 0:128], XT[:, kc, tb * 128:(tb + 1) * 128], ws[:, kc * 128:(kc + 1) * 128],
                         kc == 0, kc == KC - 1, [wr, P.XTr], [pr])
                P.copy("act", Vt[:, tb, :], pb[:, 0:128], [pr], [Vr])
            for d in range(2):
                ws, wr = pf.get(5 * hd + 2 + d)
                fm_proj(ws, wr, AF.Sigmoid, F, Fr)
                P.ts("dve", F, F, oml[:, d * KC + hd:d * KC + hd + 1], lb[:, d * KC + hd:d * KC + hd + 1],
                     ALU.mult, ALU.add, [Fr, cr], [Fr])
                P.ts("pool", K_, F, -1.0, 1.0, ALU.mult, ALU.add, [Fr], [Kr])
                P.act(TMP, F, AF.Ln, [Fr], [TMPr])
                B = F
                if d == 0:
                    kb.emit("dve", lambda e: e.tensor_tensor_scan(out=B, data0=cm[:, 0:T], data1=TMP, initial=0.0,
                                                                  op0=ALU.mult, op1=ALU.add), [TMPr, cr, Fr], [Fr])
                else:
                    kb.emit("dve", lambda e: e.tensor_tensor_scan(out=rv(B), data0=rv(cm[:, 1:T + 1]), data1=rv(TMP),
                                                                  initial=0.0, op0=ALU.mult, op1=ALU.add),
                            [TMPr, cr, Fr], [Fr])
                B3 = B.rearrange("p (c i) -> p c i", i=64)
                btot = B3[:, :, 63] if d == 0 else B3[:, :, 0]
                P.act(TMP, B, AF.Exp, [Fr], [TMPr])
                P.tt("dve", bt["QD"], Q, TMP, ALU.mult, [Qr, TMPr], [br["QD"]])
                P.act(TMP, B, AF.Exp, [Fr, br["QD"]], [TMPr], scale=-1.0)
                P.tt("dve", bt["KD"], K_, TMP, ALU.mult, [Kr, TMPr], [br["KD"]])
                P.act(DEC, btot, AF.Exp, [Fr], [DECr])
                T3 = TMP.rearrange("p (c i) -> p c i", i=64)
                P.tt("dve", T3, btot.unsqueeze(2).to_broadcast([128, 36, 64]), B3, ALU.subtract, [Fr, br["KD"]], [TMPr])
                P.act(TMP, TMP, AF.Exp, [TMPr], [TMPr])
                P.tt("dve", bt["KE"], K_, TMP, ALU.mult, [Kr, TMPr], [br["KE"]])
                mk = msk[:, d * 128:(d + 1) * 128]
                for tb in range(18):
                    pb, pr = P.bank_in(0, 4)
                    P.mm(pb[:, 0:128], bt["KD"][:, tb * 128:(tb + 1) * 128], bt["QD"][:, tb * 128:(tb + 1) * 128],
                         True, True, [br["KD"], br["QD"]], [pr])
                    P.tt("dve", ATT[:, tb, :], pb[:, 0:128], mk, ALU.mult, [pr, cr], [ATr])
                    tb_, tr = P.bank_in(0, 4)
                    tpv = tb_.bitcast(BF16)[:, 0:128]
                    src = bt["KE"][:, tb * 128:(tb + 1) * 128]
                    kb.emit("pe", lambda e, tpv=tpv, src=src: e.transpose(out=tpv, in_=src, identity=P.ident),
                            [br["KE"], P.identr], [tr])
                    P.copy("act", KET[:, tb, :], tpv, [tr], [KEr])
                kb.emit("pool", lambda e: e.memset(S, 0.0), [Sr], [Sr])
                kb.emit("pool", lambda e: e.memset(Sb, 0.0), [Sbr], [Sbr])
                order = list(range(18)) if d == 0 else [1, 0] + list(range(17, 1, -1))
                for tb in order:
                    ob, obr = P.bank_in(4, 6)
                    P.mm(ob[:, 0:128], Vt[:, tb, :], ATT[:, tb, :], True, False, [Vr, ATr], [obr], inc=False)
                    halves = (0, 1) if d == 0 else (1, 0)
                    for hi, hf in enumerate(halves):
                        c = 2 * tb + hf
                        co = hf * 64
                        P.mm(ob[:, co:co + 64], Sb, bt["QD"][:, c * 64:(c + 1) * 64], False, hi == 1,
                             [Sbr, br["QD"]], [obr], inc=True)
                        sb_, sbr_ = P.bank_in(6, 8)
                        P.mm(sb_[:, 0:128], KET[co:co + 64, tb, :], Vt[co:co + 64, tb, :], True, True,
                             [KEr, Vr], [sbr_])
                        P.stt(S, S, DEC[:, c:c + 1], sb_[:, 0:128], ALU.mult, ALU.add, [Sr, DECr, sbr_], [Sr])
                        P.copy("act", Sb, S, [Sr], [Sbr])
                    if d == 0:
                        P.copy("act", O[:, tb * 128:(tb + 1) * 128], ob[:, 0:128], [obr], [Or])
                    else:
                        P.tt("dve", O[:, tb * 128:(tb + 1) * 128], ob[:, 0:128], O[:, tb * 128:(tb + 1) * 128],
                             ALU.add, [obr, Or], [Or])
            ws, wr = pf.get(5 * hd + 4)
            fm_proj(ws, wr, AF.Silu, F, Fr)
            P.act(bt["SQ"], O, AF.Square, [Or], [br["SQ"]])
            for (a, b_) in groups:
                n = b_ - a
                pb, pr = P.bank_in(0, 4)
                P.mm(pb[:, 0:n], P.ones, bt["SQ"][:, a:b_], True, True, [br["SQ"], P.onesr], [pr])
                P.act(TMP[:, a:b_], pb[:, 0:n], AF.Ln, [pr, cr], [TMPr], bias=sm[:, 1:2], scale=1.0 / 128)
            P.act(TMP, TMP, AF.Exp, [TMPr], [TMPr], scale=-0.5)
            P.stt(O, O, sm[:, 0:1], TMP, ALU.mult, ALU.mult, [Or, TMPr, cr], [Or])
            P.tt("dve", bt["SQ"], O, F, ALU.mult, [Or, Fr], [br["SQ"]])
            P.dma("sp", P.OT[hd * 128:(hd + 1) * 128, :], bt["SQ"], [br["SQ"]], [P.OTr])


def _fm(v, nblk=None):
    v = np.asarray(v, np.float32)
    lead = v.shape[:-1]
    nb = v.shape[-1] // 128
    v = v.reshape(lead + (nb, 128))
    return np.ascontiguousarray(np.moveaxis(v, -1, 0))


def host_common(inp):
    m = {}
    aw = np.asarray(inp["ada_w"], np.float32)
    m["adaw"] = np.ascontiguousarray(
        aw.reshape(DEPTH, KC, 128, 96, 128).transpose(0, 3, 2, 1, 4)).reshape(DEPTH, 96, 128, KC * 128)
    m["adab"] = _fm(inp["ada_b"]).reshape(128, DEPTH * 96)
    m["lng"] = _fm(inp["ln_g"]).reshape(128, DEPTH * 2 * KC)
    m["lnb"] = _fm(inp["ln_b"]).reshape(128, DEPTH * 2 * KC)
    wu = np.asarray(inp["ffn_w_up"], np.float32).reshape(DEPTH, KC, 128, 2, FB, 128)
    m["wup"] = np.ascontiguousarray(wu.transpose(0, 4, 2, 1, 3, 5)).reshape(DEPTH, FB, 128, KC * 256)
    wd = np.asarray(inp["ffn_w_down"], np.float32).reshape(DEPTH, FB, 128, KC, 128)
    m["wdn"] = np.ascontiguousarray(wd.transpose(0, 3, 2, 1, 4)).reshape(DEPTH, KC, 128, FB * 128)
    cw = _fm(inp["ffn_conv_w"])
    m["fcw"] = np.ascontiguousarray(cw.transpose(0, 1, 3, 2)).reshape(128, DEPTH * 88 * 3)
    m["fcb"] = _fm(inp["ffn_conv_b"]).reshape(128, DEPTH * 88)
    return m


def host_core(inp, b):
    m = {}
    z = np.concatenate([np.asarray(inp["ctx"][b], np.float32), np.asarray(inp["x"][b], np.float32)], axis=0)
    m["zin"] = np.ascontiguousarray(z.T)
    sc = np.stack([np.asarray(inp["c"][b], np.float32), np.asarray(inp["c_ctx"], np.float32)], axis=-1)
    m["scT"] = np.ascontiguousarray(sc.reshape(KC, 128, 2).transpose(1, 0, 2)).reshape(128, KC * 2)
    return m


_CACHE = {}


def kernel(**inputs):
    cfg = {"layers": [0, 1, 2, 3], "mixer": True}
    if "prog" not in _CACHE:
        p = Prog(cfg)
        _CACHE["prog"] = (p, p.build())
    p, nc = _CACHE["prog"]
    common = host_common(inputs)
    common.update(p.host_mix(inputs))
    in_maps = []
    for core in range(NCORES):
        m = dict(common)
        m.update(host_core(inputs, core % 4))
        in_maps.append(m)
    res = run_bass_kernel_spmd(nc, in_maps, core_ids=list(range(NCORES)))
    out = np.stack([np.ascontiguousarray(res.results[b]["outT"].T) for b in range(4)], axis=0)
    return out.astype(np.float32)
```

```python
import contextlib
import math
import numpy as np
import concourse.bass as bass
import concourse.mybir as mybir
from concourse.bass_utils import run_bass_kernel_spmd

F32 = mybir.dt.float32
BF16 = mybir.dt.bfloat16
AF = mybir.ActivationFunctionType
ALU = mybir.AluOpType
AX = mybir.AxisListType

D = 2048
KC = 16
NCTX = 256
SEQ = 2048
T = NCTX + SEQ
FF = 5632
FB = FF // 128
DEPTH = 4
DN_ALPHA = (2 * DEPTH) ** 0.25
LN_EPS = 1e-5
NQ = 20
NCORES = 4


class Reg:
    __slots__ = ("w", "r")

    def __init__(self):
        self.w = None
        self.r = {}


class KB:
    ENGS = ["pe", "act", "dve", "pool", "sp"]

    def __init__(self, nc, es):
        self.nc = nc
        self.sem = {}
        for e in self.ENGS:
            self.sem[e] = es.enter_context(nc.semaphore("s_" + e))
        self.sem["bar"] = es.enter_context(nc.semaphore("s_bar"))
        self.dq = ["sp", "pool", "act"]
        for q in self.dq:
            for j in range(NQ):
                self.sem[(q, j)] = es.enter_context(nc.semaphore("d_%s_%d" % (q, j)))
        self.tick = {e: 0 for e in self.ENGS}
        self.pending = {e: False for e in self.ENGS}
        self.dcount = {q: 0 for q in self.dq}
        self.dlast = {}
        self.seen = {e: {} for e in self.ENGS}
        self.prog = {e: [] for e in self.ENGS}
        self.regs = []
        self.nbar = 0
        self.ninstr = 0

    def reg(self):
        r = Reg()
        self.regs.append(r)
        return r

    def regs_n(self, n):
        return [self.reg() for _ in range(n)]

    def _need(self, eng, waits, ev):
        if ev is None:
            return
        sid, val = ev
        if eng == "pe" and sid == "pe":
            return
        if self.seen[eng].get(sid, 0) >= val:
            return
        if waits.get(sid, 0) < val:
            waits[sid] = val

    def emit(self, eng, fn, reads=(), writes=(), inc=True, dma=False):
        waits = {}
        for r in reads:
            self._need(eng, waits, r.w)
        for w in writes:
            self._need(eng, waits, w.w)
            for sid, val in w.r.items():
                self._need(eng, waits, (sid, val))
        if dma:
            j = self.dcount[eng]
            self.dcount[eng] = j + 1
            slot = (eng, j % NQ)
            val = 16 * (j // NQ + 1)
            if val > 16:
                self._need(eng, waits, (slot, val - 16))
            ev = (slot, val)
            self.dlast[slot] = val
            kind = 2
        else:
            if inc:
                self.tick[eng] += 1
                ev = (eng, self.tick[eng])
                self.pending[eng] = False
                kind = 1
            else:
                ev = (eng, self.tick[eng] + 1)
                self.pending[eng] = True
                kind = 0
        for sid, val in waits.items():
            self.seen[eng][sid] = val
        for r in reads:
            if r.r.get(ev[0], 0) < ev[1]:
                r.r[ev[0]] = ev[1]
        for w in writes:
            w.w = ev
            w.r = {}
        self.prog[eng].append((list(waits.items()), fn, kind, ev))
        self.ninstr += 1
        return ev

    def barrier(self):
        for e in self.ENGS:
            assert not self.pending[e], e
        self.nbar += 1
        waits = {}
        for e in self.ENGS:
            if e != "sp" and self.tick[e] > 0:
                self._need("sp", waits, (e, self.tick[e]))
        for slot, val in self.dlast.items():
            self._need("sp", waits, (slot, val))
        for sid, val in waits.items():
            self.seen["sp"][sid] = val
        nb = self.nbar
        self.prog["sp"].append((list(waits.items()), ("bar", nb), 3, None))
        for e in self.ENGS:
            if e != "sp":
                self.prog[e].append(([("bar", nb)], None, 4, None))
                for e2 in self.ENGS:
                    self.seen[e][e2] = self.tick[e2]
                for slot, val in self.dlast.items():
                    self.seen[e][slot] = val
        for r in self.regs:
            r.w = None
            r.r = {}

    def replay(self, block):
        nc = self.nc
        sem = self.sem

        def run(engname, eh):
            for waits, fn, kind, ev in self.prog[engname]:
                for sid, val in waits:
                    eh.wait_ge(sem[sid], val)
                if kind == 3:
                    eh.sem_inc(sem["bar"], 1)
                    continue
                if kind == 4:
                    continue
                ins = fn(eh)
                if kind == 1:
                    ins.then_inc(sem[ev[0]], 1)
                elif kind == 2:
                    ins.then_inc(sem[ev[0]], 16)

        @block.tensor
        def _(e):
            run("pe", e)

        @block.scalar
        def _(e):
            run("act", e)

        @block.vector
        def _(e):
            run("dve", e)

        @block.gpsimd
        def _(e):
            run("pool", e)

        @block.sync
        def _(e):
            run("sp", e)


def _split_cols(c0, c1, w=512):
    out = []
    c = c0
    while c < c1:
        e = min(c + w, c1)
        if c < NCTX < e:
            e = NCTX
        out.append((c, e))
        c = e
    return out


class Prog:
    def __init__(self, cfg):
        self.cfg = cfg
        self.nc = bass.Bass("TRN2", target_bir_lowering=False)
        self.es = contextlib.ExitStack()
        self.inputs = {}
        self.kb = None

    def din(self, name, shape, dt=F32):
        t = self.nc.dram_tensor(name, list(shape), dt, kind="ExternalInput")
        self.inputs[name] = (tuple(shape), dt)
        return t.ap()

    def dscratch(self, name, shape, dt):
        return self.nc.dram_tensor(name, list(shape), dt, kind="Internal").ap()

    def dout(self, name, shape, dt=F32):
        return self.nc.dram_tensor(name, list(shape), dt, kind="ExternalOutput").ap()

    def carve_reset(self, mark=None):
        self.apos = self.abase if mark is None else mark

    def carve(self, shape, dt):
        n = int(np.prod(shape))
        words = n if dt == F32 else (n + 1) // 2
        words = (words + 7) // 8 * 8
        assert self.apos + words <= self.asize, ("arena overflow", self.apos, words, self.asize)
        ap = self.arena[:, self.apos:self.apos + words]
        self.apos += words
        if dt != F32:
            ap = ap.bitcast(dt)
        ap = ap[:, 0:n]
        if len(shape) == 2:
            ap = ap.rearrange("p (a b) -> p a b", b=shape[1])
        elif len(shape) == 3:
            ap = ap.rearrange("p (a b c) -> p a b c", b=shape[1], c=shape[2])
        return ap

    def carve_lo(self, shape, dt):
        save = (self.apos, self.asize)
        self.apos, self.asize = self.lopos, self.lo_end
        ap = self.carve(shape, dt)
        self.lopos = self.apos
        self.apos, self.asize = save
        return ap

    def dma(self, q, out, in_, reads, writes):
        fn = lambda e: e.dma_start(out=out, in_=in_)
        return self.kb.emit(q, fn, reads, writes, dma=True)

    def act(self, out, in_, func, reads, writes, bias=None, scale=None, eng="act"):
        kw = {}
        if bias is not None:
            kw["bias"] = bias
        if scale is not None:
            kw["scale"] = scale
        fn = lambda e: e.activation(out=out, in_=in_, func=func, **kw)
        return self.kb.emit("act", fn, reads, writes)

    def tt(self, eng, out, in0, in1, op, reads, writes):
        fn = lambda e: e.tensor_tensor(out=out, in0=in0, in1=in1, op=op)
        return self.kb.emit(eng, fn, reads, writes)

    def ts(self, eng, out, in0, s1, s2, op0, op1, reads, writes):
        if op1 is None:
            fn = lambda e: e.tensor_scalar(out=out, in0=in0, scalar1=s1, scalar2=None, op0=op0)
        else:
            fn = lambda e: e.tensor_scalar(out=out, in0=in0, scalar1=s1, scalar2=s2, op0=op0, op1=op1)
        return self.kb.emit(eng, fn, reads, writes)

    def stt(self, out, in0, scalar, in1, op0, op1, reads, writes):
        fn = lambda e: e.scalar_tensor_tensor(out=out, in0=in0, scalar=scalar, in1=in1, op0=op0, op1=op1)
        return self.kb.emit("dve", fn, reads, writes)

    def copy(self, eng, out, in_, reads, writes):
        if eng == "act":
            fn = lambda e: e.activation(out=out, in_=in_, func=AF.Copy)
        else:
            fn = lambda e: e.tensor_copy(out=out, in_=in_)
        return self.kb.emit(eng, fn, reads, writes)

    def mm(self, out, lhsT, rhs, start, stop, reads, writes, inc=None):
        fn = lambda e: e.matmul(out, lhsT=lhsT, rhs=rhs, start=start, stop=stop)
        return self.kb.emit("pe", fn, reads, writes, inc=(stop if inc is None else inc))

    def bank(self):
        b = self.pbank % 8
        self.pbank += 1
        return self.ps[:, b, :], self.psr[b]

    def build(self):
        nc, es, cfg = self.nc, self.es, self.cfg
        layers = cfg["layers"]
        P = self
        zin = P.din("zin", [D, T])
        P.Z = P.dscratch("Z", [D, T], F32)
        P.ACTT = P.dscratch("ACTT", [FF, T], BF16)
        P.OT = P.dscratch("OT", [D, T], BF16)
        outT = P.dout("outT", [D, SEQ])
        scT = P.din("scT", [128, KC * 2])
        adaw = P.din("adaw", [DEPTH, 96, 128, KC * 128])
        adab = P.din("adab", [128, DEPTH * 96])
        lng = P.din("lng", [128, DEPTH * 2 * KC])
        lnb = P.din("lnb", [128, DEPTH * 2 * KC])
        wup = P.din("wup", [DEPTH, FB, 128, KC * 256])
        wdn = P.din("wdn", [DEPTH, KC, 128, FB * 128])
        fcw = P.din("fcw", [128, DEPTH * 88 * 3])
        fcb = P.din("fcb", [128, DEPTH * 88])
        P.mix_inputs()
        dbg = {}
        for name, shape in cfg.get("debug", {}).items():
            dbg[name] = P.dout(name, shape)
        P.dbg = dbg

        P.asize = 51200
        P.arena = es.enter_context(nc.sbuf_tensor("arena", [128, P.asize], F32))[:]
        P.ps = es.enter_context(nc.psum_tensor("ps", [128, 8, 512], F32))[:]
        kb = P.kb = KB(nc, es)
        P.psr = kb.regs_n(8)
        P.pbank = 0
        P.apos = 0
        P.XT = P.carve([KC * T], BF16)
        P.XTr = kb.reg()
        P.WS = [P.carve([44 * 128], BF16) for _ in range(3)]
        P.WSr = kb.regs_n(3)
        P.wsi = 0
        P.lo_end = P.apos
        P.mod = P.carve([DEPTH * 96 * 2], F32)
        P.mod1 = P.carve([DEPTH * 96 * 2], F32)
        P.modr = kb.reg()
        P.ones = P.carve([128], BF16)
        P.onesr = kb.reg()
        P.lng_t = P.carve([DEPTH * 2 * KC], F32)
        P.lnb_t = P.carve([DEPTH * 2 * KC], F32)
        P.adab_t = P.carve([DEPTH * 96], F32)
        P.fcw_t = P.carve([88 * 3], F32)
        P.fcb_t = P.carve([88], F32)
        P.parr = kb.reg()
        P.Gp = P.carve([2 * KC], F32)
        P.Bp = P.carve([2 * KC], F32)
        P.gpr = P.carve([2 * KC], F32)
        P.gpreg = kb.reg()
        P.ident = P.carve([128], BF16)
        P.identr = kb.reg()
        P.QKr = kb.reg()
        P.bctr = {}
        P.abase = P.apos
        P.Zr = kb.reg()
        P.ACTTr = kb.reg()
        P.OTr = kb.reg()
        P.outr = kb.reg()

        with nc.Block() as block:
            kb.emit("pool", lambda e: e.memset(P.ones, 1.0), [], [P.onesr])
            if hasattr(P, "ident_in"):
                P.dma("pool", P.ident, P.ident_in, [], [P.identr])
            P.dma("sp", P.lng_t, lng, [], [P.parr])
            P.dma("sp", P.lnb_t, lnb, [], [P.parr])
            P.dma("sp", P.adab_t, adab, [], [P.parr])
            for i in range(8):
                P.dma("sp", P.Z[i * 256:(i + 1) * 256, :], zin[i * 256:(i + 1) * 256, :], [], [P.Zr])
            P.phase_ada(scT, adaw)
            kb.barrier()
            first = True
            for li in layers:
                last = (li == DEPTH - 1)
                if cfg.get("mixer", True):
                    if first:
                        P.ln_phase(li, None, 0, P.Z, 0, T)
                        kb.barrier()
                    P.mixer(li)
                    kb.barrier()
                    c0 = NCTX if last else 0
                    P.gemm_resid(li, "wo", c0)
                    kb.barrier()
                    P.ln_phase(li, 0, 3, P.Z, c0, T)
                    kb.barrier()
                else:
                    c0 = NCTX if last else 0
                    P.ln_phase(li, None, 3, P.Z, c0, T)
                    kb.barrier()
                P.dma("sp", P.fcw_t, fcw[:, li * 264:(li + 1) * 264], [], [P.parr])
                P.dma("sp", P.fcb_t, fcb[:, li * 88:(li + 1) * 88], [], [P.parr])
                P.ffn_up(li, wup, c0)
                kb.barrier()
                P.gemm_resid(li, "down", c0, wdn=wdn)
                kb.barrier()
                if last:
                    P.ln_phase(li, 1, None, outT, c0, T, dst_off=NCTX, dstr=P.outr)
                else:
                    nxt = li + 1 if cfg.get("mixer", True) else None
                    P.ln_phase(li, 1, (0 if nxt is not None else None), P.Z, 0, T, mod_layer=nxt)
                kb.barrier()
                first = False
            for name, ap in dbg.items():
                if name == "dbg_mod":
                    P.dma("sp", ap, P.mod, [P.modr], [P.outr])
                if name == "dbg_OT":
                    for i in range(8):
                        P.dma("pool", ap[i * 256:(i + 1) * 256, :], P.OT[i * 256:(i + 1) * 256, :], [P.OTr], [P.outr])
                if name == "dbg_Z":
                    for i in range(8):
                        P.dma("sp", ap[i * 256:(i + 1) * 256, :], P.Z[i * 256:(i + 1) * 256, :], [P.Zr], [P.outr])
            kb.barrier()
            kb.replay(block)
        return nc

    def wslot(self):
        i = self.wsi % 3
        self.wsi += 1
        return self.WS[i], self.WSr[i]

    def phase_ada(self, scT, adaw):
        P, kb = self, self.kb
        P.carve_reset()
        raw = P.carve([KC * 2], F32)
        sc = P.carve([KC * 2], BF16)
        rr = kb.reg()
        P.dma("sp", raw, scT, [], [rr])
        P.act(sc, raw, AF.Silu, [rr], [rr])
        sc3 = sc.rearrange("p (k s) -> p k s", s=2)
        nblk = DEPTH * 96
        loads = {}

        def load(b):
            ws, wr = P.wslot()
            l, j = divmod(b, 96)
            P.dma("pool", ws[:, 0:KC * 128], adaw[l, j], [], [wr])
            loads[b] = (ws, wr)

        for b in range(min(2, nblk)):
            load(b)
        for b in range(nblk):
            if b + 2 < nblk:
                load(b + 2)
            ws, wr = loads.pop(b)
            pb, pr = P.bank()
            for kc in range(KC):
                P.mm(pb[:, 0:2], ws[:, kc * 128:(kc + 1) * 128], sc3[:, kc, :], kc == 0, kc == KC - 1,
                     [wr, rr], [pr])
            P.act(P.mod[:, b * 2:b * 2 + 2], pb[:, 0:2], AF.Identity, [pr, P.parr], [P.modr],
                  bias=P.adab_t[:, b:b + 1])
        P.ts("dve", P.mod1, P.mod, 1.0, None, ALU.add, None, [P.modr], [P.modr])

    def modv(self, table, l, j, seg):
        v = table.rearrange("p (l j k s) -> p l j k s", l=DEPTH, j=6, k=KC)
        return v[:, l, j, :, seg]

    def ln_phase(self, li, which, modj, dst, c0, c1, dst_off=0, dstr=None, mod_layer=None):
        P, kb = self, self.kb
        ml = li if mod_layer is None else mod_layer
        dstr = P.Zr if dstr is None else dstr
        P.carve_reset()
        XT = P.XT.rearrange("p (k t) -> p k t", t=T)
        if modj is not None:
            for seg in range(2):
                Gs = P.Gp[:, seg * KC:(seg + 1) * KC]
                Bs = P.Bp[:, seg * KC:(seg + 1) * KC]
                s1 = P.modv(P.mod1, ml, modj + 1, seg)
                sh = P.modv(P.mod, ml, modj, seg)
                if which is None:
                    P.copy("dve", Gs, s1, [P.modr], [P.gpreg])
                    P.copy("dve", Bs, sh, [P.modr], [P.gpreg])
                else:
                    g = P.lng_t[:, (li * 2 + which) * KC:(li * 2 + which + 1) * KC]
                    b = P.lnb_t[:, (li * 2 + which) * KC:(li * 2 + which + 1) * KC]
                    P.tt("dve", Gs, g, s1, ALU.mult, [P.modr, P.parr], [P.gpreg])
                    P.tt("dve", Bs, b, s1, ALU.mult, [P.modr, P.parr], [P.gpreg])
                    P.tt("dve", Bs, Bs, sh, ALU.add, [P.modr, P.gpreg], [P.gpreg])
        NB = 3
        GW = 256
        rts = [P.carve([KC, GW], F32) for _ in range(NB)]
        rtr = kb.regs_n(NB)
        if which is not None:
            rsq = P.carve([KC, GW], BF16)
            rb = P.carve([KC, GW], BF16)
            sqr, rbr = kb.reg(), kb.reg()
            st = [P.carve([GW], F32) for _ in range(4)]
            stt_r = kb.regs_n(4)
            eps = LN_EPS / (DN_ALPHA * DN_ALPHA)
            epst = P.carve([1], F32)
            epsr = kb.reg()
            kb.emit("pool", lambda e: e.memset(epst, eps), [], [epsr])
        Zv = P.Z.rearrange("(k p) t -> p k t", p=128)
        dv = dst.rearrange("(k p) t -> p k t", p=128)
        groups = _split_cols(c0, c1, GW)
        for gi, (a, b_) in enumerate(groups):
            n = b_ - a
            seg = 1 if b_ <= NCTX else 0
            rt, rr = rts[gi % NB], rtr[gi % NB]
            P.dma("sp", rt[:, :, 0:n], Zv[:, :, a:b_], [P.Zr], [rr])
            if which is not None:
                P.act(rsq[:, :, 0:n], rt[:, :, 0:n], AF.Square, [rr], [sqr])
                P.copy("pool", rb[:, :, 0:n], rt[:, :, 0:n], [rr], [rbr])
                p1, p1r = P.bank()
                p2, p2r = P.bank()
                for kc in range(KC):
                    P.mm(p1[:, 0:n], P.ones, rb[:, kc, 0:n], kc == 0, kc == KC - 1, [P.onesr, rbr], [p1r])
                for kc in range(KC):
                    P.mm(p2[:, 0:n], P.ones, rsq[:, kc, 0:n], kc == 0, kc == KC - 1, [P.onesr, sqr], [p2r])
                mean, var, rstd, nmr = [s[:, 0:n] for s in st]
                P.ts("dve", mean, p1[:, 0:n], 1.0 / D, None, ALU.mult, None, [p1r], [stt_r[0]])
                P.tt("dve", var, mean, mean, ALU.mult, [stt_r[0]], [stt_r[1]])
                P.stt(var, p2[:, 0:n], 1.0 / D, var, ALU.mult, ALU.subtract, [p2r, stt_r[1]], [stt_r[1]])
                P.act(rstd, var, AF.Ln, [stt_r[1], epsr], [stt_r[2]], bias=epst)
                P.act(rstd, rstd, AF.Exp, [stt_r[2]], [stt_r[2]], scale=-0.5)
                P.stt(nmr, mean, -1.0, rstd, ALU.mult, ALU.mult, [stt_r[0], stt_r[2]], [stt_r[3]])
                rt3 = rt[:, :, 0:n]
                P.tt("dve", rt3, rt3, rstd.unsqueeze(1).to_broadcast([128, KC, n]), ALU.mult,
                     [rr, stt_r[2]], [rr])
                P.tt("dve", rt3, rt3, nmr.unsqueeze(1).to_broadcast([128, KC, n]), ALU.add,
                     [rr, stt_r[3]], [rr])
            if modj is not None:
                for kc in range(KC):
                    eng = "pool" if kc % 2 == 0 else "dve"
                    P.ts(eng, XT[:, kc, a:b_], rt[:, kc, 0:n],
                         P.Gp[:, seg * KC + kc:seg * KC + kc + 1], P.Bp[:, seg * KC + kc:seg * KC + kc + 1],
                         ALU.mult, ALU.add, [rr, P.gpreg], [P.XTr])
            if which is not None:
                g = P.lng_t[:, (li * 2 + which) * KC:(li * 2 + which + 1) * KC]
                b = P.lnb_t[:, (li * 2 + which) * KC:(li * 2 + which + 1) * KC]
                for kc in range(KC):
                    P.act(rt[:, kc, 0:n], rt[:, kc, 0:n], AF.Identity, [rr, P.parr], [rr],
                          bias=b[:, kc:kc + 1], scale=g[:, kc:kc + 1])
                P.dma("sp", dv[:, :, a - dst_off:b_ - dst_off], rt[:, :, 0:n], [rr], [dstr])

    def ffn_up(self, li, wup, c0):
        P, kb = self, self.kb
        P.carve_reset()
        XT = P.XT.rearrange("p (k t) -> p k t", t=T)
        groups = _split_cols(c0, T)
        segs = ([(0, NCTX)] if c0 == 0 else []) + [(NCTX, T)]
        ug = [P.carve([T], F32) for _ in range(2)]
        uv = [P.carve([T], F32) for _ in range(2)]
        ugr, uvr = kb.regs_n(2), kb.regs_n(2)
        cg, cv = P.carve([T], F32), P.carve([T], F32)
        cgr, cvr = kb.reg(), kb.reg()
        ao = [P.carve([T], BF16) for _ in range(2)]
        aor = kb.regs_n(2)
        loads = {}

        def load(j):
            ws, wr = P.wslot()
            P.dma("pool", ws[:, 0:KC * 256], wup[li, j], [], [wr])
            loads[j] = (ws, wr)

        for j in range(2):
            load(j)
        for j in range(FB):
            if j + 2 < FB:
                load(j + 2)
            ws, wr = loads.pop(j)
            w3 = ws[:, 0:KC * 256].rearrange("p (k c) -> p k c", c=256)
            u_g, u_v, u_gr, u_vr = ug[j % 2], uv[j % 2], ugr[j % 2], uvr[j % 2]
            for (a, b_) in groups:
                n = b_ - a
                for half, (ut, utr) in enumerate(((u_g, u_gr), (u_v, u_vr))):
                    pb, pr = P.bank()
                    for kc in range(KC):
                        P.mm(pb[:, 0:n], w3[:, kc, half * 128:(half + 1) * 128], XT[:, kc, a:b_],
                             kc == 0, kc == KC - 1, [wr, P.XTr], [pr])
                    P.copy("act", ut[:, a:b_], pb[:, 0:n], [pr], [utr])
            for half, (ut, utr, ct, ctr) in enumerate(((u_g, u_gr, cg, cgr), (u_v, u_vr, cv, cvr))):
                blk = half * FB + j
                w0 = P.fcw_t[:, blk * 3 + 0:blk * 3 + 1]
                w1 = P.fcw_t[:, blk * 3 + 1:blk * 3 + 2]
                w2 = P.fcw_t[:, blk * 3 + 2:blk * 3 + 3]
                bb = P.fcb_t[:, blk:blk + 1]
                for (s0, s1) in segs:
                    P.ts("dve", ct[:, s0:s1], ut[:, s0:s1], w1, bb, ALU.mult, ALU.add, [utr, P.parr], [ctr])
                    P.stt(ct[:, s0 + 1:s1], ut[:, s0:s1 - 1], w0, ct[:, s0 + 1:s1], ALU.mult, ALU.add,
                          [utr, ctr, P.parr], [ctr])
                    P.stt(ct[:, s0:s1 - 1], ut[:, s0 + 1:s1], w2, ct[:, s0:s1 - 1], ALU.mult, ALU.add,
                          [utr, ctr, P.parr], [ctr])
            a_o, a_or = ao[j % 2], aor[j % 2]
            P.act(cg[:, c0:T], cg[:, c0:T], AF.Silu, [cgr], [cgr])
            P.tt("dve", a_o[:, c0:T], cg[:, c0:T], cv[:, c0:T], ALU.mult, [cgr, cvr], [a_or])
            P.dma("sp", P.ACTT[j * 128:(j + 1) * 128, c0:T], a_o[:, c0:T], [a_or], [P.ACTTr])

    def gemm_resid(self, li, kind, c0, wdn=None):
        P, kb = self, self.kb
        P.carve_reset()
        if kind == "down":
            kc_n, src, wsrc, gj = FB, P.ACTT, (lambda nb: wdn[li, nb]), 5
            srcr = P.ACTTr
        else:
            kc_n, src, wsrc, gj = KC, P.OT, P.wo_src(li), 2
            srcr = P.OTr
        for seg in range(2):
            P.ts("dve", P.gpr[:, seg * KC:(seg + 1) * KC], P.modv(P.mod, li, gj, seg), 1.0 / DN_ALPHA, None,
                 ALU.mult, None, [P.modr], [P.gpreg])
        tw = min((KC * T) // kc_n, T) // 128 * 128
        tiles = []
        c = c0
        while c < T:
            e = min(c + tw, T)
            tiles.append((c, e))
            c = e
        srcv = src.rearrange("(k p) t -> p k t", p=128)
        xflat = P.XT
        zt = [P.carve([512], F32) for _ in range(3)]
        ztr = kb.regs_n(3)
        zi = 0
        for (t0, t1) in tiles:
            tn = t1 - t0
            X = xflat[:, 0:kc_n * tn].rearrange("p (k t) -> p k t", t=tn)
            P.dma("sp", X, srcv[:, :, t0:t1], [srcr], [P.XTr])
            subs = _split_cols(t0, t1)
            loads = {}

            def load(nb):
                ws, wr = P.wslot()
                P.dma("pool", ws[:, 0:kc_n * 128], wsrc(nb), [], [wr])
                loads[nb] = (ws, wr)

            for nb in range(2):
                load(nb)
            for nb in range(KC):
                if nb + 2 < KC:
                    load(nb + 2)
                ws, wr = loads.pop(nb)
                for (a, b_) in subs:
                    n = b_ - a
                    seg = 1 if b_ <= NCTX else 0
                    z, zr = zt[zi % 3], ztr[zi % 3]
                    zi += 1
                    P.dma("sp", z[:, 0:n], P.Z[nb * 128:(nb + 1) * 128, a:b_], [P.Zr], [zr])
                    pb, pr = P.bank()
                    for kc in range(kc_n):
                        P.mm(pb[:, 0:n], ws[:, kc * 128:(kc + 1) * 128], X[:, kc, a - t0:b_ - t0],
                             kc == 0, kc == kc_n - 1, [wr, P.XTr], [pr])
                    P.stt(z[:, 0:n], pb[:, 0:n], P.gpr[:, seg * KC + nb:seg * KC + nb + 1], z[:, 0:n],
                          ALU.mult, ALU.add, [pr, zr, P.gpreg], [zr])
                    P.dma("sp", P.Z[nb * 128:(nb + 1) * 128, a:b_], z[:, 0:n], [zr], [P.Zr])

    def mix_inputs(self):
        P = self
        L = self.cfg["layers"] if self.cfg.get("mixer", True) else []
        P.wo_in = {}
        if 0 in L:
            P.rw_par = P.din("rw_par", [128, KC * 15])
            P.rw_wrkv = P.din("rw_wrkv", [48, 128, KC * 128])
            P.rw_w1 = P.din("rw_w1", [2, 128, KC * 96])
            P.rw_a1 = P.din("rw_a1", [2, 128, KC * 64])
            P.rw_g1 = P.din("rw_g1", [2, 128, KC * 128])
            P.rw_w2c = P.din("rw_w2c", [KC, 128, 768])
            P.rw_msk = P.din("rw_msk", [128, 2 * 3 * 128])
            P.rw_bd = P.din("rw_bd", [128, 128])
            P.rw_cm = P.din("rw_cm", [128, T + 64])
            if not hasattr(P, "ident_in"):
                P.ident_in = P.din("ident", [128, 128])
            P.wo_in[0] = P.din("rw_wo", [KC, 128, KC * 128])
            P.XS = P.dscratch("XS", [6, D, T], BF16)
            P.RKV = [P.dscratch("RKV%d" % i, [D, T], F32) for i in range(3)]
            P.LWD = [P.dscratch("LWD%d" % i, [D, T], F32) for i in range(2)]
            P.ICD = [P.dscratch("ICD%d" % i, [D, T], F32) for i in range(2)]
            P.G32 = P.dscratch("G32", [D, T], F32)
        if 1 in L:
            P.da_wqk = P.din("da_wqk", [32, 128, KC * 256])
            P.da_wv = P.din("da_wv", [8, 128, KC * 256])
            P.da_cs = P.din("da_cs", [128, 2 * T])
            P.da_lam = P.din("da_lam", [1, 512])
            P.da_subg = P.din("da_subg", [128, 2])
            if not hasattr(P, "ident_in"):
                P.ident_in = P.din("ident", [128, 128])
            P.wo_in[1] = P.din("da_wo", [KC, 128, KC * 128])
            P.QT = P.dscratch("QT", [D, T], BF16)
            P.KT = P.dscratch("KT", [D, T], BF16)
        if 2 in L:
            P.hg_win = P.din("hg_win", [80, 128, KC * 128])
            P.hg_low = P.din("hg_low", [128, 2 * 4 * KC])
            P.hg_ng = P.din("hg_ng", [128, 1])
            P.hg_cm = P.din("hg_cm", [128, T + 64])
            P.hg_msk = P.din("hg_msk", [128, 256])
            if not hasattr(P, "ident_in"):
                P.ident_in = P.din("ident", [128, 128])
            P.wo_in[2] = P.din("hg_wo", [KC, 128, KC * 128])
        if 3 in L:
            P.lr_win = P.din("lr_win", [32, 128, KC * 128])
            P.lr_par = P.din("lr_par", [128, KC * 11])
            P.lr_wg = P.din("lr_wg", [8, 128, 2048])
            P.wo_in[3] = P.din("lr_wo", [KC, 128, KC * 128])

    def host_mix(self, inp):
        L = self.cfg["layers"] if self.cfg.get("mixer", True) else []
        m = {}

        def blocks(w, nb):
            w = np.asarray(w, np.float32)
            k = w.shape[0] // 128
            return np.ascontiguousarray(w.reshape(k, 128, nb, 128).transpose(2, 1, 0, 3)).reshape(nb, 128, k * 128)

        if 0 in L:
            f32 = np.float32
            mu = _fm(inp["rw_mu"][0]).transpose(0, 2, 1)
            w0 = _fm(inp["rw_w0"][0]).transpose(0, 2, 1)
            a0 = _fm(inp["rw_a0"][0]).transpose(0, 2, 1)
            one = lambda k: _fm(np.asarray(inp[k][0], f32).reshape(-1))[:, :, None]
            m["rw_par"] = np.ascontiguousarray(np.concatenate(
                [mu, w0, a0, one("rw_k_k"), one("rw_k_a"), one("rw_r_k"), one("rw_gn_g"), one("rw_gn_b")],
                axis=2)).reshape(128, KC * 15)
            m["rw_wrkv"] = np.concatenate([blocks(inp["rw_w_rkv"][0][n], KC) for n in range(3)], axis=0)

            def kmaj(w):
                w = np.asarray(w, f32)
                k = w.shape[0] // 128
                return np.ascontiguousarray(w.reshape(k, 128, w.shape[1]).transpose(1, 0, 2)).reshape(128, -1)

            m["rw_w1"] = np.stack([kmaj(inp["rw_w1"][0][d]) for d in range(2)])
            m["rw_a1"] = np.stack([kmaj(inp["rw_a1"][0][d]) for d in range(2)])
            g1 = np.asarray(inp["rw_g1"][0], f32)
            m["rw_g1"] = np.stack([kmaj(g1[:, j * 128:(j + 1) * 128]) for j in range(2)])
            w2c = np.zeros((KC, 128, 768), f32)
            w2 = np.asarray(inp["rw_w2"][0], f32)
            a2 = np.asarray(inp["rw_a2"][0], f32)
            g2 = np.asarray(inp["rw_g2"][0], f32)
            for nb in range(KC):
                cs_ = slice(nb * 128, (nb + 1) * 128)
                for d in range(2):
                    w2c[nb, 0:96, d * 128:(d + 1) * 128] = w2[d][:, cs_]
                    w2c[nb, 0:64, 256 + d * 128:256 + (d + 1) * 128] = a2[d][:, cs_]
                for k2 in range(2):
                    w2c[nb, :, 512 + k2 * 128:512 + (k2 + 1) * 128] = g2[k2 * 128:(k2 + 1) * 128, cs_]
            m["rw_w2c"] = w2c
            i_ = np.arange(128)
            same = (i_[:, None] // 64) == (i_[None, :] // 64)
            row, col = i_[:, None] % 64, i_[None, :] % 64
            mk = []
            for d in range(2):
                lt = (row < col) if d == 0 else (row > col)
                le = (row <= col) if d == 0 else (row >= col)
                gt = (row > col) if d == 0 else (row < col)
                mk += [same & lt, same & le, same & gt]
            m["rw_msk"] = np.concatenate(mk, axis=1).astype(f32)
            m["rw_bd"] = same.astype(f32)
            cm = np.ones((128, T + 64), f32)
            cm[:, 0::64] = 0.0
            m["rw_cm"] = cm
            m["ident"] = np.eye(128, dtype=f32)
            m["rw_wo"] = blocks(inp["rw_w_o"][0], KC)
        if 1 in L:
            w = np.asarray(inp["da_w_qkv"][0], np.float32)
            perm = np.concatenate([np.arange(32, 64), np.arange(0, 32), np.arange(96, 128), np.arange(64, 96)])
            wqk = w[:, 0:2 * D].reshape(KC, 128, 32, 128)
            wqk2 = np.stack([wqk, wqk[:, :, :, perm]], axis=3)
            m["da_wqk"] = np.ascontiguousarray(wqk2.transpose(2, 1, 0, 3, 4)).reshape(32, 128, KC * 256)
            wv = w[:, 2 * D:3 * D].reshape(KC, 128, 8, 256)
            m["da_wv"] = np.ascontiguousarray(wv.transpose(2, 1, 0, 3)).reshape(8, 128, KC * 256)
            f32 = np.float32
            row = np.repeat(np.arange(SEQ // 64, dtype=f32), 64)
            col = np.tile(np.arange(64, dtype=f32), SEQ // 64)
            inv = (f32(10000.0) ** (-np.arange(32, dtype=f32) / f32(32))).astype(f32)
            ar, ac = (row[:, None] * inv).astype(f32), (col[:, None] * inv).astype(f32)
            ang = np.concatenate([ar, ar, ac, ac], axis=-1)
            ang = np.concatenate([np.zeros((NCTX, 128), f32), ang], axis=0)
            sign = np.concatenate([-np.ones(32, f32), np.ones(32, f32), -np.ones(32, f32), np.ones(32, f32)])
            m["da_cs"] = np.ascontiguousarray(np.concatenate([np.cos(ang).T, (np.sin(ang) * sign).T], axis=1)).astype(f32)
            m["da_lam"] = np.asarray(inp["da_lambda"][0], f32).reshape(1, 512)
            m["da_subg"] = _fm(inp["da_sub_g"][0])
            m["ident"] = np.eye(128, dtype=f32)
            m["da_wo"] = blocks(inp["da_w_o"][0], KC)
        if 2 in L:
            m["hg_win"] = blocks(inp["hg_w_in"][0], 80)
            m["hg_low"] = _fm(inp["hg_lower"]).reshape(128, 2 * 4 * KC)
            m["hg_ng"] = np.asarray(inp["hg_norm_g"][0], np.float32).reshape(128, 1)
            cm = np.ones((128, T + 64), np.float32)
            cm[:, 0::64] = 0.0
            m["hg_cm"] = cm
            s_ = np.arange(128)[:, None]
            t_ = np.arange(128)[None, :]
            same = (s_ // 64) == (t_ // 64)
            m["hg_msk"] = np.concatenate([(same & (s_ <= t_)), (same & (s_ >= t_))], axis=1).astype(np.float32)
            m["ident"] = np.eye(128, dtype=np.float32)
            m["hg_wo"] = blocks(inp["hg_w_o"][0], KC)
        if 3 in L:
            m["lr_win"] = blocks(inp["lr_w_in"][0], 32)
            cw = _fm(inp["lr_conv_w"][0]).transpose(0, 2, 1)
            cb = _fm(inp["lr_conv_b"][0])[:, :, None]
            bg = _fm(inp["lr_b_gate"][0]).reshape(128, 4, KC).transpose(0, 2, 1)
            lam = _fm(inp["lr_lambda"][0]).transpose(0, 2, 1)
            m["lr_par"] = np.ascontiguousarray(np.concatenate([cw, cb, bg, lam], axis=2)).reshape(128, KC * 11)
            wg = np.asarray(inp["lr_w_gate"][0], np.float32).reshape(2, 2, 8, 2, 128, 2, 128)
            m["lr_wg"] = np.ascontiguousarray(wg.transpose(2, 4, 0, 1, 5, 3, 6)).reshape(8, 128, 2048)
            m["lr_wo"] = blocks(inp["lr_w_o"][0], KC)
        return m

    def wo_src(self, li):
        w = self.wo_in[li]
        return lambda nb: w[nb]

    def mixer(self, li):
        return [self.mixer_rw, self.mixer_da, self.mixer_hg, self.mixer_lr][li % 4](li)

    class _Pf:
        def __init__(self, P, seq, depth=2):
            self.P, self.seq, self.depth, self.nxt, self.got = P, seq, depth, 0, {}

        def get(self, i):
            while self.nxt < len(self.seq) and self.nxt <= i + self.depth:
                src, nel = self.seq[self.nxt]
                ws, wr = self.P.wslot()
                self.P.dma("pool", ws[:, 0:nel], src, [], [wr])
                self.got[self.nxt] = (ws, wr)
                self.nxt += 1
            return self.got.pop(i)

    def proj(self, lhs_fn, kc_n, mcols, rhs_fn, rreads, groups, evac):
        for (a, b_) in groups:
            pb, pr = self.bank()
            for kc in range(kc_n):
                self.mm(pb[0:mcols, 0:b_ - a], lhs_fn(kc), rhs_fn(kc, a, b_), kc == 0, kc == kc_n - 1, rreads, [pr])
            evac(pb, pr, a, b_)

    def mixer_lr(self, li):
        P, kb = self, self.kb
        P.carve_reset()
        XT = P.XT.rearrange("p (k t) -> p k t", t=T)
        groups = _split_cols(0, T)
        segs = [(0, NCTX), (NCTX, T)]
        par = P.carve([KC * 11], F32)
        par3 = par.rearrange("p (k j) -> p k j", j=11)
        negc = P.carve([KC * 2], F32)
        nc3 = negc.rearrange("p (k d) -> p k d", d=2)
        c1 = P.carve([1], F32)
        pr_ = kb.reg()
        P.dma("sp", par, P.lr_par, [], [pr_])
        kb.emit("pool", lambda e: e.memset(c1, 1.0), [], [pr_])
        P.act(nc3, par3[:, :, 9:11], AF.Exp, [pr_], [pr_], scale=-1.0)
        P.act(negc, negc, AF.Ln, [pr_], [pr_], bias=c1)
        P.ts("dve", negc, negc, -8.0, None, ALU.mult, None, [pr_], [pr_])
        names = ["R0", "R1", "C0", "C1", "A", "U", "Y0", "Y1"]
        tl = {n: P.carve([T], F32) for n in names}
        rg = {n: kb.reg() for n in names}
        CB = P.carve([2, T], BF16)
        CBr = kb.regs_n(2)
        seq = []
        for n in range(8):
            seq += [(P.lr_win[16 + 2 * n], KC * 128), (P.lr_win[17 + 2 * n], KC * 128), (P.lr_wg[n], 2048),
                    (P.lr_win[2 * n], KC * 128), (P.lr_win[2 * n + 1], KC * 128)]
        pf = P._Pf(P, seq)
        for n in range(8):
            for jb in range(2):
                ws, wr = pf.get(5 * n + jb)
                Rt, Rr = tl["R%d" % jb], rg["R%d" % jb]
                P.proj(lambda kc: ws[:, kc * 128:(kc + 1) * 128], KC, 128, lambda kc, a, b_: XT[:, kc, a:b_],
                       [wr, P.XTr], groups,
                       lambda pb, pr, a, b_: P.copy("act", Rt[:, a:b_], pb[:, 0:b_ - a], [pr], [Rr]))
            for jb in range(2):
                kcj = 2 * n + jb
                Rt, Rr, Ct, Cr = tl["R%d" % jb], rg["R%d" % jb], tl["C%d" % jb], rg["C%d" % jb]
                w = [par3[:, kcj, j:j + 1] for j in range(5)]
                for (s0, s1) in segs:
                    P.ts("dve", Ct[:, s0:s1], Rt[:, s0:s1], w[1], w[4], ALU.mult, ALU.add, [Rr, pr_], [Cr])
                    P.stt(Ct[:, s0 + 1:s1], Rt[:, s0:s1 - 1], w[0], Ct[:, s0 + 1:s1], ALU.mult, ALU.add, [Rr, Cr, pr_], [Cr])
                    P.stt(Ct[:, s0:s1 - 1], Rt[:, s0 + 1:s1], w[2], Ct[:, s0:s1 - 1], ALU.mult, ALU.add, [Rr, Cr, pr_], [Cr])
                    P.stt(Ct[:, s0:s1 - 2], Rt[:, s0 + 2:s1], w[3], Ct[:, s0:s1 - 2], ALU.mult, ALU.add, [Rr, Cr, pr_], [Cr])
                P.copy("pool", CB[:, jb, :], Ct, [Cr], [CBr[jb]])
            wg, wgr = pf.get(5 * n + 2)
            wg6 = wg[:, 0:2048].rearrange("p (d g j i c) -> p d g j i c", d=2, g=2, j=2, i=2)
            A, U, G, H = tl["A"], tl["U"], tl["R0"], tl["R1"]
            Ar, Ur, Gr, Hr = rg["A"], rg["U"], rg["R0"], rg["R1"]
            for d in range(2):
                for jb in range(2):
                    kcj = 2 * n + jb
                    Ct, Cr, Yt, Yr = tl["C%d" % jb], rg["C%d" % jb], tl["Y%d" % jb], rg["Y%d" % jb]
                    for gt, (Xt, Xr) in enumerate(((A, Ar), (U, Ur))):
                        bia = par3[:, kcj, 5 + d * 2 + gt:6 + d * 2 + gt]
                        P.proj(lambda ic: wg6[:, d, gt, jb, ic, :], 2, 128, lambda ic, a, b_: CB[:, ic, a:b_],
                               [wgr, CBr[0], CBr[1]], groups,
                               lambda pb, pr, a, b_: P.act(Xt[:, a:b_], pb[:, 0:b_ - a], AF.Sigmoid, [pr, pr_], [Xr], bias=bia))
                    P.act(A, A, AF.Exp, [Ar, pr_], [Ar], scale=nc3[:, kcj, d:d + 1])
                    P.tt("pool", G, A, A, ALU.mult, [Ar], [Gr])
                    P.act(G, G, AF.Sqrt, [Gr, pr_], [Gr], bias=c1, scale=-1.0)
                    P.tt("dve", U, U, Ct, ALU.mult, [Ur, Cr], [Ur])
                    P.tt("dve", U, U, G, ALU.mult, [Ur, Gr], [Ur])
                    if d == 0:
                        kb.emit("dve", lambda e: e.tensor_tensor_scan(out=H, data0=A, data1=U, initial=0.0,
                                                                      op0=ALU.mult, op1=ALU.add), [Ar, Ur], [Hr])
                        P.copy("pool", Yt, H, [Hr], [Yr])
                    else:
                        rv = lambda t, s0, s1: t[:, s0:s1][:, ::-1]
                        kb.emit("dve", lambda e: e.tensor_tensor_scan(out=rv(H, 0, NCTX), data0=rv(A, 0, NCTX),
                                                                      data1=rv(U, 0, NCTX), initial=0.0,
                                                                      op0=ALU.mult, op1=ALU.add), [Ar, Ur], [Hr])
                        kb.emit("dve", lambda e: e.tensor_tensor_scan(out=rv(H, NCTX, T), data0=rv(A, NCTX, T),
                                                                      data1=rv(U, NCTX, T), initial=H[:, 0:1],
                                                                      op0=ALU.mult, op1=ALU.add), [Ar, Ur, Hr], [Hr])
                        P.tt("pool", Yt, Yt, H, ALU.add, [Yr, Hr], [Yr])
            for jb in range(2):
                kcj = 2 * n + jb
                ws, wr = pf.get(5 * n + 3 + jb)
                Yt, Yr = tl["Y%d" % jb], rg["Y%d" % jb]

                def ev(pb, pr, a, b_):
                    P.copy("act", A[:, a:b_], pb[:, 0:b_ - a], [pr], [Ar])
                    P.act(U[:, a:b_], pb[:, 0:b_ - a], AF.Square, [pr], [Ur])

                P.proj(lambda kc: ws[:, kc * 128:(kc + 1) * 128], KC, 128, lambda kc, a, b_: XT[:, kc, a:b_],
                       [wr, P.XTr], groups, ev)
                P.ts("dve", U, U, 0.044715, 1.0, ALU.mult, ALU.add, [Ur], [Ur])
                P.tt("dve", U, U, A, ALU.mult, [Ur, Ar], [Ur])
                P.act(U, U, AF.Sigmoid, [Ur], [Ur], scale=1.5957691216057308)
                P.tt("pool", A, A, U, ALU.mult, [Ar, Ur], [Ar])
                P.tt("dve", CB[:, jb, :], Yt, A, ALU.mult, [Yr, Ar], [CBr[jb]])
                P.dma("sp", P.OT[kcj * 128:(kcj + 1) * 128, :], CB[:, jb, :], [CBr[jb]], [P.OTr])

    def mixer_rw(self, li):
        P, kb = self, self.kb
        XT3 = P.XT.rearrange("p (k t) -> p k t", t=T)
        groups = _split_cols(0, T)
        segs = [(0, NCTX, 1), (NCTX, T, 0)]
        rv = lambda t: t[:, ::-1]
        P.carve_reset()
        par = P.carve([KC * 15], F32)
        par3 = par.rearrange("p (k j) -> p k j", j=15)
        parr = kb.reg()
        P.dma("sp", par, P.rw_par, [], [parr])
        Hs = [P.carve([T], F32) for _ in range(2)]
        Ts = [P.carve([T], F32) for _ in range(2)]
        Ds = [P.carve([T], F32) for _ in range(2)]
        Hr, Tr_, Dr = kb.regs_n(2), kb.regs_n(2), kb.regs_n(2)
        xo = [P.carve([T], BF16) for _ in range(3)]
        xor_ = kb.regs_n(3)
        XSr = kb.reg()
        xi = 0
        for kc in range(KC):
            H, TM, DX = Hs[kc % 2], Ts[kc % 2], Ds[kc % 2]
            hr, tr, dr = Hr[kc % 2], Tr_[kc % 2], Dr[kc % 2]
            P.dma("sp", H, P.Z[kc * 128:(kc + 1) * 128, :], [P.Zr], [hr])
            for (s0, s1, sg_) in segs:
                P.act(H[:, s0:s1], H[:, s0:s1], AF.Identity, [hr, P.modr], [hr],
                      bias=P.modv(P.mod, li, 0, sg_)[:, kc:kc + 1], scale=P.modv(P.mod1, li, 1, sg_)[:, kc:kc + 1])
            for (s0, s1, sg_) in segs:
                P.tt("pool", TM[:, s0 + 1:s1 - 1], H[:, s0:s1 - 2], H[:, s0 + 2:s1], ALU.add, [hr], [tr])
                P.copy("pool", TM[:, s0:s0 + 1], H[:, s0 + 1:s0 + 2], [hr], [tr])
                P.copy("pool", TM[:, s1 - 1:s1], H[:, s1 - 2:s1 - 1], [hr], [tr])
            P.stt(DX, TM, 0.5, H, ALU.mult, ALU.subtract, [tr, hr], [dr])
            for n in range(6):
                o, orr = xo[xi % 3], xor_[xi % 3]
                xi += 1
                P.stt(o, DX, par3[:, kc, n:n + 1], H, ALU.mult, ALU.add, [dr, hr, parr], [orr])
                P.dma("sp", P.XS[n, kc * 128:(kc + 1) * 128, :], o, [orr], [XSr])
        kb.barrier()
        P.carve_reset()
        par = P.carve([KC * 15], F32)
        par3 = par.rearrange("p (k j) -> p k j", j=15)
        P.dma("sp", par, P.rw_par, [], [parr])
        stg = [P.carve([T], F32) for _ in range(3)]
        stgr = kb.regs_n(3)
        si = [0]

        def stage():
            i = si[0] % 3
            si[0] += 1
            return stg[i], stgr[i]

        MW = P.carve([2, T], BF16)
        MA = P.carve([2, T], BF16)
        MG = P.carve([2, T], BF16)
        MWr, MAr, MGr = kb.reg(), kb.reg(), kb.reg()
        XSv = P.XS.rearrange("n (k p) t -> n p k t", p=128)
        RKVr = kb.reg()
        seq = [(P.rw_wrkv[i], KC * 128) for i in range(48)]
        seq += [(P.rw_w1[d], KC * 96) for d in range(2)] + [(P.rw_a1[d], KC * 64) for d in range(2)]
        seq += [(P.rw_g1[j], KC * 128) for j in range(2)]
        pf = P._Pf(P, seq)
        xfn = lambda kc, a, b_: XT3[:, kc, a:b_]
        for n in range(3):
            P.dma("sp", XT3, XSv[n], [XSr], [P.XTr])
            for blk in range(KC):
                ws, wr = pf.get(n * KC + blk)
                st_, sr_ = stage()
                P.proj(lambda kc: ws[:, kc * 128:(kc + 1) * 128], KC, 128, xfn, [wr, P.XTr], groups,
                       lambda pb, pr, a, b_: P.copy("act", st_[:, a:b_], pb[:, 0:b_ - a], [pr], [sr_]))
                P.dma("sp", P.RKV[n][blk * 128:(blk + 1) * 128, :], st_, [sr_], [RKVr])
        P.dma("sp", XT3, XSv[3], [XSr], [P.XTr])
        for d in range(2):
            ws, wr = pf.get(48 + d)
            w3 = ws[:, 0:KC * 96].rearrange("p (k c) -> p k c", c=96)
            P.proj(lambda kc: w3[:, kc, :], KC, 96, xfn, [wr, P.XTr], groups,
                   lambda pb, pr, a, b_: P.act(MW[0:96, d, a:b_], pb[0:96, 0:b_ - a], AF.Tanh, [pr], [MWr]))
        P.dma("sp", XT3, XSv[4], [XSr], [P.XTr])
        for d in range(2):
            ws, wr = pf.get(50 + d)
            w3 = ws[:, 0:KC * 64].rearrange("p (k c) -> p k c", c=64)
            P.proj(lambda kc: w3[:, kc, :], KC, 64, xfn, [wr, P.XTr], groups,
                   lambda pb, pr, a, b_: P.copy("act", MA[0:64, d, a:b_], pb[0:64, 0:b_ - a], [pr], [MAr]))
        P.dma("sp", XT3, XSv[5], [XSr], [P.XTr])
        for j in range(2):
            ws, wr = pf.get(52 + j)
            P.proj(lambda kc: ws[:, kc * 128:(kc + 1) * 128], KC, 128, xfn, [wr, P.XTr], groups,
                   lambda pb, pr, a, b_: P.act(MG[:, j, a:b_], pb[:, 0:b_ - a], AF.Sigmoid, [pr], [MGr]))
        pf2 = P._Pf(P, [(P.rw_w2c[nb], 768) for nb in range(KC)])
        for nb in range(KC):
            ws, wr = pf2.get(nb)
            for d in range(2):
                st_, sr_ = stage()
                P.proj(lambda kc: ws[0:96, d * 128:(d + 1) * 128], 1, 128, lambda kc, a, b_: MW[0:96, d, a:b_],
                       [wr, MWr], groups,
                       lambda pb, pr, a, b_: P.act(st_[:, a:b_], pb[:, 0:b_ - a], AF.Sigmoid, [pr, parr], [sr_],
                                                   bias=par3[:, nb, 6 + d:7 + d]))
                P.ts("dve", st_, st_, -0.606531, None, ALU.mult, None, [sr_], [sr_])
                P.dma("sp", P.LWD[d][nb * 128:(nb + 1) * 128, :], st_, [sr_], [RKVr])
                st_, sr_ = stage()
                P.proj(lambda kc: ws[0:64, 256 + d * 128:256 + (d + 1) * 128], 1, 128,
                       lambda kc, a, b_: MA[0:64, d, a:b_], [wr, MAr], groups,
                       lambda pb, pr, a, b_: P.act(st_[:, a:b_], pb[:, 0:b_ - a], AF.Sigmoid, [pr, parr], [sr_],
                                                   bias=par3[:, nb, 8 + d:9 + d]))
                P.dma("sp", P.ICD[d][nb * 128:(nb + 1) * 128, :], st_, [sr_], [RKVr])
            st_, sr_ = stage()
            P.proj(lambda kc: ws[:, 512 + kc * 128:512 + (kc + 1) * 128], 2, 128, lambda kc, a, b_: MG[:, kc, a:b_],
                   [wr, MGr], groups,
                   lambda pb, pr, a, b_: P.copy("act", st_[:, a:b_], pb[:, 0:b_ - a], [pr], [sr_]))
            P.dma("sp", P.G32[nb * 128:(nb + 1) * 128, :], st_, [sr_], [RKVr])
        kb.barrier()
        P.carve_reset()
        P.lopos = 0
        lo_names = ["R", "K", "KK", "IC", "B", "LW", "TMP", "KS", "Y"]
        tl = {n: P.carve_lo([T], F32) for n in lo_names}
        rg = {n: kb.reg() for n in lo_names}
        SQb = P.carve_lo([T], BF16)
        ZV = P.carve_lo([36, 128], BF16)
        SQr, ZVr = kb.reg(), kb.reg()
        par = P.carve([KC * 15], F32)
        par3 = par.rearrange("p (k j) -> p k j", j=15)
        ZAR = P.carve([36, 2, 128], BF16)
        ZB = P.carve([36, 128], BF16)
        ZK = P.carve([36, 128], BF16)
        ZARr, ZBr, ZKr = kb.reg(), kb.reg(), kb.reg()
        cm = P.carve([T + 64], BF16)
        BD = P.carve([128], BF16)
        msk = P.carve([2 * 3 * 128], BF16)
        I32 = P.carve([128], F32)
        cst = P.carve([4], F32)
        omka = P.carve([KC], F32)
        DEC = P.carve([36], F32)
        Hp = P.carve([128], F32)
        cr, DECr, Hpr = kb.reg(), kb.reg(), kb.reg()
        P.dma("sp", par, P.rw_par, [], [cr])
        P.dma("pool", cm, P.rw_cm, [], [cr])
        P.dma("pool", BD, P.rw_bd, [], [cr])
        P.dma("pool", msk, P.rw_msk, [], [cr])
        P.dma("sp", I32, P.ident_in, [], [cr])
        kb.emit("pool", lambda e: e.memset(cst[:, 0:1], 1e-12), [], [cr])
        kb.emit("pool", lambda e: e.memset(cst[:, 1:2], 64e-5), [], [cr])
        P.ts("dve", omka, par3[:, :, 11], -1.0, 1.0, ALU.mult, ALU.add, [cr], [cr])
        kb.emit("pool", lambda e: e.memset(ZV, 0.0), [], [ZVr])
        kb.emit("pool", lambda e: e.memset(ZAR, 0.0), [], [ZARr])
        kb.emit("pool", lambda e: e.memset(ZB, 0.0), [], [ZBr])
        kb.emit("pool", lambda e: e.memset(ZK, 0.0), [], [ZKr])
        G_ = 6
        slot = []
        for i in range(G_):
            sl = {"UL": [P.carve([2, 128], BF16) for _ in range(2)],
                  "MK": P.carve([3, 128], BF16), "Pb": P.carve([128], BF16), "TR": P.carve([4, 128], BF16),
                  "ATX": P.carve([2, 128], BF16), "CV": P.carve([128], BF16), "MT": P.carve([128], F32),
                  "RT": P.carve([128], F32), "NY": P.carve([2, 128], F32)}
            sl["r"] = {k: kb.reg() for k in ["UL0", "UL1", "PQ", "MK", "Pb", "TR", "ATX", "CV", "MT", "RT", "NY"]}
            slot.append(sl)
        R, K_, KK, IC, Bt, LW, TMP, KS, Y = [tl[n] for n in lo_names]
        Rr, Kr, KKr, ICr, Br, LWr, TMPr, KSr, Yr = [rg[n] for n in lo_names]
        c3 = lambda t: t.rearrange("p (c i) -> p c i", i=64)

        def pad_write(eng_a, dstZ, dreg, fn):
            for hp in range(2):
                rows = slice(hp * 64, (hp + 1) * 64)
                fn(dstZ[rows, :, hp * 64:(hp + 1) * 64], rows)

        def bdsum(src_bf, sreg, dst, dreg, func=AF.Copy, **kw):
            for (a, b_) in groups:
                pb, pr = P.bank()
                P.mm(pb[:, 0:b_ - a], BD, src_bf[:, a:b_], True, True, [sreg, cr], [pr])
                P.act(dst[:, a:b_], pb[:, 0:b_ - a], func, [pr, cr], [dreg], **kw)

        for kc in range(KC):
            rows128 = slice(kc * 128, (kc + 1) * 128)
            pk = lambda j: par3[:, kc, j:j + 1]
            P.dma("sp", R, P.RKV[0][rows128, :], [RKVr], [Rr])
            P.dma("sp", K_, P.RKV[1][rows128, :], [RKVr], [Kr])
            P.dma("sp", TMP, P.RKV[2][rows128, :], [RKVr], [TMPr])
            pad_write("act", ZV, ZVr, lambda o, rows: P.copy("act", o, c3(TMP)[rows], [TMPr], [ZVr]))
            P.ts("dve", KK, K_, pk(10), None, ALU.mult, None, [Kr, cr], [KKr])
            P.act(SQb, KK, AF.Square, [KKr], [SQr])
            bdsum(SQb, SQr, TMP, TMPr, AF.Ln, bias=cst[:, 0:1])
            P.act(TMP, TMP, AF.Exp, [TMPr], [TMPr], scale=-0.5)
            P.tt("dve", KK, KK, TMP, ALU.mult, [KKr, TMPr], [KKr])
            kb.emit("pool", lambda e: e.memset(KS, 0.0), [KSr], [KSr])
            kb.emit("pool", lambda e: e.memset(Y, 0.0), [Yr], [Yr])
            for d in range(2):
                mA = msk[:, (d * 3 + 0) * 128:(d * 3 + 1) * 128]
                mR = msk[:, (d * 3 + 1) * 128:(d * 3 + 2) * 128]
                mL = msk[:, (d * 3 + 2) * 128:(d * 3 + 3) * 128]
                P.dma("sp", LW, P.LWD[d][rows128, :], [RKVr], [LWr])
                P.dma("sp", IC, P.ICD[d][rows128, :], [RKVr], [ICr])
                if d == 0:
                    kb.emit("dve", lambda e: e.tensor_tensor_scan(out=Bt, data0=cm[:, 0:T], data1=LW, initial=0.0,
                                                                  op0=ALU.mult, op1=ALU.add), [LWr, cr], [Br])
                else:
                    kb.emit("dve", lambda e: e.tensor_tensor_scan(out=rv(Bt), data0=rv(cm[:, 1:T + 1]), data1=rv(LW),
                                                                  initial=0.0, op0=ALU.mult, op1=ALU.add), [LWr, cr], [Br])
                btot = c3(Bt)[:, :, 63] if d == 0 else c3(Bt)[:, :, 0]
                P.act(DEC, btot, AF.Exp, [Br], [DECr])
                P.tt("dve", TMP, Bt, LW, ALU.subtract, [Br, LWr], [TMPr])
                P.act(TMP, TMP, AF.Exp, [TMPr], [TMPr])
                pad_write("dve", ZAR[:, :, 0, :], ZARr,
                          lambda o, rows: P.tt("dve", o, c3(KK)[rows], c3(TMP)[rows], ALU.mult, [KKr, TMPr], [ZARr]))
                P.act(TMP, Bt, AF.Exp, [Br, ZARr], [TMPr])
                pad_write("dve", ZAR[:, :, 1, :], ZARr,
                          lambda o, rows: P.tt("dve", o, c3(R)[rows], c3(TMP)[rows], ALU.mult, [Rr, TMPr], [ZARr]))
                P.act(LW, Bt, AF.Exp, [Br, LWr, TMPr], [LWr], scale=-1.0)
                P.tt("pool", TMP, KK, IC, ALU.mult, [KKr, ICr, ZARr], [TMPr])
                pad_write("dve", ZB, ZBr,
                          lambda o, rows: P.stt(o, c3(TMP)[rows], -1.0, c3(LW)[rows], ALU.mult, ALU.mult,
                                                [TMPr, LWr], [ZBr]))
                P.ts("dve", TMP, IC, pk(11), omka[:, kc:kc + 1], ALU.mult, ALU.add, [ICr, cr, ZBr], [TMPr])
                P.tt("dve", TMP, TMP, K_, ALU.mult, [TMPr, Kr], [TMPr])
                P.tt("pool", KS, KS, TMP, ALU.add, [KSr, TMPr], [KSr])
                pad_write("dve", ZK, ZKr,
                          lambda o, rows: P.tt("dve", o, c3(TMP)[rows], c3(LW)[rows], ALU.mult, [TMPr, LWr], [ZKr]))
                kb.emit("pool", lambda e: e.memset(Hp, 0.0), [Hpr], [Hpr])
                order = list(range(36)) if d == 0 else [3, 2, 1, 0] + list(range(35, 3, -1))
                for g0 in range(0, 36, G_):
                    cs_ = order[g0:g0 + G_]
                    for i, c in enumerate(cs_):
                        sl = slot[i]
                        r_ = sl["r"]
                        zar2 = ZAR[:, c, :, :].rearrange("p a b -> p (a b)")
                        n1, n1r = P.bank()
                        P.mm(n1[:, 0:256], ZB[:, c, :], zar2, True, True, [ZBr, ZARr], [n1r])
                        n2, n2r = P.bank()
                        P.mm(n2[:, 0:256], ZK[:, c, :], zar2, True, True, [ZKr, ZARr], [n2r])
                        n3, n3r = P.bank()
                        P.mm(n3[:, 0:128], ZAR[:, c, 0, :], ZB[:, c, :], True, True, [ZBr, ZARr], [n3r])
                        tb_, tbr = P.bank()
                        tpv = tb_.bitcast(BF16)
                        for j, src in enumerate((ZV[:, c, :], ZB[:, c, :], ZK[:, c, :], ZAR[:, c, 0, :])):
                            kb.emit("pe", lambda e, o=tpv[:, j * 128:(j + 1) * 128], src=src:
                                    e.transpose(out=o, in_=src, identity=P.ident),
                                    [ZVr, ZBr, ZKr, ZARr, P.identr], [tbr], inc=(j == 3))
                        UL0 = sl["UL"][0]
                        P.tt("dve", UL0[:, 0, :], n1[:, 0:128], mA, ALU.mult, [n1r, cr], [r_["UL0"]])
                        P.tt("dve", sl["MK"][:, 1, :], n1[:, 128:256], mR, ALU.mult, [n1r, cr], [r_["MK"]])
                        P.tt("dve", sl["MK"][:, 0, :], n2[:, 0:128], mA, ALU.mult, [n2r, cr], [r_["MK"]])
                        P.tt("dve", sl["MK"][:, 2, :], n2[:, 128:256], mR, ALU.mult, [n2r, cr], [r_["MK"]])
                        P.tt("dve", UL0[:, 1, :], n3[:, 0:128], mL, ALU.mult, [n3r, cr], [r_["UL0"]])
                        P.copy("act", sl["TR"].rearrange("p a b -> p (a b)"), tpv[:, 0:512], [tbr], [r_["TR"]])
                        P.tt("dve", sl["Pb"], UL0[:, 0, :], P.ident, ALU.add, [r_["UL0"], P.identr], [r_["Pb"]])
                    for j in range(1, 7):
                        for i, c in enumerate(cs_):
                            sl = slot[i]
                            r_ = sl["r"]
                            X, Xr = sl["UL"][(j - 1) % 2], r_["UL%d" % ((j - 1) % 2)]
                            Xn, Xnr = sl["UL"][j % 2], r_["UL%d" % (j % 2)]
                            UU, LL = X[:, 0, :], X[:, 1, :]
                            if j <= 5:
                                pa, par_ = P.bank()
                                P.mm(pa[:, 0:128], LL, UU, True, True, [Xr], [par_], inc=False)
                                P.mm(pa[:, 128:256], UU, LL, True, True, [Xr], [par_], inc=True)
                            if j >= 2:
                                pb, pbr = P.bank()
                                P.mm(pb[:, 0:128], LL, sl["Pb"], True, True, [Xr, r_["Pb"]], [pbr])
                            if j <= 5:
                                P.copy("act", Xn.rearrange("p a b -> p (a b)"), pa[:, 0:256], [par_], [Xnr])
                            if j >= 2:
                                P.tt("dve", sl["Pb"], pb[:, 0:128], sl["Pb"], ALU.add, [pbr, r_["Pb"]], [r_["Pb"]])
                    for i, c in enumerate(cs_):
                        sl = slot[i]
                        r_ = sl["r"]
                        TR, MK = sl["TR"], sl["MK"]
                        m1, m1r = P.bank()
                        P.mm(m1[:, 0:128], sl["Pb"], TR[:, 3, :], True, True, [r_["Pb"], r_["TR"]], [m1r], inc=False)
                        P.mm(m1[:, 128:256], MK[:, 0, :], TR[:, 0, :], True, True, [r_["MK"], r_["TR"]], [m1r], inc=True)
                        P.copy("act", sl["ATX"].rearrange("p a b -> p (a b)"), m1[:, 0:256], [m1r], [r_["ATX"]])
                    for i, c in enumerate(cs_):
                        sl = slot[i]
                        r_ = sl["r"]
                        TR, MK, ATm, X1 = sl["TR"], sl["MK"], sl["ATX"][:, 0, :], sl["ATX"][:, 1, :]
                        m2, m2r = P.bank()
                        P.mm(m2[:, 0:128], sl["Pb"], X1, True, True, [r_["Pb"], r_["ATX"]], [m2r])
                        P.copy("act", sl["CV"], m2[:, 0:128], [m2r], [r_["CV"]])
                        m3, m3r = P.bank()
                        P.mm(m3[:, 0:128], ATm, TR[:, 1, :], True, True, [r_["ATX"], r_["TR"]], [m3r], inc=False)
                        P.mm(m3[:, 128:256], ATm, MK[:, 1, :], True, True, [r_["ATX"], r_["MK"]], [m3r], inc=True)
                        P.tt("dve", sl["MT"], m3[:, 0:128], I32, ALU.add, [m3r, cr], [r_["MT"]])
                        P.tt("dve", sl["RT"], m3[:, 128:256], ZAR[:, c, 1, :], ALU.add, [m3r, ZARr], [r_["RT"]])
                    for i, c in enumerate(cs_):
                        sl = slot[i]
                        r_ = sl["r"]
                        TR, MK, CV = sl["TR"], sl["MK"], sl["CV"]
                        m4, m4r = P.bank()
                        P.mm(m4[:, 0:128], TR[:, 1, :], CV, True, False, [r_["TR"], r_["CV"]], [m4r], inc=False)
                        P.mm(m4[:, 0:128], TR[:, 2, :], TR[:, 0, :], False, True, [r_["TR"]], [m4r], inc=False)
                        P.mm(m4[:, 128:256], CV, MK[:, 1, :], True, False, [r_["CV"], r_["MK"]], [m4r], inc=False)
                        P.mm(m4[:, 128:256], TR[:, 0, :], MK[:, 2, :], False, True, [r_["TR"], r_["MK"]], [m4r], inc=True)
                        P.copy("act", sl["NY"].rearrange("p a b -> p (a b)"), m4[:, 0:256], [m4r], [r_["NY"]])
                    for i, c in enumerate(cs_):
                        sl = slot[i]
                        r_ = sl["r"]
                        yb, ybr = P.bank()
                        P.mm(yb[:, 0:128], Hp, sl["RT"], True, True, [Hpr, r_["RT"]], [ybr])
                        for hp in range(2):
                            rows = slice(hp * 64, (hp + 1) * 64)
                            yc = Y[rows, c * 64:(c + 1) * 64]
                            P.tt("dve", yc, yb[rows, hp * 64:(hp + 1) * 64], yc, ALU.add, [ybr, Yr], [Yr])
                            P.tt("pool", yc, sl["NY"][rows, 1, hp * 64:(hp + 1) * 64], yc, ALU.add, [r_["NY"], Yr], [Yr])
                        hb, hbr = P.bank()
                        P.mm(hb[:, 0:128], sl["MT"], Hp, True, True, [Hpr, r_["MT"]], [hbr])
                        P.tt("dve", Hp, hb[:, 0:128], sl["NY"][:, 0, :], ALU.add, [hbr, r_["NY"], Hpr], [Hpr])
                        P.ts("dve", Hp, Hp, DEC[:, c:c + 1], None, ALU.mult, None, [Hpr, DECr], [Hpr])
            P.tt("dve", TMP, R, KS, ALU.mult, [Rr, KSr], [TMPr])
            P.ts("dve", SQb, TMP, pk(12), None, ALU.mult, None, [TMPr, cr], [SQr])
            bdsum(SQb, SQr, LW, LWr)
            P.dma("sp", TMP, P.RKV[2][rows128, :], [RKVr, SQr], [TMPr])
            P.tt("dve", LW, LW, TMP, ALU.mult, [LWr, TMPr], [LWr])
            P.copy("act", SQb, Y, [Yr, LWr], [SQr])
            bdsum(SQb, SQr, IC, ICr)
            P.act(SQb, Y, AF.Square, [Yr, ICr], [SQr])
            bdsum(SQb, SQr, Bt, Br)
            P.ts("dve", IC, IC, 1.0 / 64, None, ALU.mult, None, [ICr], [ICr])
            P.tt("dve", TMP, IC, IC, ALU.mult, [ICr, LWr], [TMPr])
            P.stt(Bt, Bt, 1.0 / 64, TMP, ALU.mult, ALU.subtract, [Br, TMPr], [Br])
            P.act(Bt, Bt, AF.Ln, [Br, cr], [Br], bias=cst[:, 1:2])
            P.act(Bt, Bt, AF.Exp, [Br], [Br], scale=-0.5)
            P.tt("dve", Y, Y, IC, ALU.subtract, [Yr, ICr], [Yr])
            P.tt("dve", Y, Y, Bt, ALU.mult, [Yr, Br], [Yr])
            P.ts("dve", Y, Y, pk(13), pk(14), ALU.mult, ALU.add, [Yr, cr], [Yr])
            P.tt("pool", Y, Y, LW, ALU.add, [Yr, LWr], [Yr])
            P.dma("sp", TMP, P.G32[rows128, :], [RKVr, Br], [TMPr])
            P.tt("dve", SQb, Y, TMP, ALU.mult, [Yr, TMPr], [SQr])
            P.dma("sp", P.OT[rows128, :], SQb, [SQr], [P.OTr])

    def bank_in(self, lo, hi):
        c = self.bctr.get((lo, hi), 0)
        self.bctr[(lo, hi)] = c + 1
        b = lo + c % (hi - lo)
        return self.ps[:, b, :], self.psr[b]

    def mixer_da(self, li):
        P, kb = self, self.kb
        XT = P.XT.rearrange("p (k t) -> p k t", t=T)
        groups = _split_cols(0, T)
        P.carve_reset()
        cs = P.carve([2 * T], F32)
        csr = kb.reg()
        P.dma("sp", cs, P.da_cs, [], [csr])
        t1, t2 = P.carve([512], F32), P.carve([512], F32)
        t1r, t2r = kb.reg(), kb.reg()
        qo = [P.carve([T], BF16) for _ in range(2)]
        qor = kb.regs_n(2)
        pf = P._Pf(P, [(P.da_wqk[b], KC * 256) for b in range(32)])
        for blk in range(32):
            ws, wr = pf.get(blk)
            w3 = ws[:, 0:KC * 256].rearrange("p (k c) -> p k c", c=256)
            q_o, q_or = qo[blk % 2], qor[blk % 2]
            for (a, b_) in groups:
                n = b_ - a
                pa, par_ = P.bank()
                pb, pbr = P.bank()
                for kc in range(KC):
                    P.mm(pa[:, 0:n], w3[:, kc, 0:128], XT[:, kc, a:b_], kc == 0, kc == KC - 1, [wr, P.XTr], [par_])
                for kc in range(KC):
                    P.mm(pb[:, 0:n], w3[:, kc, 128:256], XT[:, kc, a:b_], kc == 0, kc == KC - 1, [wr, P.XTr], [pbr])
                P.tt("dve", t1[:, 0:n], pa[:, 0:n], cs[:, a:b_], ALU.mult, [par_, csr], [t1r])
                P.tt("dve", t2[:, 0:n], pb[:, 0:n], cs[:, T + a:T + b_], ALU.mult, [pbr, csr], [t2r])
                P.tt("pool", t1[:, 0:n], t1[:, 0:n], t2[:, 0:n], ALU.add, [t1r, t2r], [t1r])
                P.copy("act", q_o[:, a:b_], t1[:, 0:n], [t1r], [q_or])
            dst = P.QT if blk < 16 else P.KT
            r0 = (blk % 16) * 128
            P.dma("sp", dst[r0:r0 + 128, :], q_o, [q_or], [P.QKr])
        kb.barrier()
        stop = P.cfg.get("da_stop", 9)
        if stop <= 1:
            return
        P.carve_reset()
        lam_init = 0.8 - 0.6 * math.exp(-0.3 * li)
        lt = P.carve([512], F32)
        sgt = P.carve([2], F32)
        sm = P.carve([8], F32)
        smr = kb.reg()
        P.dma("sp", lt, P.da_lam.to_broadcast([128, 512]), [], [smr])
        P.dma("sp", sgt, P.da_subg, [], [smr])
        P.ts("dve", sgt, sgt, 1.0 - lam_init, None, ALU.mult, None, [smr], [smr])
        P.tt("dve", lt[:, 0:128], lt[:, 0:128], lt[:, 128:256], ALU.mult, [smr], [smr])
        P.tt("dve", lt[:, 256:384], lt[:, 256:384], lt[:, 384:512], ALU.mult, [smr], [smr])
        kb.emit("dve", lambda e: e.tensor_reduce(out=sm[:, 0:1], in_=lt[:, 0:128], axis=AX.X, op=ALU.add), [smr], [smr])
        kb.emit("dve", lambda e: e.tensor_reduce(out=sm[:, 1:2], in_=lt[:, 256:384], axis=AX.X, op=ALU.add), [smr], [smr])
        P.act(sm[:, 0:2], sm[:, 0:2], AF.Exp, [smr], [smr])
        P.stt(sm[:, 2:3], sm[:, 1:2], -lam_init, sm[:, 0:1], ALU.add, ALU.subtract, [smr], [smr])
        neglam = sm[:, 2:3]
        kb.emit("pool", lambda e: e.memset(sm[:, 3:4], 1e-5), [smr], [smr])
        epsc = sm[:, 3:4]
        Va = P.carve([18, 256], BF16)
        Var = kb.reg()
        qk = [P.carve([T], BF16) for _ in range(4)]
        qkr = kb.regs_n(4)
        PT = [P.carve([512], BF16) for _ in range(4)]
        PTr = kb.regs_n(4)
        pti = 0
        Od = P.carve([2, 512], F32)
        Odr = kb.reg()
        tmp = P.carve([2, 512], F32)
        tmpr = kb.reg()
        rl = P.carve([512], F32)
        rlr = kb.reg()
        sq = P.carve([2, 512], BF16)
        sqr = kb.reg()
        OTh = P.carve([2, T], BF16)
        OThr = kb.reg()
        pfv = P._Pf(P, [(P.da_wv[h], KC * 256) for h in range(8)], depth=1)
        for hd in range(8):
            ws, wr = pfv.get(hd)
            w3 = ws[:, 0:KC * 256].rearrange("p (k c) -> p k c", c=256)
            for tb in range(18):
                pb, pr = P.bank_in(3, 8)
                for kc in range(KC):
                    P.mm(pb[:, 0:256], XT[:, kc, tb * 128:(tb + 1) * 128], w3[:, kc, :], kc == 0, kc == KC - 1,
                         [wr, P.XTr], [pr])
                P.copy("act", Va[:, tb, :], pb[:, 0:256], [pr], [Var])
            for m in range(2):
                r0 = (hd * 2 + m) * 128
                P.dma("sp", qk[m], P.QT[r0:r0 + 128, :], [P.QKr], [qkr[m]])
                P.dma("sp", qk[2 + m], P.KT[r0:r0 + 128, :], [P.QKr], [qkr[2 + m]])
            its = []
            for (a, b_) in groups:
                nk = 2 if b_ <= NCTX else 18
                for m in range(2):
                    for kbk in range(nk):
                        its.append((a, b_, nk, m, kbk))
            sbanks = {}

            def emit_s(i):
                a, b_, nk, m, kbk = its[i]
                sb, sr = P.bank_in(3, 7)
                P.mm(sb[:, 0:b_ - a], qk[2 + m][:, kbk * 128:(kbk + 1) * 128], qk[m][:, a:b_], True, True,
                     [qkr[m], qkr[2 + m]], [sr])
                sbanks[i] = (sb, sr)

            for i in range(min(2, len(its))):
                emit_s(i)
            for i, (a, b_, nk, m, kbk) in enumerate(its):
                n = b_ - a
                if i + 2 < len(its):
                    emit_s(i + 2)
                sb, sr = sbanks.pop(i)
                pt, ptr = PT[pti % 4], PTr[pti % 4]
                pti += 1
                P.act(pt[:, 0:n], sb[:, 0:n], AF.Exp, [sr], [ptr], scale=float(128 ** -0.5))
                st_, sp_ = kbk == 0, kbk == nk - 1
                for eh in range(2):
                    P.mm(P.ps[:, eh, 0:n], Va[:, kbk, eh * 128:(eh + 1) * 128], pt[:, 0:n], st_, sp_,
                         [ptr, Var], [P.psr[eh]])
                P.mm(P.ps[:, 2, 0:n], P.ones, pt[:, 0:n], st_, sp_, [ptr, P.onesr], [P.psr[2]])
                if kbk != nk - 1:
                    continue
                P.act(rl[:, 0:n], P.ps[:, 2, 0:n], AF.Ln, [P.psr[2]], [rlr])
                P.act(rl[:, 0:n], rl[:, 0:n], AF.Exp, [rlr], [rlr], scale=-1.0)
                for eh in range(2):
                    if m == 0:
                        P.tt("dve", Od[:, eh, 0:n], P.ps[:, eh, 0:n], rl[:, 0:n], ALU.mult, [P.psr[eh], rlr], [Odr])
                    else:
                        P.tt("dve", tmp[:, eh, 0:n], P.ps[:, eh, 0:n], rl[:, 0:n], ALU.mult, [P.psr[eh], rlr], [tmpr])
                        P.stt(Od[:, eh, 0:n], tmp[:, eh, 0:n], neglam, Od[:, eh, 0:n], ALU.mult, ALU.add,
                              [tmpr, Odr, smr], [Odr])
                if m == 0:
                    continue
                P.act(sq[:, :, 0:n], Od[:, :, 0:n], AF.Square, [Odr], [sqr])
                rb_, rbr = P.ps[:, 7, :], P.psr[7]
                for eh in range(2):
                    P.mm(rb_[:, 0:n], P.ones, sq[:, eh, 0:n], eh == 0, eh == 1, [sqr, P.onesr], [rbr])
                P.act(rl[:, 0:n], rb_[:, 0:n], AF.Ln, [rbr, smr], [rlr], bias=epsc, scale=1.0 / 256)
                P.act(rl[:, 0:n], rl[:, 0:n], AF.Exp, [rlr], [rlr], scale=-0.5)
                for eh in range(2):
                    P.stt(OTh[:, eh, a:b_], Od[:, eh, 0:n], sgt[:, eh:eh + 1], rl[:, 0:n], ALU.mult, ALU.mult,
                          [Odr, rlr, smr], [OThr])
            for eh in range(2):
                r0 = hd * 256 + eh * 128
                P.dma("sp", P.OT[r0:r0 + 128, :], OTh[:, eh, :], [OThr], [P.OTr])

    def mixer_hg(self, li):
        P, kb = self, self.kb
        P.carve_reset()
        XT = P.XT.rearrange("p (k t) -> p k t", t=T)
        groups = _split_cols(0, T)
        rv = lambda t: t[:, ::-1]
        low = P.carve([2 * 4 * KC], F32)
        low4 = low.rearrange("p (d l k) -> p d l k", d=2, l=4)
        lb = P.carve([2 * KC], F32)
        oml = P.carve([2 * KC], F32)
        den = P.carve([2 * KC], F32)
        sm = P.carve([4], F32)
        cr = kb.reg()
        lb3 = lb.rearrange("p (d k) -> p d k", d=2)
        den3 = den.rearrange("p (d k) -> p d k", d=2)
        P.dma("sp", low, P.hg_low, [], [cr])
        P.dma("sp", sm[:, 0:1], P.hg_ng, [], [cr])
        kb.emit("pool", lambda e: e.memset(sm[:, 1:2], 1e-5), [], [cr])
        kb.emit("pool", lambda e: e.memset(sm[:, 2:3], 1.0), [], [cr])
        P.act(low, low, AF.Exp, [cr], [cr])
        P.tt("dve", den3, low4[:, :, 0, :], low4[:, :, 1, :], ALU.add, [cr], [cr])
        P.tt("dve", den3, den3, low4[:, :, 2, :], ALU.add, [cr], [cr])
        P.tt("dve", den3, den3, low4[:, :, 3, :], ALU.add, [cr], [cr])
        kb.emit("dve", lambda e: e.reciprocal(out=den, in_=den), [cr], [cr])
        P.copy("dve", lb3, low4[:, :, 1, :], [cr], [cr])
        for l in range(2, li + 1):
            P.tt("dve", lb3, lb3, low4[:, :, l, :], ALU.add, [cr], [cr])
        P.tt("dve", lb, lb, den, ALU.mult, [cr], [cr])
        P.ts("dve", oml, lb, -1.0, 1.0, ALU.mult, ALU.add, [cr], [cr])
        cm = P.carve([T + 64], BF16)
        msk = P.carve([256], BF16)
        P.dma("pool", cm, P.hg_cm, [], [cr])
        P.dma("pool", msk, P.hg_msk, [], [cr])
        names = ["Q", "F", "K", "TMP", "O"]
        tl = {n: P.carve([T], F32) for n in names}
        rg = {n: kb.reg() for n in names}
        bn = ["QD", "KD", "KE", "SQ"]
        bt = {n: P.carve([T], BF16) for n in bn}
        br = {n: kb.reg() for n in bn}
        Vt = P.carve([18, 128], BF16)
        ATT = P.carve([18, 128], BF16)
        KET = P.carve([18, 128], BF16)
        Vr, ATr, KEr = kb.reg(), kb.reg(), kb.reg()
        S = P.carve([128], F32)
        Sb = P.carve([128], BF16)
        DEC = P.carve([36], F32)
        Sr, Sbr, DECr = kb.reg(), kb.reg(), kb.reg()
        seq = []
        for hd in range(16):
            seq += [(P.hg_win[c * 16 + hd], KC * 128) for c in (0, 1, 3, 4, 2)]
        pf = P._Pf(P, seq)
        Q, F, K_, TMP, O = [tl[n] for n in names]
        Qr, Fr, Kr, TMPr, Or = [rg[n] for n in names]

        def fm_proj(ws, wr, func, dst, dstr):
            P.proj(lambda kc: ws[:, kc * 128:(kc + 1) * 128], KC, 128, lambda kc, a, b_: XT[:, kc, a:b_],
                   [wr, P.XTr], groups,
                   lambda pb, pr, a, b_: P.act(dst[:, a:b_], pb[:, 0:b_ - a], func, [pr], [dstr]))

        for hd in range(16):
            ws, wr = pf.get(5 * hd + 0)
            fm_proj(ws, wr, AF.Silu, Q, Qr)
            ws, wr = pf.get(5 * hd + 1)
            for tb in range(18):
                pb, pr = P.bank()
                for kc in range(KC):
                    P.mm(pb[:, 0:128], XT[:, kc, tb * 128:(tb + 1) * 128], ws[:, kc * 128:(kc + 1) * 128],
                         kc == 0, kc == KC - 1, [wr, P.XTr], [pr])
                P.copy("act", Vt[:, tb, :], pb[:, 0:128], [pr], [Vr])
            for d in range(2):
                ws, wr = pf.get(5 * hd + 2 + d)
                fm_proj(ws, wr, AF.Sigmoid, F, Fr)
                P.ts("dve", F, F, oml[:, d * KC + hd:d * KC + hd + 1], lb[:, d * KC + hd:d * KC + hd + 1],
                     ALU.mult, ALU.add, [Fr, cr], [Fr])
                P.ts("pool", K_, F, -1.0, 1.0, ALU.mult, ALU.add, [Fr], [Kr])
                P.act(TMP, F, AF.Ln, [Fr], [TMPr])
                B = F
                if d == 0:
                    kb.emit("dve", lambda e: e.tensor_tensor_scan(out=B, data0=cm[:, 0:T], data1=TMP, initial=0.0,
                                                                  op0=ALU.mult, op1=ALU.add), [TMPr, cr, Fr], [Fr])
                else:
                    kb.emit("dve", lambda e: e.tensor_tensor_scan(out=rv(B), data0=rv(cm[:, 1:T + 1]), data1=rv(TMP),
                                                                  initial=0.0, op0=ALU.mult, op1=ALU.add),
                            [TMPr, cr, Fr], [Fr])
                B3 = B.rearrange("p (c i) -> p c i", i=64)
                btot = B3[:, :, 63] if d == 0 else B3[:, :, 0]
                P.act(TMP, B, AF.Exp, [Fr], [TMPr])
                P.tt("dve", bt["QD"], Q, TMP, ALU.mult, [Qr, TMPr], [br["QD"]])
                P.act(TMP, B, AF.Exp, [Fr, br["QD"]], [TMPr], scale=-1.0)
                P.tt("dve", bt["KD"], K_, TMP, ALU.mult, [Kr, TMPr], [br["KD"]])
                P.act(DEC, btot, AF.Exp, [Fr], [DECr])
                T3 = TMP.rearrange("p (c i) -> p c i", i=64)
                P.tt("dve", T3, btot.unsqueeze(2).to_broadcast([128, 36, 64]), B3, ALU.subtract, [Fr, br["KD"]], [TMPr])
                P.act(TMP, TMP, AF.Exp, [TMPr], [TMPr])
                P.tt("dve", bt["KE"], K_, TMP, ALU.mult, [Kr, TMPr], [br["KE"]])
                mk = msk[:, d * 128:(d + 1) * 128]
                for tb in range(18):
                    pb, pr = P.bank_in(0, 4)
                    P.mm(pb[:, 0:128], bt["KD"][:, tb * 128:(tb + 1) * 128], bt["QD"][:, tb * 128:(tb + 1) * 128],
                         True, True, [br["KD"], br["QD"]], [pr])
                    P.tt("dve", ATT[:, tb, :], pb[:, 0:128], mk, ALU.mult, [pr, cr], [ATr])
                    tb_, tr = P.bank_in(0, 4)
                    tpv = tb_.bitcast(BF16)[:, 0:128]
                    src = bt["KE"][:, tb * 128:(tb + 1) * 128]
                    kb.emit("pe", lambda e, tpv=tpv, src=src: e.transpose(out=tpv, in_=src, identity=P.ident),
                            [br["KE"], P.identr], [tr])
                    P.copy("act", KET[:, tb, :], tpv, [tr], [KEr])
                kb.emit("pool", lambda e: e.memset(S, 0.0), [Sr], [Sr])
                kb.emit("pool", lambda e: e.memset(Sb, 0.0), [Sbr], [Sbr])
                order = list(range(18)) if d == 0 else [1, 0] + list(range(17, 1, -1))
                for tb in order:
                    ob, obr = P.bank_in(4, 6)
                    P.mm(ob[:, 0:128], Vt[:, tb, :], ATT[:, tb, :], True, False, [Vr, ATr], [obr], inc=False)
                    halves = (0, 1) if d == 0 else (1, 0)
                    for hi, hf in enumerate(halves):
                        c = 2 * tb + hf
                        co = hf * 64
                        P.mm(ob[:, co:co + 64], Sb, bt["QD"][:, c * 64:(c + 1) * 64], False, hi == 1,
                             [Sbr, br["QD"]], [obr], inc=True)
                        sb_, sbr_ = P.bank_in(6, 8)
                        P.mm(sb_[:, 0:128], KET[co:co + 64, tb, :], Vt[co:co + 64, tb, :], True, True,
                             [KEr, Vr], [sbr_])
                        P.stt(S, S, DEC[:, c:c + 1], sb_[:, 0:128], ALU.mult, ALU.add, [Sr, DECr, sbr_], [Sr])
                        P.copy("act", Sb, S, [Sr], [Sbr])
                    if d == 0:
                        P.copy("act", O[:, tb * 128:(tb + 1) * 128], ob[:, 0:128], [obr], [Or])
                    else:
                        P.tt("dve", O[:, tb * 128:(tb + 1) * 128], ob[:, 0:128], O[:, tb * 128:(tb + 1) * 128],
                             ALU.add, [obr, Or], [Or])
            ws, wr = pf.get(5 * hd + 4)
            fm_proj(ws, wr, AF.Silu, F, Fr)
            P.act(bt["SQ"], O, AF.Square, [Or], [br["SQ"]])
            for (a, b_) in groups:
                n = b_ - a
                pb, pr = P.bank_in(0, 4)
                P.mm(pb[:, 0:n], P.ones, bt["SQ"][:, a:b_], True, True, [br["SQ"], P.onesr], [pr])
                P.act(TMP[:, a:b_], pb[:, 0:n], AF.Ln, [pr, cr], [TMPr], bias=sm[:, 1:2], scale=1.0 / 128)
            P.act(TMP, TMP, AF.Exp, [TMPr], [TMPr], scale=-0.5)
            P.stt(O, O, sm[:, 0:1], TMP, ALU.mult, ALU.mult, [Or, TMPr, cr], [Or])
            P.tt("dve", bt["SQ"], O, F, ALU.mult, [Or, Fr], [br["SQ"]])
            P.dma("sp", P.OT[hd * 128:(hd + 1) * 128, :], bt["SQ"], [br["SQ"]], [P.OTr])


def _fm(v, nblk=None):
    v = np.asarray(v, np.float32)
    lead = v.shape[:-1]
    nb = v.shape[-1] // 128
    v = v.reshape(lead + (nb, 128))
    return np.ascontiguousarray(np.moveaxis(v, -1, 0))


def host_common(inp):
    m = {}
    aw = np.asarray(inp["ada_w"], np.float32)
    m["adaw"] = np.ascontiguousarray(
        aw.reshape(DEPTH, KC, 128, 96, 128).transpose(0, 3, 2, 1, 4)).reshape(DEPTH, 96, 128, KC * 128)
    m["adab"] = _fm(inp["ada_b"]).reshape(128, DEPTH * 96)
    m["lng"] = _fm(inp["ln_g"]).reshape(128, DEPTH * 2 * KC)
    m["lnb"] = _fm(inp["ln_b"]).reshape(128, DEPTH * 2 * KC)
    wu = np.asarray(inp["ffn_w_up"], np.float32).reshape(DEPTH, KC, 128, 2, FB, 128)
    m["wup"] = np.ascontiguousarray(wu.transpose(0, 4, 2, 1, 3, 5)).reshape(DEPTH, FB, 128, KC * 256)
    wd = np.asarray(inp["ffn_w_down"], np.float32).reshape(DEPTH, FB, 128, KC, 128)
    m["wdn"] = np.ascontiguousarray(wd.transpose(0, 3, 2, 1, 4)).reshape(DEPTH, KC, 128, FB * 128)
    cw = _fm(inp["ffn_conv_w"])
    m["fcw"] = np.ascontiguousarray(cw.transpose(0, 1, 3, 2)).reshape(128, DEPTH * 88 * 3)
    m["fcb"] = _fm(inp["ffn_conv_b"]).reshape(128, DEPTH * 88)
    return m


def host_core(inp, b):
    m = {}
    z = np.concatenate([np.asarray(inp["ctx"][b], np.float32), np.asarray(inp["x"][b], np.float32)], axis=0)
    m["zin"] = np.ascontiguousarray(z.T)
    sc = np.stack([np.asarray(inp["c"][b], np.float32), np.asarray(inp["c_ctx"], np.float32)], axis=-1)
    m["scT"] = np.ascontiguousarray(sc.reshape(KC, 128, 2).transpose(1, 0, 2)).reshape(128, KC * 2)
    return m


_CACHE = {}


def kernel(**inputs):
    cfg = {"layers": [0, 1, 2, 3], "mixer": True}
    if "prog" not in _CACHE:
        p = Prog(cfg)
        _CACHE["prog"] = (p, p.build())
    p, nc = _CACHE["prog"]
    common = host_common(inputs)
    common.update(p.host_mix(inputs))
    in_maps = []
    for core in range(NCORES):
        m = dict(common)
        m.update(host_core(inputs, core % 4))
        in_maps.append(m)
    res = run_bass_kernel_spmd(nc, in_maps, core_ids=list(range(NCORES)))
    out = np.stack([np.ascontiguousarray(res.results[b]["outT"].T) for b in range(4)], axis=0)
    return out.astype(np.float32)
```

```python
import contextlib
import math
import numpy as np
import concourse.bass as bass
import concourse.mybir as mybir
from concourse.bass_utils import run_bass_kernel_spmd

F32 = mybir.dt.float32
BF16 = mybir.dt.bfloat16
AF = mybir.ActivationFunctionType
ALU = mybir.AluOpType
AX = mybir.AxisListType

D = 2048
KC = 16
NCTX = 256
SEQ = 2048
T = NCTX + SEQ
FF = 5632
FB = FF // 128
DEPTH = 4
DN_ALPHA = (2 * DEPTH) ** 0.25
LN_EPS = 1e-5
NQ = 20
NCORES = 4


class Reg:
    __slots__ = ("w", "r")

    def __init__(self):
        self.w = None
        self.r = {}


class KB:
    ENGS = ["pe", "act", "dve", "pool", "sp"]

    def __init__(self, nc, es):
        self.nc = nc
        self.sem = {}
        for e in self.ENGS:
            self.sem[e] = es.enter_context(nc.semaphore("s_" + e))
        self.sem["bar"] = es.enter_context(nc.semaphore("s_bar"))
        self.dq = ["sp", "pool", "act"]
        for q in self.dq:
            for j in range(NQ):
                self.sem[(q, j)] = es.enter_context(nc.semaphore("d_%s_%d" % (q, j)))
        self.tick = {e: 0 for e in self.ENGS}
        self.pending = {e: False for e in self.ENGS}
        self.dcount = {q: 0 for q in self.dq}
        self.dlast = {}
        self.seen = {e: {} for e in self.ENGS}
        self.prog = {e: [] for e in self.ENGS}
        self.regs = []
        self.nbar = 0
        self.ninstr = 0

    def reg(self):
        r = Reg()
        self.regs.append(r)
        return r

    def regs_n(self, n):
        return [self.reg() for _ in range(n)]

    def _need(self, eng, waits, ev):
        if ev is None:
            return
        sid, val = ev
        if eng == "pe" and sid == "pe":
            return
        if self.seen[eng].get(sid, 0) >= val:
            return
        if waits.get(sid, 0) < val:
            waits[sid] = val

    def emit(self, eng, fn, reads=(), writes=(), inc=True, dma=False):
        waits = {}
        for r in reads:
            self._need(eng, waits, r.w)
        for w in writes:
            self._need(eng, waits, w.w)
            for sid, val in w.r.items():
                self._need(eng, waits, (sid, val))
        if dma:
            j = self.dcount[eng]
            self.dcount[eng] = j + 1
            slot = (eng, j % NQ)
            val = 16 * (j // NQ + 1)
            if val > 16:
                self._need(eng, waits, (slot, val - 16))
            ev = (slot, val)
            self.dlast[slot] = val
            kind = 2
        else:
            if inc:
                self.tick[eng] += 1
                ev = (eng, self.tick[eng])
                self.pending[eng] = False
                kind = 1
            else:
                ev = (eng, self.tick[eng] + 1)
                self.pending[eng] = True
                kind = 0
        for sid, val in waits.items():
            self.seen[eng][sid] = val
        for r in reads:
            if r.r.get(ev[0], 0) < ev[1]:
                r.r[ev[0]] = ev[1]
        for w in writes:
            w.w = ev
            w.r = {}
        self.prog[eng].append((list(waits.items()), fn, kind, ev))
        self.ninstr += 1
        return ev

    def barrier(self):
        for e in self.ENGS:
            assert not self.pending[e], e
        self.nbar += 1
        waits = {}
        for e in self.ENGS:
            if e != "sp" and self.tick[e] > 0:
                self._need("sp", waits, (e, self.tick[e]))
        for slot, val in self.dlast.items():
            self._need("sp", waits, (slot, val))
        for sid, val in waits.items():
            self.seen["sp"][sid] = val
        nb = self.nbar
        self.prog["sp"].append((list(waits.items()), ("bar", nb), 3, None))
        for e in self.ENGS:
            if e != "sp":
                self.prog[e].append(([("bar", nb)], None, 4, None))
                for e2 in self.ENGS:
                    self.seen[e][e2] = self.tick[e2]
                for slot, val in self.dlast.items():
                    self.seen[e][slot] = val
        for r in self.regs:
            r.w = None
            r.r = {}

    def replay(self, block):
        nc = self.nc
        sem = self.sem

        def run(engname, eh):
            for waits, fn, kind, ev in self.prog[engname]:
                for sid, val in waits:
                    eh.wait_ge(sem[sid], val)
                if kind == 3:
                    eh.sem_inc(sem["bar"], 1)
                    continue
                if kind == 4:
                    continue
                ins = fn(eh)
                if kind == 1:
                    ins.then_inc(sem[ev[0]], 1)
                elif kind == 2:
                    ins.then_inc(sem[ev[0]], 16)

        @block.tensor
        def _(e):
            run("pe", e)

        @block.scalar
        def _(e):
            run("act", e)

        @block.vector
        def _(e):
            run("dve", e)

        @block.gpsimd
        def _(e):
            run("pool", e)

        @block.sync
        def _(e):
            run("sp", e)


def _split_cols(c0, c1, w=512):
    out = []
    c = c0
    while c < c1:
        e = min(c + w, c1)
        if c < NCTX < e:
            e = NCTX
        out.append((c, e))
        c = e
    return out


class Prog:
    def __init__(self, cfg):
        self.cfg = cfg
        self.nc = bass.Bass("TRN2", target_bir_lowering=False)
        self.es = contextlib.ExitStack()
        self.inputs = {}
        self.kb = None

    def din(self, name, shape, dt=F32):
        t = self.nc.dram_tensor(name, list(shape), dt, kind="ExternalInput")
        self.inputs[name] = (tuple(shape), dt)
        return t.ap()

    def dscratch(self, name, shape, dt):
        return self.nc.dram_tensor(name, list(shape), dt, kind="Internal").ap()

    def dout(self, name, shape, dt=F32):
        return self.nc.dram_tensor(name, list(shape), dt, kind="ExternalOutput").ap()

    def carve_reset(self, mark=None):
        self.apos = self.abase if mark is None else mark

    def carve(self, shape, dt):
        n = int(np.prod(shape))
        words = n if dt == F32 else (n + 1) // 2
        words = (words + 7) // 8 * 8
        assert self.apos + words <= self.asize, ("arena overflow", self.apos, words, self.asize)
        ap = self.arena[:, self.apos:self.apos + words]
        self.apos += words
        if dt != F32:
            ap = ap.bitcast(dt)
        ap = ap[:, 0:n]
        if len(shape) == 2:
            ap = ap.rearrange("p (a b) -> p a b", b=shape[1])
        elif len(shape) == 3:
            ap = ap.rearrange("p (a b c) -> p a b c", b=shape[1], c=shape[2])
        return ap

    def carve_lo(self, shape, dt):
        save = (self.apos, self.asize)
        self.apos, self.asize = self.lopos, self.lo_end
        ap = self.carve(shape, dt)
        self.lopos = self.apos
        self.apos, self.asize = save
        return ap

    def dma(self, q, out, in_, reads, writes):
        fn = lambda e: e.dma_start(out=out, in_=in_)
        return self.kb.emit(q, fn, reads, writes, dma=True)

    def act(self, out, in_, func, reads, writes, bias=None, scale=None, eng="act"):
        kw = {}
        if bias is not None:
            kw["bias"] = bias
        if scale is not None:
            kw["scale"] = scale
        fn = lambda e: e.activation(out=out, in_=in_, func=func, **kw)
        return self.kb.emit("act", fn, reads, writes)

    def tt(self, eng, out, in0, in1, op, reads, writes):
        fn = lambda e: e.tensor_tensor(out=out, in0=in0, in1=in1, op=op)
        return self.kb.emit(eng, fn, reads, writes)

    def ts(self, eng, out, in0, s1, s2, op0, op1, reads, writes):
        if op1 is None:
            fn = lambda e: e.tensor_scalar(out=out, in0=in0, scalar1=s1, scalar2=None, op0=op0)
        else:
            fn = lambda e: e.tensor_scalar(out=out, in0=in0, scalar1=s1, scalar2=s2, op0=op0, op1=op1)
        return self.kb.emit(eng, fn, reads, writes)

    def stt(self, out, in0, scalar, in1, op0, op1, reads, writes):
        fn = lambda e: e.scalar_tensor_tensor(out=out, in0=in0, scalar=scalar, in1=in1, op0=op0, op1=op1)
        return self.kb.emit("dve", fn, reads, writes)

    def copy(self, eng, out, in_, reads, writes):
        if eng == "act":
            fn = lambda e: e.activation(out=out, in_=in_, func=AF.Copy)
        else:
            fn = lambda e: e.tensor_copy(out=out, in_=in_)
        return self.kb.emit(eng, fn, reads, writes)

    def mm(self, out, lhsT, rhs, start, stop, reads, writes, inc=None):
        fn = lambda e: e.matmul(out, lhsT=lhsT, rhs=rhs, start=start, stop=stop)
        return self.kb.emit("pe", fn, reads, writes, inc=(stop if inc is None else inc))

    def bank(self):
        b = self.pbank % 8
        self.pbank += 1
        return self.ps[:, b, :], self.psr[b]

    def build(self):
        nc, es, cfg = self.nc, self.es, self.cfg
        layers = cfg["layers"]
        P = self
        zin = P.din("zin", [D, T])
        P.Z = P.dscratch("Z", [D, T], F32)
        P.ACTT = P.dscratch("ACTT", [FF, T], BF16)
        P.OT = P.dscratch("OT", [D, T], BF16)
        outT = P.dout("outT", [D, SEQ])
        scT = P.din("scT", [128, KC * 2])
        adaw = P.din("adaw", [DEPTH, 96, 128, KC * 128])
        adab = P.din("adab", [128, DEPTH * 96])
        lng = P.din("lng", [128, DEPTH * 2 * KC])
        lnb = P.din("lnb", [128, DEPTH * 2 * KC])
        wup = P.din("wup", [DEPTH, FB, 128, KC * 256])
        wdn = P.din("wdn", [DEPTH, KC, 128, FB * 128])
        fcw = P.din("fcw", [128, DEPTH * 88 * 3])
        fcb = P.din("fcb", [128, DEPTH * 88])
        P.mix_inputs()
        dbg = {}
        for name, shape in cfg.get("debug", {}).items():
            dbg[name] = P.dout(name, shape)
        P.dbg = dbg

        P.asize = 51200
        P.arena = es.enter_context(nc.sbuf_tensor("arena", [128, P.asize], F32))[:]
        P.ps = es.enter_context(nc.psum_tensor("ps", [128, 8, 512], F32))[:]
        kb = P.kb = KB(nc, es)
        P.psr = kb.regs_n(8)
        P.pbank = 0
        P.apos = 0
        P.XT = P.carve([KC * T], BF16)
        P.XTr = kb.reg()
        P.WS = [P.carve([44 * 128], BF16) for _ in range(3)]
        P.WSr = kb.regs_n(3)
        P.wsi = 0
        P.lo_end = P.apos
        P.mod = P.carve([DEPTH * 96 * 2], F32)
        P.mod1 = P.carve([DEPTH * 96 * 2], F32)
        P.modr = kb.reg()
        P.ones = P.carve([128], BF16)
        P.onesr = kb.reg()
        P.lng_t = P.carve([DEPTH * 2 * KC], F32)
        P.lnb_t = P.carve([DEPTH * 2 * KC], F32)
        P.adab_t = P.carve([DEPTH * 96], F32)
        P.fcw_t = P.carve([88 * 3], F32)
        P.fcb_t = P.carve([88], F32)
        P.parr = kb.reg()
        P.Gp = P.carve([2 * KC], F32)
        P.Bp = P.carve([2 * KC], F32)
        P.gpr = P.carve([2 * KC], F32)
        P.gpreg = kb.reg()
        P.ident = P.carve([128], BF16)
        P.identr = kb.reg()
        P.QKr = kb.reg()
        P.bctr = {}
        P.abase = P.apos
        P.Zr = kb.reg()
        P.ACTTr = kb.reg()
        P.OTr = kb.reg()
        P.outr = kb.reg()

        with nc.Block() as block:
            kb.emit("pool", lambda e: e.memset(P.ones, 1.0), [], [P.onesr])
            if hasattr(P, "ident_in"):
                P.dma("pool", P.ident, P.ident_in, [], [P.identr])
            P.dma("sp", P.lng_t, lng, [], [P.parr])
            P.dma("sp", P.lnb_t, lnb, [], [P.parr])
            P.dma("sp", P.adab_t, adab, [], [P.parr])
            for i in range(8):
                P.dma("sp", P.Z[i * 256:(i + 1) * 256, :], zin[i * 256:(i + 1) * 256, :], [], [P.Zr])
            P.phase_ada(scT, adaw)
            kb.barrier()
            first = True
            for li in layers:
                last = (li == DEPTH - 1)
                if cfg.get("mixer", True):
                    if first:
                        P.ln_phase(li, None, 0, P.Z, 0, T)
                        kb.barrier()
                    P.mixer(li)
                    kb.barrier()
                    c0 = NCTX if last else 0
                    P.gemm_resid(li, "wo", c0)
                    kb.barrier()
                    P.ln_phase(li, 0, 3, P.Z, c0, T)
                    kb.barrier()
                else:
                    c0 = NCTX if last else 0
                    P.ln_phase(li, None, 3, P.Z, c0, T)
                    kb.barrier()
                P.dma("sp", P.fcw_t, fcw[:, li * 264:(li + 1) * 264], [], [P.parr])
                P.dma("sp", P.fcb_t, fcb[:, li * 88:(li + 1) * 88], [], [P.parr])
                P.ffn_up(li, wup, c0)
                kb.barrier()
                P.gemm_resid(li, "down", c0, wdn=wdn)
                kb.barrier()
                if last:
                    P.ln_phase(li, 1, None, outT, c0, T, dst_off=NCTX, dstr=P.outr)
                else:
                    nxt = li + 1 if cfg.get("mixer", True) else None
                    P.ln_phase(li, 1, (0 if nxt is not None else None), P.Z, 0, T, mod_layer=nxt)
                kb.barrier()
                first = False
            for name, ap in dbg.items():
                if name == "dbg_mod":
                    P.dma("sp", ap, P.mod, [P.modr], [P.outr])
                if name == "dbg_OT":
                    for i in range(8):
                        P.dma("pool", ap[i * 256:(i + 1) * 256, :], P.OT[i * 256:(i + 1) * 256, :], [P.OTr], [P.outr])
                if name == "dbg_Z":
                    for i in range(8):
                        P.dma("sp", ap[i * 256:(i + 1) * 256, :], P.Z[i * 256:(i + 1) * 256, :], [P.Zr], [P.outr])
            kb.barrier()
            kb.replay(block)
        return nc

    def wslot(self):
        i = self.wsi % 3
        self.wsi += 1
        return self.WS[i], self.WSr[i]

    def phase_ada(self, scT, adaw):
        P, kb = self, self.kb
        P.carve_reset()
        raw = P.carve([KC * 2], F32)
        sc = P.carve([KC * 2], BF16)
        rr = kb.reg()
        P.dma("sp", raw, scT, [], [rr])
        P.act(sc, raw, AF.Silu, [rr], [rr])
        sc3 = sc.rearrange("p (k s) -> p k s", s=2)
        nblk = DEPTH * 96
        loads = {}

        def load(b):
            ws, wr = P.wslot()
            l, j = divmod(b, 96)
            P.dma("pool", ws[:, 0:KC * 128], adaw[l, j], [], [wr])
            loads[b] = (ws, wr)

        for b in range(min(2, nblk)):
            load(b)
        for b in range(nblk):
            if b + 2 < nblk:
                load(b + 2)
            ws, wr = loads.pop(b)
            pb, pr = P.bank()
            for kc in range(KC):
                P.mm(pb[:, 0:2], ws[:, kc * 128:(kc + 1) * 128], sc3[:, kc, :], kc == 0, kc == KC - 1,
                     [wr, rr], [pr])
            P.act(P.mod[:, b * 2:b * 2 + 2], pb[:, 0:2], AF.Identity, [pr, P.parr], [P.modr],
                  bias=P.adab_t[:, b:b + 1])
        P.ts("dve", P.mod1, P.mod, 1.0, None, ALU.add, None, [P.modr], [P.modr])

    def modv(self, table, l, j, seg):
        v = table.rearrange("p (l j k s) -> p l j k s", l=DEPTH, j=6, k=KC)
        return v[:, l, j, :, seg]

    def ln_phase(self, li, which, modj, dst, c0, c1, dst_off=0, dstr=None, mod_layer=None):
        P, kb = self, self.kb
        ml = li if mod_layer is None else mod_layer
        dstr = P.Zr if dstr is None else dstr
        P.carve_reset()
        XT = P.XT.rearrange("p (k t) -> p k t", t=T)
        if modj is not None:
            for seg in range(2):
                Gs = P.Gp[:, seg * KC:(seg + 1) * KC]
                Bs = P.Bp[:, seg * KC:(seg + 1) * KC]
                s1 = P.modv(P.mod1, ml, modj + 1, seg)
                sh = P.modv(P.mod, ml, modj, seg)
                if which is None:
                    P.copy("dve", Gs, s1, [P.modr], [P.gpreg])
                    P.copy("dve", Bs, sh, [P.modr], [P.gpreg])
                else:
                    g = P.lng_t[:, (li * 2 + which) * KC:(li * 2 + which + 1) * KC]
                    b = P.lnb_t[:, (li * 2 + which) * KC:(li * 2 + which + 1) * KC]
                    P.tt("dve", Gs, g, s1, ALU.mult, [P.modr, P.parr], [P.gpreg])
                    P.tt("dve", Bs, b, s1, ALU.mult, [P.modr, P.parr], [P.gpreg])
                    P.tt("dve", Bs, Bs, sh, ALU.add, [P.modr, P.gpreg], [P.gpreg])
        NB = 3
        GW = 256
        rts = [P.carve([KC, GW], F32) for _ in range(NB)]
        rtr = kb.regs_n(NB)
        if which is not None:
            rsq = P.carve([KC, GW], BF16)
            rb = P.carve([KC, GW], BF16)
            sqr, rbr = kb.reg(), kb.reg()
            st = [P.carve([GW], F32) for _ in range(4)]
            stt_r = kb.regs_n(4)
            eps = LN_EPS / (DN_ALPHA * DN_ALPHA)
            epst = P.carve([1], F32)
            epsr = kb.reg()
            kb.emit("pool", lambda e: e.memset(epst, eps), [], [epsr])
        Zv = P.Z.rearrange("(k p) t -> p k t", p=128)
        dv = dst.rearrange("(k p) t -> p k t", p=128)
        groups = _split_cols(c0, c1, GW)
        for gi, (a, b_) in enumerate(groups):
            n = b_ - a
            seg = 1 if b_ <= NCTX else 0
            rt, rr = rts[gi % NB], rtr[gi % NB]
            P.dma("sp", rt[:, :, 0:n], Zv[:, :, a:b_], [P.Zr], [rr])
            if which is not None:
                P.act(rsq[:, :, 0:n], rt[:, :, 0:n], AF.Square, [rr], [sqr])
                P.copy("pool", rb[:, :, 0:n], rt[:, :, 0:n], [rr], [rbr])
                p1, p1r = P.bank()
                p2, p2r = P.bank()
                for kc in range(KC):
                    P.mm(p1[:, 0:n], P.ones, rb[:, kc, 0:n], kc == 0, kc == KC - 1, [P.onesr, rbr], [p1r])
                for kc in range(KC):
                    P.mm(p2[:, 0:n], P.ones, rsq[:, kc, 0:n], kc == 0, kc == KC - 1, [P.onesr, sqr], [p2r])
                mean, var, rstd, nmr = [s[:, 0:n] for s in st]
                P.ts("dve", mean, p1[:, 0:n], 1.0 / D, None, ALU.mult, None, [p1r], [stt_r[0]])
                P.tt("dve", var, mean, mean, ALU.mult, [stt_r[0]], [stt_r[1]])
                P.stt(var, p2[:, 0:n], 1.0 / D, var, ALU.mult, ALU.subtract, [p2r, stt_r[1]], [stt_r[1]])
                P.act(rstd, var, AF.Ln, [stt_r[1], epsr], [stt_r[2]], bias=epst)
                P.act(rstd, rstd, AF.Exp, [stt_r[2]], [stt_r[2]], scale=-0.5)
                P.stt(nmr, mean, -1.0, rstd, ALU.mult, ALU.mult, [stt_r[0], stt_r[2]], [stt_r[3]])
                rt3 = rt[:, :, 0:n]
                P.tt("dve", rt3, rt3, rstd.unsqueeze(1).to_broadcast([128, KC, n]), ALU.mult,
                     [rr, stt_r[2]], [rr])
                P.tt("dve", rt3, rt3, nmr.unsqueeze(1).to_broadcast([128, KC, n]), ALU.add,
                     [rr, stt_r[3]], [rr])
            if modj is not None:
                for kc in range(KC):
                    eng = "pool" if kc % 2 == 0 else "dve"
                    P.ts(eng, XT[:, kc, a:b_], rt[:, kc, 0:n],
                         P.Gp[:, seg * KC + kc:seg * KC + kc + 1], P.Bp[:, seg * KC + kc:seg * KC + kc + 1],
                         ALU.mult, ALU.add, [rr, P.gpreg], [P.XTr])
            if which is not None:
                g = P.lng_t[:, (li * 2 + which) * KC:(li * 2 + which + 1) * KC]
                b = P.lnb_t[:, (li * 2 + which) * KC:(li * 2 + which + 1) * KC]
                for kc in range(KC):
                    P.act(rt[:, kc, 0:n], rt[:, kc, 0:n], AF.Identity, [rr, P.parr], [rr],
                          bias=b[:, kc:kc + 1], scale=g[:, kc:kc + 1])
                P.dma("sp", dv[:, :, a - dst_off:b_ - dst_off], rt[:, :, 0:n], [rr], [dstr])

    def ffn_up(self, li, wup, c0):
        P, kb = self, self.kb
        P.carve_reset()
        XT = P.XT.rearrange("p (k t) -> p k t", t=T)
        groups = _split_cols(c0, T)
        segs = ([(0, NCTX)] if c0 == 0 else []) + [(NCTX, T)]
        ug = [P.carve([T], F32) for _ in range(2)]
        uv = [P.carve([T], F32) for _ in range(2)]
        ugr, uvr = kb.regs_n(2), kb.regs_n(2)
        cg, cv = P.carve([T], F32), P.carve([T], F32)
        cgr, cvr = kb.reg(), kb.reg()
        ao = [P.carve([T], BF16) for _ in range(2)]
        aor = kb.regs_n(2)
        loads = {}

        def load(j):
            ws, wr = P.wslot()
            P.dma("pool", ws[:, 0:KC * 256], wup[li, j], [], [wr])
            loads[j] = (ws, wr)

        for j in range(2):
            load(j)
        for j in range(FB):
            if j + 2 < FB:
                load(j + 2)
            ws, wr = loads.pop(j)
            w3 = ws[:, 0:KC * 256].rearrange("p (k c) -> p k c", c=256)
            u_g, u_v, u_gr, u_vr = ug[j % 2], uv[j % 2], ugr[j % 2], uvr[j % 2]
            for (a, b_) in groups:
                n = b_ - a
                for half, (ut, utr) in enumerate(((u_g, u_gr), (u_v, u_vr))):
                    pb, pr = P.bank()
                    for kc in range(KC):
                        P.mm(pb[:, 0:n], w3[:, kc, half * 128:(half + 1) * 128], XT[:, kc, a:b_],
                             kc == 0, kc == KC - 1, [wr, P.XTr], [pr])
                    P.copy("act", ut[:, a:b_], pb[:, 0:n], [pr], [utr])
            for half, (ut, utr, ct, ctr) in enumerate(((u_g, u_gr, cg, cgr), (u_v, u_vr, cv, cvr))):
                blk = half * FB + j
                w0 = P.fcw_t[:, blk * 3 + 0:blk * 3 + 1]
                w1 = P.fcw_t[:, blk * 3 + 1:blk * 3 + 2]
                w2 = P.fcw_t[:, blk * 3 + 2:blk * 3 + 3]
                bb = P.fcb_t[:, blk:blk + 1]
                for (s0, s1) in segs:
                    P.ts("dve", ct[:, s0:s1], ut[:, s0:s1], w1, bb, ALU.mult, ALU.add, [utr, P.parr], [ctr])
                    P.stt(ct[:, s0 + 1:s1], ut[:, s0:s1 - 1], w0, ct[:, s0 + 1:s1], ALU.mult, ALU.add,
                          [utr, ctr, P.parr], [ctr])
                    P.stt(ct[:, s0:s1 - 1], ut[:, s0 + 1:s1], w2, ct[:, s0:s1 - 1], ALU.mult, ALU.add,
                          [utr, ctr, P.parr], [ctr])
            a_o, a_or = ao[j % 2], aor[j % 2]
            P.act(cg[:, c0:T], cg[:, c0:T], AF.Silu, [cgr], [cgr])
            P.tt("dve", a_o[:, c0:T], cg[:, c0:T], cv[:, c0:T], ALU.mult, [cgr, cvr], [a_or])
            P.dma("sp", P.ACTT[j * 128:(j + 1) * 128, c0:T], a_o[:, c0:T], [a_or], [P.ACTTr])

    def gemm_resid(self, li, kind, c0, wdn=None):
        P, kb = self, self.kb
        P.carve_reset()
        if kind == "down":
            kc_n, src, wsrc, gj = FB, P.ACTT, (lambda nb: wdn[li, nb]), 5
            srcr = P.ACTTr
        else:
            kc_n, src, wsrc, gj = KC, P.OT, P.wo_src(li), 2
            srcr = P.OTr
        for seg in range(2):
            P.ts("dve", P.gpr[:, seg * KC:(seg + 1) * KC], P.modv(P.mod, li, gj, seg), 1.0 / DN_ALPHA, None,
                 ALU.mult, None, [P.modr], [P.gpreg])
        tw = min((KC * T) // kc_n, T) // 128 * 128
        tiles = []
        c = c0
        while c < T:
            e = min(c + tw, T)
            tiles.append((c, e))
            c = e
        srcv = src.rearrange("(k p) t -> p k t", p=128)
        xflat = P.XT
        zt = [P.carve([512], F32) for _ in range(3)]
        ztr = kb.regs_n(3)
        zi = 0
        for (t0, t1) in tiles:
            tn = t1 - t0
            X = xflat[:, 0:kc_n * tn].rearrange("p (k t) -> p k t", t=tn)
            P.dma("sp", X, srcv[:, :, t0:t1], [srcr], [P.XTr])
            subs = _split_cols(t0, t1)
            loads = {}

            def load(nb):
                ws, wr = P.wslot()
                P.dma("pool", ws[:, 0:kc_n * 128], wsrc(nb), [], [wr])
                loads[nb] = (ws, wr)

            for nb in range(2):
                load(nb)
            for nb in range(KC):
                if nb + 2 < KC:
                    load(nb + 2)
                ws, wr = loads.pop(nb)
                for (a, b_) in subs:
                    n = b_ - a
                    seg = 1 if b_ <= NCTX else 0
                    z, zr = zt[zi % 3], ztr[zi % 3]
                    zi += 1
                    P.dma("sp", z[:, 0:n], P.Z[nb * 128:(nb + 1) * 128, a:b_], [P.Zr], [zr])
                    pb, pr = P.bank()
                    for kc in range(kc_n):
                        P.mm(pb[:, 0:n], ws[:, kc * 128:(kc + 1) * 128], X[:, kc, a - t0:b_ - t0],
                             kc == 0, kc == kc_n - 1, [wr, P.XTr], [pr])
                    P.stt(z[:, 0:n], pb[:, 0:n], P.gpr[:, seg * KC + nb:seg * KC + nb + 1], z[:, 0:n],
                          ALU.mult, ALU.add, [pr, zr, P.gpreg], [zr])
                    P.dma("sp", P.Z[nb * 128:(nb + 1) * 128, a:b_], z[:, 0:n], [zr], [P.Zr])

    def mix_inputs(self):
        P = self
        L = self.cfg["layers"] if self.cfg.get("mixer", True) else []
        P.wo_in = {}
        if 0 in L:
            P.rw_par = P.din("rw_par", [128, KC * 15])
            P.rw_wrkv = P.din("rw_wrkv", [48, 128, KC * 128])
            P.rw_w1 = P.din("rw_w1", [2, 128, KC * 96])
            P.rw_a1 = P.din("rw_a1", [2, 128, KC * 64])
            P.rw_g1 = P.din("rw_g1", [2, 128, KC * 128])
            P.rw_w2c = P.din("rw_w2c", [KC, 128, 768])
            P.rw_msk = P.din("rw_msk", [128, 2 * 3 * 128])
            P.rw_bd = P.din("rw_bd", [128, 128])
            P.rw_cm = P.din("rw_cm", [128, T + 64])
            if not hasattr(P, "ident_in"):
                P.ident_in = P.din("ident", [128, 128])
            P.wo_in[0] = P.din("rw_wo", [KC, 128, KC * 128])
            P.XS = P.dscratch("XS", [6, D, T], BF16)
            P.RKV = [P.dscratch("RKV%d" % i, [D, T], F32) for i in range(3)]
            P.LWD = [P.dscratch("LWD%d" % i, [D, T], F32) for i in range(2)]
            P.ICD = [P.dscratch("ICD%d" % i, [D, T], F32) for i in range(2)]
            P.G32 = P.dscratch("G32", [D, T], F32)
        if 1 in L:
            P.da_wqk = P.din("da_wqk", [32, 128, KC * 256])
            P.da_wv = P.din("da_wv", [8, 128, KC * 256])
            P.da_cs = P.din("da_cs", [128, 2 * T])
            P.da_lam = P.din("da_lam", [1, 512])
            P.da_subg = P.din("da_subg", [128, 2])
            if not hasattr(P, "ident_in"):
                P.ident_in = P.din("ident", [128, 128])
            P.wo_in[1] = P.din("da_wo", [KC, 128, KC * 128])
            P.QT = P.dscratch("QT", [D, T], BF16)
            P.KT = P.dscratch("KT", [D, T], BF16)
        if 2 in L:
            P.hg_win = P.din("hg_win", [80, 128, KC * 128])
            P.hg_low = P.din("hg_low", [128, 2 * 4 * KC])
            P.hg_ng = P.din("hg_ng", [128, 1])
            P.hg_cm = P.din("hg_cm", [128, T + 64])
            P.hg_msk = P.din("hg_msk", [128, 256])
            if not hasattr(P, "ident_in"):
                P.ident_in = P.din("ident", [128, 128])
            P.wo_in[2] = P.din("hg_wo", [KC, 128, KC * 128])
        if 3 in L:
            P.lr_win = P.din("lr_win", [32, 128, KC * 128])
            P.lr_par = P.din("lr_par", [128, KC * 11])
            P.lr_wg = P.din("lr_wg", [8, 128, 2048])
            P.wo_in[3] = P.din("lr_wo", [KC, 128, KC * 128])

    def host_mix(self, inp):
        L = self.cfg["layers"] if self.cfg.get("mixer", True) else []
        m = {}

        def blocks(w, nb):
            w = np.asarray(w, np.float32)
            k = w.shape[0] // 128
            return np.ascontiguousarray(w.reshape(k, 128, nb, 128).transpose(2, 1, 0, 3)).reshape(nb, 128, k * 128)

        if 0 in L:
            f32 = np.float32
            mu = _fm(inp["rw_mu"][0]).transpose(0, 2, 1)
            w0 = _fm(inp["rw_w0"][0]).transpose(0, 2, 1)
            a0 = _fm(inp["rw_a0"][0]).transpose(0, 2, 1)
            one = lambda k: _fm(np.asarray(inp[k][0], f32).reshape(-1))[:, :, None]
            m["rw_par"] = np.ascontiguousarray(np.concatenate(
                [mu, w0, a0, one("rw_k_k"), one("rw_k_a"), one("rw_r_k"), one("rw_gn_g"), one("rw_gn_b")],
                axis=2)).reshape(128, KC * 15)
            m["rw_wrkv"] = np.concatenate([blocks(inp["rw_w_rkv"][0][n], KC) for n in range(3)], axis=0)

            def kmaj(w):
                w = np.asarray(w, f32)
                k = w.shape[0] // 128
                return np.ascontiguousarray(w.reshape(k, 128, w.shape[1]).transpose(1, 0, 2)).reshape(128, -1)

            m["rw_w1"] = np.stack([kmaj(inp["rw_w1"][0][d]) for d in range(2)])
            m["rw_a1"] = np.stack([kmaj(inp["rw_a1"][0][d]) for d in range(2)])
            g1 = np.asarray(inp["rw_g1"][0], f32)
            m["rw_g1"] = np.stack([kmaj(g1[:, j * 128:(j + 1) * 128]) for j in range(2)])
            w2c = np.zeros((KC, 128, 768), f32)
            w2 = np.asarray(inp["rw_w2"][0], f32)
            a2 = np.asarray(inp["rw_a2"][0], f32)
            g2 = np.asarray(inp["rw_g2"][0], f32)
            for nb in range(KC):
                cs_ = slice(nb * 128, (nb + 1) * 128)
                for d in range(2):
                    w2c[nb, 0:96, d * 128:(d + 1) * 128] = w2[d][:, cs_]
                    w2c[nb, 0:64, 256 + d * 128:256 + (d + 1) * 128] = a2[d][:, cs_]
                for k2 in range(2):
                    w2c[nb, :, 512 + k2 * 128:512 + (k2 + 1) * 128] = g2[k2 * 128:(k2 + 1) * 128, cs_]
            m["rw_w2c"] = w2c
            i_ = np.arange(128)
            same = (i_[:, None] // 64) == (i_[None, :] // 64)
            row, col = i_[:, None] % 64, i_[None, :] % 64
            mk = []
            for d in range(2):
                lt = (row < col) if d == 0 else (row > col)
                le = (row <= col) if d == 0 else (row >= col)
                gt = (row > col) if d == 0 else (row < col)
                mk += [same & lt, same & le, same & gt]
            m["rw_msk"] = np.concatenate(mk, axis=1).astype(f32)
            m["rw_bd"] = same.astype(f32)
            cm = np.ones((128, T + 64), f32)
            cm[:, 0::64] = 0.0
            m["rw_cm"] = cm
            m["ident"] = np.eye(128, dtype=f32)
            m["rw_wo"] = blocks(inp["rw_w_o"][0], KC)
        if 1 in L:
            w = np.asarray(inp["da_w_qkv"][0], np.float32)
            perm = np.concatenate([np.arange(32, 64), np.arange(0, 32), np.arange(96, 128), np.arange(64, 96)])
            wqk = w[:, 0:2 * D].reshape(KC, 128, 32, 128)
            wqk2 = np.stack([wqk, wqk[:, :, :, perm]], axis=3)
            m["da_wqk"] = np.ascontiguousarray(wqk2.transpose(2, 1, 0, 3, 4)).reshape(32, 128, KC * 256)
            wv = w[:, 2 * D:3 * D].reshape(KC, 128, 8, 256)
            m["da_wv"] = np.ascontiguousarray(wv.transpose(2, 1, 0, 3)).reshape(8, 128, KC * 256)
            f32 = np.float32
            row = np.repeat(np.arange(SEQ // 64, dtype=f32), 64)
            col = np.tile(np.arange(64, dtype=f32), SEQ // 64)
            inv = (f32(10000.0) ** (-np.arange(32, dtype=f32) / f32(32))).astype(f32)
            ar, ac = (row[:, None] * inv).astype(f32), (col[:, None] * inv).astype(f32)
            ang = np.concatenate([ar, ar, ac, ac], axis=-1)
            ang = np.concatenate([np.zeros((NCTX, 128), f32), ang], axis=0)
            sign = np.concatenate([-np.ones(32, f32), np.ones(32, f32), -np.ones(32, f32), np.ones(32, f32)])
            m["da_cs"] = np.ascontiguousarray(np.concatenate([np.cos(ang).T, (np.sin(ang) * sign).T], axis=1)).astype(f32)
            m["da_lam"] = np.asarray(inp["da_lambda"][0], f32).reshape(1, 512)
            m["da_subg"] = _fm(inp["da_sub_g"][0])
            m["ident"] = np.eye(128, dtype=f32)
            m["da_wo"] = blocks(inp["da_w_o"][0], KC)
        if 2 in L:
            m["hg_win"] = blocks(inp["hg_w_in"][0], 80)
            m["hg_low"] = _fm(inp["hg_lower"]).reshape(128, 2 * 4 * KC)
            m["hg_ng"] = np.asarray(inp["hg_norm_g"][0], np.float32).reshape(128, 1)
            cm = np.ones((128, T + 64), np.float32)
            cm[:, 0::64] = 0.0
            m["hg_cm"] = cm
            s_ = np.arange(128)[:, None]
            t_ = np.arange(128)[None, :]
            same = (s_ // 64) == (t_ // 64)
            m["hg_msk"] = np.concatenate([(same & (s_ <= t_)), (same & (s_ >= t_))], axis=1).astype(np.float32)
            m["ident"] = np.eye(128, dtype=np.float32)
            m["hg_wo"] = blocks(inp["hg_w_o"][0], KC)
        if 3 in L:
            m["lr_win"] = blocks(inp["lr_w_in"][0], 32)
            cw = _fm(inp["lr_conv_w"][0]).transpose(0, 2, 1)
            cb = _fm(inp["lr_conv_b"][0])[:, :, None]
            bg = _fm(inp["lr_b_gate"][0]).reshape(128, 4, KC).transpose(0, 2, 1)
            lam = _fm(inp["lr_lambda"][0]).transpose(0, 2, 1)
            m["lr_par"] = np.ascontiguousarray(np.concatenate([cw, cb, bg, lam], axis=2)).reshape(128, KC * 11)
            wg = np.asarray(inp["lr_w_gate"][0], np.float32).reshape(2, 2, 8, 2, 128, 2, 128)
            m["lr_wg"] = np.ascontiguousarray(wg.transpose(2, 4, 0, 1, 5, 3, 6)).reshape(8, 128, 2048)
            m["lr_wo"] = blocks(inp["lr_w_o"][0], KC)
        return m

    def wo_src(self, li):
        w = self.wo_in[li]
        return lambda nb: w[nb]

    def mixer(self, li):
        return [self.mixer_rw, self.mixer_da, self.mixer_hg, self.mixer_lr][li % 4](li)

    class _Pf:
        def __init__(self, P, seq, depth=2):
            self.P, self.seq, self.depth, self.nxt, self.got = P, seq, depth, 0, {}

        def get(self, i):
            while self.nxt < len(self.seq) and self.nxt <= i + self.depth:
                src, nel = self.seq[self.nxt]
                ws, wr = self.P.wslot()
                self.P.dma("pool", ws[:, 0:nel], src, [], [wr])
                self.got[self.nxt] = (ws, wr)
                self.nxt += 1
            return self.got.pop(i)

    def proj(self, lhs_fn, kc_n, mcols, rhs_fn, rreads, groups, evac):
        for (a, b_) in groups:
            pb, pr = self.bank()
            for kc in range(kc_n):
                self.mm(pb[0:mcols, 0:b_ - a], lhs_fn(kc), rhs_fn(kc, a, b_), kc == 0, kc == kc_n - 1, rreads, [pr])
            evac(pb, pr, a, b_)

    def mixer_lr(self, li):
        P, kb = self, self.kb
        P.carve_reset()
        XT = P.XT.rearrange("p (k t) -> p k t", t=T)
        groups = _split_cols(0, T)
        segs = [(0, NCTX), (NCTX, T)]
        par = P.carve([KC * 11], F32)
        par3 = par.rearrange("p (k j) -> p k j", j=11)
        negc = P.carve([KC * 2], F32)
        nc3 = negc.rearrange("p (k d) -> p k d", d=2)
        c1 = P.carve([1], F32)
        pr_ = kb.reg()
        P.dma("sp", par, P.lr_par, [], [pr_])
        kb.emit("pool", lambda e: e.memset(c1, 1.0), [], [pr_])
        P.act(nc3, par3[:, :, 9:11], AF.Exp, [pr_], [pr_], scale=-1.0)
        P.act(negc, negc, AF.Ln, [pr_], [pr_], bias=c1)
        P.ts("dve", negc, negc, -8.0, None, ALU.mult, None, [pr_], [pr_])
        names = ["R0", "R1", "C0", "C1", "A", "U", "Y0", "Y1"]
        tl = {n: P.carve([T], F32) for n in names}
        rg = {n: kb.reg() for n in names}
        CB = P.carve([2, T], BF16)
        CBr = kb.regs_n(2)
        seq = []
        for n in range(8):
            seq += [(P.lr_win[16 + 2 * n], KC * 128), (P.lr_win[17 + 2 * n], KC * 128), (P.lr_wg[n], 2048),
                    (P.lr_win[2 * n], KC * 128), (P.lr_win[2 * n + 1], KC * 128)]
        pf = P._Pf(P, seq)
        for n in range(8):
            for jb in range(2):
                ws, wr = pf.get(5 * n + jb)
                Rt, Rr = tl["R%d" % jb], rg["R%d" % jb]
                P.proj(lambda kc: ws[:, kc * 128:(kc + 1) * 128], KC, 128, lambda kc, a, b_: XT[:, kc, a:b_],
                       [wr, P.XTr], groups,
                       lambda pb, pr, a, b_: P.copy("act", Rt[:, a:b_], pb[:, 0:b_ - a], [pr], [Rr]))
            for jb in range(2):
                kcj = 2 * n + jb
                Rt, Rr, Ct, Cr = tl["R%d" % jb], rg["R%d" % jb], tl["C%d" % jb], rg["C%d" % jb]
                w = [par3[:, kcj, j:j + 1] for j in range(5)]
                for (s0, s1) in segs:
                    P.ts("dve", Ct[:, s0:s1], Rt[:, s0:s1], w[1], w[4], ALU.mult, ALU.add, [Rr, pr_], [Cr])
                    P.stt(Ct[:, s0 + 1:s1], Rt[:, s0:s1 - 1], w[0], Ct[:, s0 + 1:s1], ALU.mult, ALU.add, [Rr, Cr, pr_], [Cr])
                    P.stt(Ct[:, s0:s1 - 1], Rt[:, s0 + 1:s1], w[2], Ct[:, s0:s1 - 1], ALU.mult, ALU.add, [Rr, Cr, pr_], [Cr])
                    P.stt(Ct[:, s0:s1 - 2], Rt[:, s0 + 2:s1], w[3], Ct[:, s0:s1 - 2], ALU.mult, ALU.add, [Rr, Cr, pr_], [Cr])
                P.copy("pool", CB[:, jb, :], Ct, [Cr], [CBr[jb]])
            wg, wgr = pf.get(5 * n + 2)
            wg6 = wg[:, 0:2048].rearrange("p (d g j i c) -> p d g j i c", d=2, g=2, j=2, i=2)
            A, U, G, H = tl["A"], tl["U"], tl["R0"], tl["R1"]
            Ar, Ur, Gr, Hr = rg["A"], rg["U"], rg["R0"], rg["R1"]
            for d in range(2):
                for jb in range(2):
                    kcj = 2 * n + jb
                    Ct, Cr, Yt, Yr = tl["C%d" % jb], rg["C%d" % jb], tl["Y%d" % jb], rg["Y%d" % jb]
                    for gt, (Xt, Xr) in enumerate(((A, Ar), (U, Ur))):
                        bia = par3[:, kcj, 5 + d * 2 + gt:6 + d * 2 + gt]
                        P.proj(lambda ic: wg6[:, d, gt, jb, ic, :], 2, 128, lambda ic, a, b_: CB[:, ic, a:b_],
                               [wgr, CBr[0], CBr[1]], groups,
                               lambda pb, pr, a, b_: P.act(Xt[:, a:b_], pb[:, 0:b_ - a], AF.Sigmoid, [pr, pr_], [Xr], bias=bia))
                    P.act(A, A, AF.Exp, [Ar, pr_], [Ar], scale=nc3[:, kcj, d:d + 1])
                    P.tt("pool", G, A, A, ALU.mult, [Ar], [Gr])
                    P.act(G, G, AF.Sqrt, [Gr, pr_], [Gr], bias=c1, scale=-1.0)
                    P.tt("dve", U, U, Ct, ALU.mult, [Ur, Cr], [Ur])
                    P.tt("dve", U, U, G, ALU.mult, [Ur, Gr], [Ur])
                    if d == 0:
                        kb.emit("dve", lambda e: e.tensor_tensor_scan(out=H, data0=A, data1=U, initial=0.0,
                                                                      op0=ALU.mult, op1=ALU.add), [Ar, Ur], [Hr])
                        P.copy("pool", Yt, H, [Hr], [Yr])
                    else:
                        rv = lambda t, s0, s1: t[:, s0:s1][:, ::-1]
                        kb.emit("dve", lambda e: e.tensor_tensor_scan(out=rv(H, 0, NCTX), data0=rv(A, 0, NCTX),
                                                                      data1=rv(U, 0, NCTX), initial=0.0,
                                                                      op0=ALU.mult, op1=ALU.add), [Ar, Ur], [Hr])
                        kb.emit("dve", lambda e: e.tensor_tensor_scan(out=rv(H, NCTX, T), data0=rv(A, NCTX, T),
                                                                      data1=rv(U, NCTX, T), initial=H[:, 0:1],
                                                                      op0=ALU.mult, op1=ALU.add), [Ar, Ur, Hr], [Hr])
                        P.tt("pool", Yt, Yt, H, ALU.add, [Yr, Hr], [Yr])
            for jb in range(2):
                kcj = 2 * n + jb
                ws, wr = pf.get(5 * n + 3 + jb)
                Yt, Yr = tl["Y%d" % jb], rg["Y%d" % jb]

                def ev(pb, pr, a, b_):
                    P.copy("act", A[:, a:b_], pb[:, 0:b_ - a], [pr], [Ar])
                    P.act(U[:, a:b_], pb[:, 0:b_ - a], AF.Square, [pr], [Ur])

                P.proj(lambda kc: ws[:, kc * 128:(kc + 1) * 128], KC, 128, lambda kc, a, b_: XT[:, kc, a:b_],
                       [wr, P.XTr], groups, ev)
                P.ts("dve", U, U, 0.044715, 1.0, ALU.mult, ALU.add, [Ur], [Ur])
                P.tt("dve", U, U, A, ALU.mult, [Ur, Ar], [Ur])
                P.act(U, U, AF.Sigmoid, [Ur], [Ur], scale=1.5957691216057308)
                P.tt("pool", A, A, U, ALU.mult, [Ar, Ur], [Ar])
                P.tt("dve", CB[:, jb, :], Yt, A, ALU.mult, [Yr, Ar], [CBr[jb]])
                P.dma("sp", P.OT[kcj * 128:(kcj + 1) * 128, :], CB[:, jb, :], [CBr[jb]], [P.OTr])

    def mixer_rw(self, li):
        P, kb = self, self.kb
        XT3 = P.XT.rearrange("p (k t) -> p k t", t=T)
        groups = _split_cols(0, T)
        segs = [(0, NCTX, 1), (NCTX, T, 0)]
        rv = lambda t: t[:, ::-1]
        P.carve_reset()
        par = P.carve([KC * 15], F32)
        par3 = par.rearrange("p (k j) -> p k j", j=15)
        parr = kb.reg()
        P.dma("sp", par, P.rw_par, [], [parr])
        Hs = [P.carve([T], F32) for _ in range(2)]
        Ts = [P.carve([T], F32) for _ in range(2)]
        Ds = [P.carve([T], F32) for _ in range(2)]
        Hr, Tr_, Dr = kb.regs_n(2), kb.regs_n(2), kb.regs_n(2)
        xo = [P.carve([T], BF16) for _ in range(3)]
        xor_ = kb.regs_n(3)
        XSr = kb.reg()
        xi = 0
        for kc in range(KC):
            H, TM, DX = Hs[kc % 2], Ts[kc % 2], Ds[kc % 2]
            hr, tr, dr = Hr[kc % 2], Tr_[kc % 2], Dr[kc % 2]
            P.dma("sp", H, P.Z[kc * 128:(kc + 1) * 128, :], [P.Zr], [hr])
            for (s0, s1, sg_) in segs:
                P.act(H[:, s0:s1], H[:, s0:s1], AF.Identity, [hr, P.modr], [hr],
                      bias=P.modv(P.mod, li, 0, sg_)[:, kc:kc + 1], scale=P.modv(P.mod1, li, 1, sg_)[:, kc:kc + 1])
            for (s0, s1, sg_) in segs:
                P.tt("pool", TM[:, s0 + 1:s1 - 1], H[:, s0:s1 - 2], H[:, s0 + 2:s1], ALU.add, [hr], [tr])
                P.copy("pool", TM[:, s0:s0 + 1], H[:, s0 + 1:s0 + 2], [hr], [tr])
                P.copy("pool", TM[:, s1 - 1:s1], H[:, s1 - 2:s1 - 1], [hr], [tr])
            P.stt(DX, TM, 0.5, H, ALU.mult, ALU.subtract, [tr, hr], [dr])
            for n in range(6):
                o, orr = xo[xi % 3], xor_[xi % 3]
                xi += 1
                P.stt(o, DX, par3[:, kc, n:n + 1], H, ALU.mult, ALU.add, [dr, hr, parr], [orr])
                P.dma("sp", P.XS[n, kc * 128:(kc + 1) * 128, :], o, [orr], [XSr])
        kb.barrier()
        P.carve_reset()
        par = P.carve([KC * 15], F32)
        par3 = par.rearrange("p (k j) -> p k j", j=15)
        P.dma("sp", par, P.rw_par, [], [parr])
        stg = [P.carve([T], F32) for _ in range(3)]
        stgr = kb.regs_n(3)
        si = [0]

        def stage():
            i = si[0] % 3
            si[0] += 1
            return stg[i], stgr[i]

        MW = P.carve([2, T], BF16)
        MA = P.carve([2, T], BF16)
        MG = P.carve([2, T], BF16)
        MWr, MAr, MGr = kb.reg(), kb.reg(), kb.reg()
        XSv = P.XS.rearrange("n (k p) t -> n p k t", p=128)
        RKVr = kb.reg()
        seq = [(P.rw_wrkv[i], KC * 128) for i in range(48)]
        seq += [(P.rw_w1[d], KC * 96) for d in range(2)] + [(P.rw_a1[d], KC * 64) for d in range(2)]
        seq += [(P.rw_g1[j], KC * 128) for j in range(2)]
        pf = P._Pf(P, seq)
        xfn = lambda kc, a, b_: XT3[:, kc, a:b_]
        for n in range(3):
            P.dma("sp", XT3, XSv[n], [XSr], [P.XTr])
            for blk in range(KC):
                ws, wr = pf.get(n * KC + blk)
                st_, sr_ = stage()
                P.proj(lambda kc: ws[:, kc * 128:(kc + 1) * 128], KC, 128, xfn, [wr, P.XTr], groups,
                       lambda pb, pr, a, b_: P.copy("act", st_[:, a:b_], pb[:, 0:b_ - a], [pr], [sr_]))
                P.dma("sp", P.RKV[n][blk * 128:(blk + 1) * 128, :], st_, [sr_], [RKVr])
        P.dma("sp", XT3, XSv[3], [XSr], [P.XTr])
        for d in range(2):
            ws, wr = pf.get(48 + d)
            w3 = ws[:, 0:KC * 96].rearrange("p (k c) -> p k c", c=96)
            P.proj(lambda kc: w3[:, kc, :], KC, 96, xfn, [wr, P.XTr], groups,
                   lambda pb, pr, a, b_: P.act(MW[0:96, d, a:b_], pb[0:96, 0:b_ - a], AF.Tanh, [pr], [MWr]))
        P.dma("sp", XT3, XSv[4], [XSr], [P.XTr])
        for d in range(2):
            ws, wr = pf.get(50 + d)
            w3 = ws[:, 0:KC * 64].rearrange("p (k c) -> p k c", c=64)
            P.proj(lambda kc: w3[:, kc, :], KC, 64, xfn, [wr, P.XTr], groups,
                   lambda pb, pr, a, b_: P.copy("act", MA[0:64, d, a:b_], pb[0:64, 0:b_ - a], [pr], [MAr]))
        P.dma("sp", XT3, XSv[5], [XSr], [P.XTr])
        for j in range(2):
            ws, wr = pf.get(52 + j)
            P.proj(lambda kc: ws[:, kc * 128:(kc + 1) * 128], KC, 128, xfn, [wr, P.XTr], groups,
                   lambda pb, pr, a, b_: P.act(MG[:, j, a:b_], pb[:, 0:b_ - a], AF.Sigmoid, [pr], [MGr]))
        pf2 = P._Pf(P, [(P.rw_w2c[nb], 768) for nb in range(KC)])
        for nb in range(KC):
            ws, wr = pf2.get(nb)
            for d in range(2):
                st_, sr_ = stage()
                P.proj(lambda kc: ws[0:96, d * 128:(d + 1) * 128], 1, 128, lambda kc, a, b_: MW[0:96, d, a:b_],
                       [wr, MWr], groups,
                       lambda pb, pr, a, b_: P.act(st_[:, a:b_], pb[:, 0:b_ - a], AF.Sigmoid, [pr, parr], [sr_],
                                                   bias=par3[:, nb, 6 + d:7 + d]))
                P.ts("dve", st_, st_, -0.606531, None, ALU.mult, None, [sr_], [sr_])
                P.dma("sp", P.LWD[d][nb * 128:(nb + 1) * 128, :], st_, [sr_], [RKVr])
                st_, sr_ = stage()
                P.proj(lambda kc: ws[0:64, 256 + d * 128:256 + (d + 1) * 128], 1, 128,
                       lambda kc, a, b_: MA[0:64, d, a:b_], [wr, MAr], groups,
                       lambda pb, pr, a, b_: P.act(st_[:, a:b_], pb[:, 0:b_ - a], AF.Sigmoid, [pr, parr], [sr_],
                                                   bias=par3[:, nb, 8 + d:9 + d]))
                P.dma("sp", P.ICD[d][nb * 128:(nb + 1) * 128, :], st_, [sr_], [RKVr])
            st_, sr_ = stage()
            P.proj(lambda kc: ws[:, 512 + kc * 128:512 + (kc + 1) * 128], 2, 128, lambda kc, a, b_: MG[:, kc, a:b_],
                   [wr, MGr], groups,
                   lambda pb, pr, a, b_: P.copy("act", st_[:, a:b_], pb[:, 0:b_ - a], [pr], [sr_]))
            P.dma("sp", P.G32[nb * 128:(nb + 1) * 128, :], st_, [sr_], [RKVr])
        kb.barrier()
        P.carve_reset()
        P.lopos = 0
        lo_names = ["R", "K", "KK", "IC", "B", "LW", "TMP", "KS", "Y"]
        tl = {n: P.carve_lo([T], F32) for n in lo_names}
        rg = {n: kb.reg() for n in lo_names}
        SQb = P.carve_lo([T], BF16)
        ZV = P.carve_lo([36, 128], BF16)
        SQr, ZVr = kb.reg(), kb.reg()
        par = P.carve([KC * 15], F32)
        par3 = par.rearrange("p (k j) -> p k j", j=15)
        ZAR = P.carve([36, 2, 128], BF16)
        ZB = P.carve([36, 128], BF16)
        ZK = P.carve([36, 128], BF16)
        ZARr, ZBr, ZKr = kb.reg(), kb.reg(), kb.reg()
        cm = P.carve([T + 64], BF16)
        BD = P.carve([128], BF16)
        msk = P.carve([2 * 3 * 128], BF16)
        I32 = P.carve([128], F32)
        cst = P.carve([4], F32)
        omka = P.carve([KC], F32)
        DEC = P.carve([36], F32)
        Hp = P.carve([128], F32)
        cr, DECr, Hpr = kb.reg(), kb.reg(), kb.reg()
        P.dma("sp", par, P.rw_par, [], [cr])
        P.dma("pool", cm, P.rw_cm, [], [cr])
        P.dma("pool", BD, P.rw_bd, [], [cr])
        P.dma("pool", msk, P.rw_msk, [], [cr])
        P.dma("sp", I32, P.ident_in, [], [cr])
        kb.emit("pool", lambda e: e.memset(cst[:, 0:1], 1e-12), [], [cr])
        kb.emit("pool", lambda e: e.memset(cst[:, 1:2], 64e-5), [], [cr])
        P.ts("dve", omka, par3[:, :, 11], -1.0, 1.0, ALU.mult, ALU.add, [cr], [cr])
        kb.emit("pool", lambda e: e.memset(ZV, 0.0), [], [ZVr])
        kb.emit("pool", lambda e: e.memset(ZAR, 0.0), [], [ZARr])
        kb.emit("pool", lambda e: e.memset(ZB, 0.0), [], [ZBr])
        kb.emit("pool", lambda e: e.memset(ZK, 0.0), [], [ZKr])
        G_ = 6
        slot = []
        for i in range(G_):
            sl = {"UL": [P.carve([2, 128], BF16) for _ in range(2)],
                  "MK": P.carve([3, 128], BF16), "Pb": P.carve([128], BF16), "TR": P.carve([4, 128], BF16),
                  "ATX": P.carve([2, 128], BF16), "CV": P.carve([128], BF16), "MT": P.carve([128], F32),
                  "RT": P.carve([128], F32), "NY": P.carve([2, 128], F32)}
            sl["r"] = {k: kb.reg() for k in ["UL0", "UL1", "PQ", "MK", "Pb", "TR", "ATX", "CV", "MT", "RT", "NY"]}
            slot.append(sl)
        R, K_, KK, IC, Bt, LW, TMP, KS, Y = [tl[n] for n in lo_names]
        Rr, Kr, KKr, ICr, Br, LWr, TMPr, KSr, Yr = [rg[n] for n in lo_names]
        c3 = lambda t: t.rearrange("p (c i) -> p c i", i=64)

        def pad_write(eng_a, dstZ, dreg, fn):
            for hp in range(2):
                rows = slice(hp * 64, (hp + 1) * 64)
                fn(dstZ[rows, :, hp * 64:(hp + 1) * 64], rows)

        def bdsum(src_bf, sreg, dst, dreg, func=AF.Copy, **kw):
            for (a, b_) in groups:
                pb, pr = P.bank()
                P.mm(pb[:, 0:b_ - a], BD, src_bf[:, a:b_], True, True, [sreg, cr], [pr])
                P.act(dst[:, a:b_], pb[:, 0:b_ - a], func, [pr, cr], [dreg], **kw)

        for kc in range(KC):
            rows128 = slice(kc * 128, (kc + 1) * 128)
            pk = lambda j: par3[:, kc, j:j + 1]
            P.dma("sp", R, P.RKV[0][rows128, :], [RKVr], [Rr])
            P.dma("sp", K_, P.RKV[1][rows128, :], [RKVr], [Kr])
            P.dma("sp", TMP, P.RKV[2][rows128, :], [RKVr], [TMPr])
            pad_write("act", ZV, ZVr, lambda o, rows: P.copy("act", o, c3(TMP)[rows], [TMPr], [ZVr]))
            P.ts("dve", KK, K_, pk(10), None, ALU.mult, None, [Kr, cr], [KKr])
            P.act(SQb, KK, AF.Square, [KKr], [SQr])
            bdsum(SQb, SQr, TMP, TMPr, AF.Ln, bias=cst[:, 0:1])
            P.act(TMP, TMP, AF.Exp, [TMPr], [TMPr], scale=-0.5)
            P.tt("dve", KK, KK, TMP, ALU.mult, [KKr, TMPr], [KKr])
            kb.emit("pool", lambda e: e.memset(KS, 0.0), [KSr], [KSr])
            kb.emit("pool", lambda e: e.memset(Y, 0.0), [Yr], [Yr])
            for d in range(2):
                mA = msk[:, (d * 3 + 0) * 128:(d * 3 + 1) * 128]
                mR = msk[:, (d * 3 + 1) * 128:(d * 3 + 2) * 128]
                mL = msk[:, (d * 3 + 2) * 128:(d * 3 + 3) * 128]
                P.dma("sp", LW, P.LWD[d][rows128, :], [RKVr], [LWr])
                P.dma("sp", IC, P.ICD[d][rows128, :], [RKVr], [ICr])
                if d == 0:
                    kb.emit("dve", lambda e: e.tensor_tensor_scan(out=Bt, data0=cm[:, 0:T], data1=LW, initial=0.0,
                                                                  op0=ALU.mult, op1=ALU.add), [LWr, cr], [Br])
                else:
                    kb.emit("dve", lambda e: e.tensor_tensor_scan(out=rv(Bt), data0=rv(cm[:, 1:T + 1]), data1=rv(LW),
                                                                  initial=0.0, op0=ALU.mult, op1=ALU.add), [LWr, cr], [Br])
                btot = c3(Bt)[:, :, 63] if d == 0 else c3(Bt)[:, :, 0]
                P.act(DEC, btot, AF.Exp, [Br], [DECr])
                P.tt("dve", TMP, Bt, LW, ALU.subtract, [Br, LWr], [TMPr])
                P.act(TMP, TMP, AF.Exp, [TMPr], [TMPr])
                pad_write("dve", ZAR[:, :, 0, :], ZARr,
                          lambda o, rows: P.tt("dve", o, c3(KK)[rows], c3(TMP)[rows], ALU.mult, [KKr, TMPr], [ZARr]))
                P.act(TMP, Bt, AF.Exp, [Br, ZARr], [TMPr])
                pad_write("dve", ZAR[:, :, 1, :], ZARr,
                          lambda o, rows: P.tt("dve", o, c3(R)[rows], c3(TMP)[rows], ALU.mult, [Rr, TMPr], [ZARr]))
                P.act(LW, Bt, AF.Exp, [Br, LWr, TMPr], [LWr], scale=-1.0)
                P.tt("pool", TMP, KK, IC, ALU.mult, [KKr, ICr, ZARr], [TMPr])
                pad_write("dve", ZB, ZBr,
                          lambda o, rows: P.stt(o, c3(TMP)[rows], -1.0, c3(LW)[rows], ALU.mult, ALU.mult,
                                                [TMPr, LWr], [ZBr]))
                P.ts("dve", TMP, IC, pk(11), omka[:, kc:kc + 1], ALU.mult, ALU.add, [ICr, cr, ZBr], [TMPr])
                P.tt("dve", TMP, TMP, K_, ALU.mult, [TMPr, Kr], [TMPr])
                P.tt("pool", KS, KS, TMP, ALU.add, [KSr, TMPr], [KSr])
                pad_write("dve", ZK, ZKr,
                          lambda o, rows: P.tt("dve", o, c3(TMP)[rows], c3(LW)[rows], ALU.mult, [TMPr, LWr], [ZKr]))
                kb.emit("pool", lambda e: e.memset(Hp, 0.0), [Hpr], [Hpr])
                order = list(range(36)) if d == 0 else [3, 2, 1, 0] + list(range(35, 3, -1))
                for g0 in range(0, 36, G_):
                    cs_ = order[g0:g0 + G_]
                    for i, c in enumerate(cs_):
                        sl = slot[i]
                        r_ = sl["r"]
                        zar2 = ZAR[:, c, :, :].rearrange("p a b -> p (a b)")
                        n1, n1r = P.bank()
                        P.mm(n1[:, 0:256], ZB[:, c, :], zar2, True, True, [ZBr, ZARr], [n1r])
                        n2, n2r = P.bank()
                        P.mm(n2[:, 0:256], ZK[:, c, :], zar2, True, True, [ZKr, ZARr], [n2r])
                        n3, n3r = P.bank()
                        P.mm(n3[:, 0:128], ZAR[:, c, 0, :], ZB[:, c, :], True, True, [ZBr, ZARr], [n3r])
                        tb_, tbr = P.bank()
                        tpv = tb_.bitcast(BF16)
                        for j, src in enumerate((ZV[:, c, :], ZB[:, c, :], ZK[:, c, :], ZAR[:, c, 0, :])):
                            kb.emit("pe", lambda e, o=tpv[:, j * 128:(j + 1) * 128], src=src:
                                    e.transpose(out=o, in_=src, identity=P.ident),
                                    [ZVr, ZBr, ZKr, ZARr, P.identr], [tbr], inc=(j == 3))
                        UL0 = sl["UL"][0]
                        P.tt("dve", UL0[:, 0, :], n1[:, 0:128], mA, ALU.mult, [n1r, cr], [r_["UL0"]])
                        P.tt("dve", sl["MK"][:, 1, :], n1[:, 128:256], mR, ALU.mult, [n1r, cr], [r_["MK"]])
                        P.tt("dve", sl["MK"][:, 0, :], n2[:, 0:128], mA, ALU.mult, [n2r, cr], [r_["MK"]])
                        P.tt("dve", sl["MK"][:, 2, :], n2[:, 128:256], mR, ALU.mult, [n2r, cr], [r_["MK"]])
                        P.tt("dve", UL0[:, 1, :], n3[:, 0:128], mL, ALU.mult, [n3r, cr], [r_["UL0"]])
                        P.copy("act", sl["TR"].rearrange("p a b -> p (a b)"), tpv[:, 0:512], [tbr], [r_["TR"]])
                        P.tt("dve", sl["Pb"], UL0[:, 0, :], P.ident, ALU.add, [r_["UL0"], P.identr], [r_["Pb"]])
                    for j in range(1, 7):
                        for i, c in enumerate(cs_):
                            sl = slot[i]
                            r_ = sl["r"]
                            X, Xr = sl["UL"][(j - 1) % 2], r_["UL%d" % ((j - 1) % 2)]
                            Xn, Xnr = sl["UL"][j % 2], r_["UL%d" % (j % 2)]
                            UU, LL = X[:, 0, :], X[:, 1, :]
                            if j <= 5:
                                pa, par_ = P.bank()
                                P.mm(pa[:, 0:128], LL, UU, True, True, [Xr], [par_], inc=False)
                                P.mm(pa[:, 128:256], UU, LL, True, True, [Xr], [par_], inc=True)
                            if j >= 2:
                                pb, pbr = P.bank()
                                P.mm(pb[:, 0:128], LL, sl["Pb"], True, True, [Xr, r_["Pb"]], [pbr])
                            if j <= 5:
                                P.copy("act", Xn.rearrange("p a b -> p (a b)"), pa[:, 0:256], [par_], [Xnr])
                            if j >= 2:
                                P.tt("dve", sl["Pb"], pb[:, 0:128], sl["Pb"], ALU.add, [pbr, r_["Pb"]], [r_["Pb"]])
                    for i, c in enumerate(cs_):
                        sl = slot[i]
                        r_ = sl["r"]
                        TR, MK = sl["TR"], sl["MK"]
                        m1, m1r = P.bank()
                        P.mm(m1[:, 0:128], sl["Pb"], TR[:, 3, :], True, True, [r_["Pb"], r_["TR"]], [m1r], inc=False)
                        P.mm(m1[:, 128:256], MK[:, 0, :], TR[:, 0, :], True, True, [r_["MK"], r_["TR"]], [m1r], inc=True)
                        P.copy("act", sl["ATX"].rearrange("p a b -> p (a b)"), m1[:, 0:256], [m1r], [r_["ATX"]])
                    for i, c in enumerate(cs_):
                        sl = slot[i]
                        r_ = sl["r"]
                        TR, MK, ATm, X1 = sl["TR"], sl["MK"], sl["ATX"][:, 0, :], sl["ATX"][:, 1, :]
                        m2, m2r = P.bank()
                        P.mm(m2[:, 0:128], sl["Pb"], X1, True, True, [r_["Pb"], r_["ATX"]], [m2r])
                        P.copy("act", sl["CV"], m2[:, 0:128], [m2r], [r_["CV"]])
                        m3, m3r = P.bank()
                        P.mm(m3[:, 0:128], ATm, TR[:, 1, :], True, True, [r_["ATX"], r_["TR"]], [m3r], inc=False)
                        P.mm(m3[:, 128:256], ATm, MK[:, 1, :], True, True, [r_["ATX"], r_["MK"]], [m3r], inc=True)
                        P.tt("dve", sl["MT"], m3[:, 0:128], I32, ALU.add, [m3r, cr], [r_["MT"]])
                        P.tt("dve", sl["RT"], m3[:, 128:256], ZAR[:, c, 1, :], ALU.add, [m3r, ZARr], [r_["RT"]])
                    for i, c in enumerate(cs_):
                        sl = slot[i]
                        r_ = sl["r"]
                        TR, MK, CV = sl["TR"], sl["MK"], sl["CV"]
                        m4, m4r = P.bank()
                        P.mm(m4[:, 0:128], TR[:, 1, :], CV, True, False, [r_["TR"], r_["CV"]], [m4r], inc=False)
                        P.mm(m4[:, 0:128], TR[:, 2, :], TR[:, 0, :], False, True, [r_["TR"]], [m4r], inc=False)
                        P.mm(m4[:, 128:256], CV, MK[:, 1, :], True, False, [r_["CV"], r_["MK"]], [m4r], inc=False)
                        P.mm(m4[:, 128:256], TR[:, 0, :], MK[:, 2, :], False, True, [r_["TR"], r_["MK"]], [m4r], inc=True)
                        P.copy("act", sl["NY"].rearrange("p a b -> p (a b)"), m4[:, 0:256], [m4r], [r_["NY"]])
                    for i, c in enumerate(cs_):
                        sl = slot[i]
                        r_ = sl["r"]
                        yb, ybr = P.bank()
                        P.mm(yb[:, 0:128], Hp, sl["RT"], True, True, [Hpr, r_["RT"]], [ybr])
                        for hp in range(2):
                            rows = slice(hp * 64, (hp + 1) * 64)
                            yc = Y[rows, c * 64:(c + 1) * 64]
                            P.tt("dve", yc, yb[rows, hp * 64:(hp + 1) * 64], yc, ALU.add, [ybr, Yr], [Yr])
                            P.tt("pool", yc, sl["NY"][rows, 1, hp * 64:(hp + 1) * 64], yc, ALU.add, [r_["NY"], Yr], [Yr])
                        hb, hbr = P.bank()
                        P.mm(hb[:, 0:128], sl["MT"], Hp, True, True, [Hpr, r_["MT"]], [hbr])
                        P.tt("dve", Hp, hb[:, 0:128], sl["NY"][:, 0, :], ALU.add, [hbr, r_["NY"], Hpr], [Hpr])
                        P.ts("dve", Hp, Hp, DEC[:, c:c + 1], None, ALU.mult, None, [Hpr, DECr], [Hpr])
            P.tt("dve", TMP, R, KS, ALU.mult, [Rr, KSr], [TMPr])
            P.ts("dve", SQb, TMP, pk(12), None, ALU.mult, None, [TMPr, cr], [SQr])
            bdsum(SQb, SQr, LW, LWr)
            P.dma("sp", TMP, P.RKV[2][rows128, :], [RKVr, SQr], [TMPr])
            P.tt("dve", LW, LW, TMP, ALU.mult, [LWr, TMPr], [LWr])
            P.copy("act", SQb, Y, [Yr, LWr], [SQr])
            bdsum(SQb, SQr, IC, ICr)
            P.act(SQb, Y, AF.Square, [Yr, ICr], [SQr])
            bdsum(SQb, SQr, Bt, Br)
            P.ts("dve", IC, IC, 1.0 / 64, None, ALU.mult, None, [ICr], [ICr])
            P.tt("dve", TMP, IC, IC, ALU.mult, [ICr, LWr], [TMPr])
            P.stt(Bt, Bt, 1.0 / 64, TMP, ALU.mult, ALU.subtract, [Br, TMPr], [Br])
            P.act(Bt, Bt, AF.Ln, [Br, cr], [Br], bias=cst[:, 1:2])
            P.act(Bt, Bt, AF.Exp, [Br], [Br], scale=-0.5)
            P.tt("dve", Y, Y, IC, ALU.subtract, [Yr, ICr], [Yr])
            P.tt("dve", Y, Y, Bt, ALU.mult, [Yr, Br], [Yr])
            P.ts("dve", Y, Y, pk(13), pk(14), ALU.mult, ALU.add, [Yr, cr], [Yr])
            P.tt("pool", Y, Y, LW, ALU.add, [Yr, LWr], [Yr])
            P.dma("sp", TMP, P.G32[rows128, :], [RKVr, Br], [TMPr])
            P.tt("dve", SQb, Y, TMP, ALU.mult, [Yr, TMPr], [SQr])
            P.dma("sp", P.OT[rows128, :], SQb, [SQr], [P.OTr])

    def bank_in(self, lo, hi):
        c = self.bctr.get((lo, hi), 0)
        self.bctr[(lo, hi)] = c + 1
        b = lo + c % (hi - lo)
        return self.ps[:, b, :], self.psr[b]

    def mixer_da(self, li):
        P, kb = self, self.kb
        XT = P.XT.rearrange("p (k t) -> p k t", t=T)
        groups = _split_cols(0, T)
        P.carve_reset()
        cs = P.carve([2 * T], F32)
        csr = kb.reg()
        P.dma("sp", cs, P.da_cs, [], [csr])
        t1, t2 = P.carve([512], F32), P.carve([512], F32)
        t1r, t2r = kb.reg(), kb.reg()
        qo = [P.carve([T], BF16) for _ in range(2)]
        qor = kb.regs_n(2)
        pf = P._Pf(P, [(P.da_wqk[b], KC * 256) for b in range(32)])
        for blk in range(32):
            ws, wr = pf.get(blk)
            w3 = ws[:, 0:KC * 256].rearrange("p (k c) -> p k c", c=256)
            q_o, q_or = qo[blk % 2], qor[blk % 2]
            for (a, b_) in groups:
                n = b_ - a
                pa, par_ = P.bank()
                pb, pbr = P.bank()
                for kc in range(KC):
                    P.mm(pa[:, 0:n], w3[:, kc, 0:128], XT[:, kc, a:b_], kc == 0, kc == KC - 1, [wr, P.XTr], [par_])
                for kc in range(KC):
                    P.mm(pb[:, 0:n], w3[:, kc, 128:256], XT[:, kc, a:b_], kc == 0, kc == KC - 1, [wr, P.XTr], [pbr])
                P.tt("dve", t1[:, 0:n], pa[:, 0:n], cs[:, a:b_], ALU.mult, [par_, csr], [t1r])
                P.tt("dve", t2[:, 0:n], pb[:, 0:n], cs[:, T + a:T + b_], ALU.mult, [pbr, csr], [t2r])
                P.tt("pool", t1[:, 0:n], t1[:, 0:n], t2[:, 0:n], ALU.add, [t1r, t2r], [t1r])
                P.copy("act", q_o[:, a:b_], t1[:, 0:n], [t1r], [q_or])
            dst = P.QT if blk < 16 else P.KT
            r0 = (blk % 16) * 128
            P.dma("sp", dst[r0:r0 + 128, :], q_o, [q_or], [P.QKr])
        kb.barrier()
        stop = P.cfg.get("da_stop", 9)
        if stop <= 1:
            return
        P.carve_reset()
        lam_init = 0.8 - 0.6 * math.exp(-0.3 * li)
        lt = P.carve([512], F32)
        sgt = P.carve([2], F32)
        sm = P.carve([8], F32)
        smr = kb.reg()
        P.dma("sp", lt, P.da_lam.to_broadcast([128, 512]), [], [smr])
        P.dma("sp", sgt, P.da_subg, [], [smr])
        P.ts("dve", sgt, sgt, 1.0 - lam_init, None, ALU.mult, None, [smr], [smr])
        P.tt("dve", lt[:, 0:128], lt[:, 0:128], lt[:, 128:256], ALU.mult, [smr], [smr])
        P.tt("dve", lt[:, 256:384], lt[:, 256:384], lt[:, 384:512], ALU.mult, [smr], [smr])
        kb.emit("dve", lambda e: e.tensor_reduce(out=sm[:, 0:1], in_=lt[:, 0:128], axis=AX.X, op=ALU.add), [smr], [smr])
        kb.emit("dve", lambda e: e.tensor_reduce(out=sm[:, 1:2], in_=lt[:, 256:384], axis=AX.X, op=ALU.add), [smr], [smr])
        P.act(sm[:, 0:2], sm[:, 0:2], AF.Exp, [smr], [smr])
        P.stt(sm[:, 2:3], sm[:, 1:2], -lam_init, sm[:, 0:1], ALU.add, ALU.subtract, [smr], [smr])
        neglam = sm[:, 2:3]
        kb.emit("pool", lambda e: e.memset(sm[:, 3:4], 1e-5), [smr], [smr])
        epsc = sm[:, 3:4]
        Va = P.carve([18, 256], BF16)
        Var = kb.reg()
        qk = [P.carve([T], BF16) for _ in range(4)]
        qkr = kb.regs_n(4)
        PT = [P.carve([512], BF16) for _ in range(4)]
        PTr = kb.regs_n(4)
        pti = 0
        Od = P.carve([2, 512], F32)
        Odr = kb.reg()
        tmp = P.carve([2, 512], F32)
        tmpr = kb.reg()
        rl = P.carve([512], F32)
        rlr = kb.reg()
        sq = P.carve([2, 512], BF16)
        sqr = kb.reg()
        OTh = P.carve([2, T], BF16)
        OThr = kb.reg()
        pfv = P._Pf(P, [(P.da_wv[h], KC * 256) for h in range(8)], depth=1)
        for hd in range(8):
            ws, wr = pfv.get(hd)
            w3 = ws[:, 0:KC * 256].rearrange("p (k c) -> p k c", c=256)
            for tb in range(18):
                pb, pr = P.bank_in(3, 8)
                for kc in range(KC):
                    P.mm(pb[:, 0:256], XT[:, kc, tb * 128:(tb + 1) * 128], w3[:, kc, :], kc == 0, kc == KC - 1,
                         [wr, P.XTr], [pr])
                P.copy("act", Va[:, tb, :], pb[:, 0:256], [pr], [Var])
            for m in range(2):
                r0 = (hd * 2 + m) * 128
                P.dma("sp", qk[m], P.QT[r0:r0 + 128, :], [P.QKr], [qkr[m]])
                P.dma("sp", qk[2 + m], P.KT[r0:r0 + 128, :], [P.QKr], [qkr[2 + m]])
            its = []
            for (a, b_) in groups:
                nk = 2 if b_ <= NCTX else 18
                for m in range(2):
                    for kbk in range(nk):
                        its.append((a, b_, nk, m, kbk))
            sbanks = {}

            def emit_s(i):
                a, b_, nk, m, kbk = its[i]
                sb, sr = P.bank_in(3, 7)
                P.mm(sb[:, 0:b_ - a], qk[2 + m][:, kbk * 128:(kbk + 1) * 128], qk[m][:, a:b_], True, True,
                     [qkr[m], qkr[2 + m]], [sr])
                sbanks[i] = (sb, sr)

            for i in range(min(2, len(its))):
                emit_s(i)
            for i, (a, b_, nk, m, kbk) in enumerate(its):
                n = b_ - a
                if i + 2 < len(its):
                    emit_s(i + 2)
                sb, sr = sbanks.pop(i)
                pt, ptr = PT[pti % 4], PTr[pti % 4]
                pti += 1
                P.act(pt[:, 0:n], sb[:, 0:n], AF.Exp, [sr], [ptr], scale=float(128 ** -0.5))
                st_, sp_ = kbk == 0, kbk == nk - 1
                for eh in range(2):
                    P.mm(P.ps[:, eh, 0:n], Va[:, kbk, eh * 128:(eh + 1) * 128], pt[:, 0:n], st_, sp_,
                         [ptr, Var], [P.psr[eh]])
                P.mm(P.ps[:, 2, 0:n], P.ones, pt[:, 0:n], st_, sp_, [ptr, P.onesr], [P.psr[2]])
                if kbk != nk - 1:
                    continue
                P.act(rl[:, 0:n], P.ps[:, 2, 0:n], AF.Ln, [P.psr[2]], [rlr])
                P.act(rl[:, 0:n], rl[:, 0:n], AF.Exp, [rlr], [rlr], scale=-1.0)
                for eh in range(2):
                    if m == 0:
                        P.tt("dve", Od[:, eh, 0:n], P.ps[:, eh, 0:n], rl[:, 0:n], ALU.mult, [P.psr[eh], rlr], [Odr])
                    else:
                        P.tt("dve", tmp[:, eh, 0:n], P.ps[:, eh, 0:n], rl[:, 0:n], ALU.mult, [P.psr[eh], rlr], [tmpr])
                        P.stt(Od[:, eh, 0:n], tmp[:, eh, 0:n], neglam, Od[:, eh, 0:n], ALU.mult, ALU.add,
                              [tmpr, Odr, smr], [Odr])
                if m == 0:
                    continue
                P.act(sq[:, :, 0:n], Od[:, :, 0:n], AF.Square, [Odr], [sqr])
                rb_, rbr = P.ps[:, 7, :], P.psr[7]
                for eh in range(2):
                    P.mm(rb_[:, 0:n], P.ones, sq[:, eh, 0:n], eh == 0, eh == 1, [sqr, P.onesr], [rbr])
                P.act(rl[:, 0:n], rb_[:, 0:n], AF.Ln, [rbr, smr], [rlr], bias=epsc, scale=1.0 / 256)
                P.act(rl[:, 0:n], rl[:, 0:n], AF.Exp, [rlr], [rlr], scale=-0.5)
                for eh in range(2):
                    P.stt(OTh[:, eh, a:b_], Od[:, eh, 0:n], sgt[:, eh:eh + 1], rl[:, 0:n], ALU.mult, ALU.mult,
                          [Odr, rlr, smr], [OThr])
            for eh in range(2):
                r0 = hd * 256 + eh * 128
                P.dma("sp", P.OT[r0:r0 + 128, :], OTh[:, eh, :], [OThr], [P.OTr])

    def mixer_hg(self, li):
        P, kb = self, self.kb
        P.carve_reset()
        XT = P.XT.rearrange("p (k t) -> p k t", t=T)
        groups = _split_cols(0, T)
        rv = lambda t: t[:, ::-1]
        low = P.carve([2 * 4 * KC], F32)
        low4 = low.rearrange("p (d l k) -> p d l k", d=2, l=4)
        lb = P.carve([2 * KC], F32)
        oml = P.carve([2 * KC], F32)
        den = P.carve([2 * KC], F32)
        sm = P.carve([4], F32)
        cr = kb.reg()
        lb3 = lb.rearrange("p (d k) -> p d k", d=2)
        den3 = den.rearrange("p (d k) -> p d k", d=2)
        P.dma("sp", low, P.hg_low, [], [cr])
        P.dma("sp", sm[:, 0:1], P.hg_ng, [], [cr])
        kb.emit("pool", lambda e: e.memset(sm[:, 1:2], 1e-5), [], [cr])
        kb.emit("pool", lambda e: e.memset(sm[:, 2:3], 1.0), [], [cr])
        P.act(low, low, AF.Exp, [cr], [cr])
        P.tt("dve", den3, low4[:, :, 0, :], low4[:, :, 1, :], ALU.add, [cr], [cr])
        P.tt("dve", den3, den3, low4[:, :, 2, :], ALU.add, [cr], [cr])
        P.tt("dve", den3, den3, low4[:, :, 3, :], ALU.add, [cr], [cr])
        kb.emit("dve", lambda e: e.reciprocal(out=den, in_=den), [cr], [cr])
        P.copy("dve", lb3, low4[:, :, 1, :], [cr], [cr])
        for l in range(2, li + 1):
            P.tt("dve", lb3, lb3, low4[:, :, l, :], ALU.add, [cr], [cr])
        P.tt("dve", lb, lb, den, ALU.mult, [cr], [cr])
        P.ts("dve", oml, lb, -1.0, 1.0, ALU.mult, ALU.add, [cr], [cr])
        cm = P.carve([T + 64], BF16)
        msk = P.carve([256], BF16)
        P.dma("pool", cm, P.hg_cm, [], [cr])
        P.dma("pool", msk, P.hg_msk, [], [cr])
        names = ["Q", "F", "K", "TMP", "O"]
        tl = {n: P.carve([T], F32) for n in names}
        rg = {n: kb.reg() for n in names}
        bn = ["QD", "KD", "KE", "SQ"]
        bt = {n: P.carve([T], BF16) for n in bn}
        br = {n: kb.reg() for n in bn}
        Vt = P.carve([18, 128], BF16)
        ATT = P.carve([18, 128], BF16)
        KET = P.carve([18, 128], BF16)
        Vr, ATr, KEr = kb.reg(), kb.reg(), kb.reg()
        S = P.carve([128], F32)
        Sb = P.carve([128], BF16)
        DEC = P.carve([36], F32)
        Sr, Sbr, DECr = kb.reg(), kb.reg(), kb.reg()
        seq = []
        for hd in range(16):
            seq += [(P.hg_win[c * 16 + hd], KC * 128) for c in (0, 1, 3, 4, 2)]
        pf = P._Pf(P, seq)
        Q, F, K_, TMP, O = [tl[n] for n in names]
        Qr, Fr, Kr, TMPr, Or = [rg[n] for n in names]

        def fm_proj(ws, wr, func, dst, dstr):
            P.proj(lambda kc: ws[:, kc * 128:(kc + 1) * 128], KC, 128, lambda kc, a, b_: XT[:, kc, a:b_],
                   [wr, P.XTr], groups,
                   lambda pb, pr, a, b_: P.act(dst[:, a:b_], pb[:, 0:b_ - a], func, [pr], [dstr]))

        for hd in range(16):
            ws, wr = pf.get(5 * hd + 0)
            fm_proj(ws, wr, AF.Silu, Q, Qr)
            ws, wr = pf.get(5 * hd + 1)
            for tb in range(18):
                pb, pr = P.bank()
                for kc in range(KC):
                    P.mm(pb[:, 0:128], XT[:, kc, tb * 128:(tb + 1) * 128], ws[:, kc * 128:(kc + 1) * 128],
                         kc == 0, kc == KC - 1, [wr, P.XTr], [pr])
                P.copy("act", Vt[:, tb, :], pb[:, 0:128], [pr], [Vr])
            for d in range(2):
                ws, wr = pf.get(5 * hd + 2 + d)
                fm_proj(ws, wr, AF.Sigmoid, F, Fr)
                P.ts("dve", F, F, oml[:, d * KC + hd:d * KC + hd + 1], lb[:, d * KC + hd:d * KC + hd + 1],
                     ALU.mult, ALU.add, [Fr, cr], [Fr])
                P.ts("pool", K_, F, -1.0, 1.0, ALU.mult, ALU.add, [Fr], [Kr])
                P.act(TMP, F, AF.Ln, [Fr], [TMPr])
                B = F
                if d == 0:
                    kb.emit("dve", lambda e: e.tensor_tensor_scan(out=B, data0=cm[:, 0:T], data1=TMP, initial=0.0,
                                                                  op0=ALU.mult, op1=ALU.add), [TMPr, cr, Fr], [Fr])
                else:
                    kb.emit("dve", lambda e: e.tensor_tensor_scan(out=rv(B), data0=rv(cm[:, 1:T + 1]), data1=rv(TMP),
                                                                  initial=0.0, op0=ALU.mult, op1=ALU.add),
                            [TMPr, cr, Fr], [Fr])
                B3 = B.rearrange("p (c i) -> p c i", i=64)
                btot = B3[:, :, 63] if d == 0 else B3[:, :, 0]
                P.act(TMP, B, AF.Exp, [Fr], [TMPr])
                P.tt("dve", bt["QD"], Q, TMP, ALU.mult, [Qr, TMPr], [br["QD"]])
                P.act(TMP, B, AF.Exp, [Fr, br["QD"]], [TMPr], scale=-1.0)
                P.tt("dve", bt["KD"], K_, TMP, ALU.mult, [Kr, TMPr], [br["KD"]])
                P.act(DEC, btot, AF.Exp, [Fr], [DECr])
                T3 = TMP.rearrange("p (c i) -> p c i", i=64)
                P.tt("dve", T3, btot.unsqueeze(2).to_broadcast([128, 36, 64]), B3, ALU.subtract, [Fr, br["KD"]], [TMPr])
                P.act(TMP, TMP, AF.Exp, [TMPr], [TMPr])
                P.tt("dve", bt["KE"], K_, TMP, ALU.mult, [Kr, TMPr], [br["KE"]])
                mk = msk[:, d * 128:(d + 1) * 128]
                for tb in range(18):
                    pb, pr = P.bank_in(0, 4)
                    P.mm(pb[:, 0:128], bt["KD"][:, tb * 128:(tb + 1) * 128], bt["QD"][:, tb * 128:(tb + 1) * 128],
                         True, True, [br["KD"], br["QD"]], [pr])
                    P.tt("dve", ATT[:, tb, :], pb[:, 0:128], mk, ALU.mult, [pr, cr], [ATr])
                    tb_, tr = P.bank_in(0, 4)
                    tpv = tb_.bitcast(BF16)[:, 0:128]
                    src = bt["KE"][:, tb * 128:(tb + 1) * 128]
                    kb.emit("pe", lambda e, tpv=tpv, src=src: e.transpose(out=tpv, in_=src, identity=P.ident),
                            [br["KE"], P.identr], [tr])
                    P.copy("act", KET[:, tb, :], tpv, [tr], [KEr])
                kb.emit("pool", lambda e: e.memset(S, 0.0), [Sr], [Sr])
                kb.emit("pool", lambda e: e.memset(Sb, 0.0), [Sbr], [Sbr])
                order = list(range(18)) if d == 0 else [1, 0] + list(range(17, 1, -1))
                for tb in order:
                    ob, obr = P.bank_in(4, 6)
                    P.mm(ob[:, 0:128], Vt[:, tb, :], ATT[:, tb, :], True, False, [Vr, ATr], [obr], inc=False)
                    halves = (0, 1) if d == 0 else (1, 0)
                    for hi, hf in enumerate(halves):
                        c = 2 * tb + hf
                        co = hf * 64
                        P.mm(ob[:, co:co + 64], Sb, bt["QD"][:, c * 64:(c + 1) * 64], False, hi == 1,
                             [Sbr, br["QD"]], [obr], inc=True)
                        sb_, sbr_ = P.bank_in(6, 8)
                        P.mm(sb_[:, 0:128], KET[co:co + 64, tb, :], Vt[co:co + 64, tb, :], True, True,
                             [KEr, Vr], [sbr_])
                        P.stt(S, S, DEC[:, c:c + 1], sb_[:, 0:128], ALU.mult, ALU.add, [Sr, DECr, sbr_], [Sr])
                        P.copy("dve", Sb, S, [Sr], [Sbr])
                    if d == 0:
                        P.copy("act", O[:, tb * 128:(tb + 1) * 128], ob[:, 0:128], [obr], [Or])
                    else:
                        P.tt("dve", O[:, tb * 128:(tb + 1) * 128], ob[:, 0:128], O[:, tb * 128:(tb + 1) * 128],
                             ALU.add, [obr, Or], [Or])
            ws, wr = pf.get(5 * hd + 4)
            fm_proj(ws, wr, AF.Silu, F, Fr)
            P.act(bt["SQ"], O, AF.Square, [Or], [br["SQ"]])
            for (a, b_) in groups:
                n = b_ - a
                pb, pr = P.bank_in(0, 4)
                P.mm(pb[:, 0:n], P.ones, bt["SQ"][:, a:b_], True, True, [br["SQ"], P.onesr], [pr])
                P.act(TMP[:, a:b_], pb[:, 0:n], AF.Ln, [pr, cr], [TMPr], bias=sm[:, 1:2], scale=1.0 / 128)
            P.act(TMP, TMP, AF.Exp, [TMPr], [TMPr], scale=-0.5)
            P.stt(O, O, sm[:, 0:1], TMP, ALU.mult, ALU.mult, [Or, TMPr, cr], [Or])
            P.tt("dve", bt["SQ"], O, F, ALU.mult, [Or, Fr], [br["SQ"]])
            P.dma("sp", P.OT[hd * 128:(hd + 1) * 128, :], bt["SQ"], [br["SQ"]], [P.OTr])


def _fm(v, nblk=None):
    v = np.asarray(v, np.float32)
    lead = v.shape[:-1]
    nb = v.shape[-1] // 128
    v = v.reshape(lead + (nb, 128))
    return np.ascontiguousarray(np.moveaxis(v, -1, 0))


def host_common(inp):
    m = {}
    aw = np.asarray(inp["ada_w"], np.float32)
    m["adaw"] = np.ascontiguousarray(
        aw.reshape(DEPTH, KC, 128, 96, 128).transpose(0, 3, 2, 1, 4)).reshape(DEPTH, 96, 128, KC * 128)
    m["adab"] = _fm(inp["ada_b"]).reshape(128, DEPTH * 96)
    m["lng"] = _fm(inp["ln_g"]).reshape(128, DEPTH * 2 * KC)
    m["lnb"] = _fm(inp["ln_b"]).reshape(128, DEPTH * 2 * KC)
    wu = np.asarray(inp["ffn_w_up"], np.float32).reshape(DEPTH, KC, 128, 2, FB, 128)
    m["wup"] = np.ascontiguousarray(wu.transpose(0, 4, 2, 1, 3, 5)).reshape(DEPTH, FB, 128, KC * 256)
    wd = np.asarray(inp["ffn_w_down"], np.float32).reshape(DEPTH, FB, 128, KC, 128)
    m["wdn"] = np.ascontiguousarray(wd.transpose(0, 3, 2, 1, 4)).reshape(DEPTH, KC, 128, FB * 128)
    cw = _fm(inp["ffn_conv_w"])
    m["fcw"] = np.ascontiguousarray(cw.transpose(0, 1, 3, 2)).reshape(128, DEPTH * 88 * 3)
    m["fcb"] = _fm(inp["ffn_conv_b"]).reshape(128, DEPTH * 88)
    return m


def host_core(inp, b):
    m = {}
    z = np.concatenate([np.asarray(inp["ctx"][b], np.float32), np.asarray(inp["x"][b], np.float32)], axis=0)
    m["zin"] = np.ascontiguousarray(z.T)
    sc = np.stack([np.asarray(inp["c"][b], np.float32), np.asarray(inp["c_ctx"], np.float32)], axis=-1)
    m["scT"] = np.ascontiguousarray(sc.reshape(KC, 128, 2).transpose(1, 0, 2)).reshape(128, KC * 2)
    return m


_CACHE = {}


def kernel(**inputs):
    cfg = {"layers": [0, 1, 2, 3], "mixer": True}
    if "prog" not in _CACHE:
        p = Prog(cfg)
        _CACHE["prog"] = (p, p.build())
    p, nc = _CACHE["prog"]
    common = host_common(inputs)
    common.update(p.host_mix(inputs))
    in_maps = []
    for core in range(NCORES):
        m = dict(common)
        m.update(host_core(inputs, core % 4))
        in_maps.append(m)
    res = run_bass_kernel_spmd(nc, in_maps, core_ids=list(range(NCORES)))
    out = np.stack([np.ascontiguousarray(res.results[b]["outT"].T) for b in range(4)], axis=0)
    return out.astype(np.float32)
```
